# Optimizing a Trainium2 kernel written in Bass

```python
import math
import jax, jax.numpy as jnp
from jax import lax
import numpy as np

D_MODEL = 1024
BATCH = 8
SEQ = 2048
DEPTH = 2
DEC_BATCH = 128
DEC_SEQ = 8
PAST_LEN = 16384
PAGE_SIZE = 128

N_BRANCH = 4
CHUNK = 64
NORM_EPS = 1e-6
MIN_FORGET = 1e-30
HG_H = 4
HG_DK = 64
HG_DV = 64
HG_W = HG_H * HG_DV
GLA_H = 4
GLA_DK = 32
GLA_DV = 64
GLA_RANK = 16
GLA_TAU = 16.0
GLA_W = GLA_H * GLA_DV
RW_H = 4
RW_N = 64
RW_W = RW_H * RW_N
RW_DECAY_RANK = 64
RW_A_RANK = 64
RW_G_RANK = 128
RW_COLS = 3 * RW_W + RW_DECAY_RANK + RW_A_RANK + RW_G_RANK
RW_LN_EPS = 64e-5
RET_H = 4
RET_DK = 64
RET_DV = 64
RET_W = RET_H * RET_DV
ROPE_BASE = 10000.0
BRANCH_W = HG_W
D_FF = 256 * (-(-(8 * D_MODEL) // (3 * 256)))

IN_WIDTHS = (HG_H * HG_DK, HG_H * HG_DK, HG_W, HG_W,
             GLA_H * GLA_DK, GLA_H * GLA_DK, GLA_W, GLA_RANK, GLA_W,
             RW_COLS,
             RET_H * RET_DK, RET_H * RET_DK, RET_W, RET_W,
             N_BRANCH * D_MODEL)
N_IN = sum(IN_WIDTHS)
RW_WIDTHS = (RW_W, RW_DECAY_RANK, RW_W, RW_W, RW_A_RANK, RW_G_RANK)

kernel_name = 'hybrid_hgrn2_gla_rwkv7_retnet_step'


def _split(t, widths):
    return jnp.split(t, [int(i) for i in np.cumsum(widths)[:-1]], axis=-1)


def _heads(t, n):
    return t.reshape(t.shape[:-1] + (n, t.shape[-1] // n))


def _rmsnorm(x, w):
    xf = x.astype(jnp.float32)
    y = xf * lax.rsqrt(jnp.mean(xf * xf, axis=-1, keepdims=True) + NORM_EPS)
    return (y * w.astype(jnp.float32)).astype(x.dtype)


def _head_rms(o):
    return o * lax.rsqrt(jnp.mean(o * o, axis=-1, keepdims=True) + NORM_EPS)


def _head_layernorm(o, w, b):
    mu = jnp.mean(o, axis=-1, keepdims=True)
    var = jnp.mean(jnp.square(o - mu), axis=-1, keepdims=True)
    return (o - mu) * lax.rsqrt(var + RW_LN_EPS) * w.reshape(o.shape[-2:]) + b.reshape(o.shape[-2:])


def _rotary(x, pos):
    half = x.shape[-1] // 2
    inv = ROPE_BASE ** (-jnp.arange(half, dtype=jnp.float32) / half)
    ang = pos[:, None] * inv[None, :]
    cos = jnp.cos(ang)[None, :, None, :]
    sin = jnp.sin(ang)[None, :, None, :]
    x1, x2 = x[..., :half], x[..., half:]
    return jnp.concatenate([x1 * cos - x2 * sin, x1 * sin + x2 * cos], axis=-1)


def gated_linear_attention_chunked(q, k, v, log_g, s0):
    B, L, H, _ = q.shape
    dv = v.shape[-1]
    C = math.gcd(L, CHUNK)
    n = L // C

    def to_chunks(t):
        return t.reshape(B, n, C, H, t.shape[-1]).transpose(1, 0, 3, 2, 4)

    causal = jnp.tril(jnp.ones((C, C), dtype=jnp.float32))[None, None, :, :, None]

    def step(S, inp):
        qb, kb, vb, gb = inp
        b = jnp.cumsum(gb, axis=2)
        diff = b[:, :, :, None, :] - b[:, :, None, :, :]
        decay = jnp.exp(jnp.where(causal > 0, diff, 0.0)) * causal
        scores = jnp.sum(qb[:, :, :, None, :] * kb[:, :, None, :, :] * decay, axis=-1)
        o = (jnp.einsum('bhts,bhsv->bhtv', scores, vb)
             + jnp.einsum('bhtk,bhkv->bhtv', qb * jnp.exp(b), S))
        b_last = b[:, :, -1:, :]
        S_new = (jnp.exp(b_last[:, :, 0, :])[..., None] * S
                 + jnp.einsum('bhsk,bhsv->bhkv', kb * jnp.exp(b_last - b), vb))
        return S_new, o

    S, o = lax.scan(step, s0.astype(jnp.float32),
                    (to_chunks(q), to_chunks(k), to_chunks(v), to_chunks(log_g)))
    o = o.transpose(1, 0, 3, 2, 4).reshape(B, L, H, dv)
    return o, S


def rwkv7_recurrence(r, log_w, k, v, kk, a, s0):
    def step(S, inp):
        r_t, lw_t, k_t, v_t, kk_t, a_t = inp
        kS = jnp.einsum('bhk,bhkv->bhv', kk_t, S)
        S = (jnp.exp(lw_t)[..., None] * S
             - (a_t * kk_t)[..., None] * kS[..., None, :]
             + k_t[..., None] * v_t[..., None, :])
        return S, jnp.einsum('bhk,bhkv->bhv', r_t, S)

    xs = (r.transpose(1, 0, 2, 3), log_w.transpose(1, 0, 2, 3), k.transpose(1, 0, 2, 3),
          v.transpose(1, 0, 2, 3), kk.transpose(1, 0, 2, 3), a.transpose(1, 0, 2, 3))
    S, o = lax.scan(step, s0.astype(jnp.float32), xs)
    return o.transpose(1, 0, 2, 3), S


def token_mixing(h, pos, states, lw):
    f32 = jnp.float32
    B, L, _ = h.shape
    s_hg, s_gla, s_rw, s_shift, s_ret = states
    proj = (h @ lw['w_in']).astype(f32)
    (hg_q, hg_f, hg_i, hg_g, gla_q, gla_k, gla_v, gla_a, gla_g, rw_cols,
     ret_q, ret_k, ret_v, ret_g, gate_logits) = _split(proj, IN_WIDTHS)

    lb = lw['hg_lb']
    forget = lb + (1.0 - lb) * jax.nn.sigmoid(hg_f)
    log_f = jnp.log(jnp.maximum(forget, MIN_FORGET))
    o_hg, n_hg = gated_linear_attention_chunked(
        _heads(jax.nn.silu(hg_q), HG_H), _heads(1.0 - forget, HG_H),
        _heads(hg_i, HG_H), _heads(log_f, HG_H), s_hg)
    o_hg = (_head_rms(o_hg) * lw['hg_norm_w']).reshape(B, L, HG_W) * jax.nn.silu(hg_g)

    log_a = jax.nn.log_sigmoid(gla_a @ lw['gla_wa2'] + lw['gla_ba']) / GLA_TAU
    o_gla, n_gla = gated_linear_attention_chunked(
        _heads(gla_q * GLA_DK ** -0.5, GLA_H), _heads(gla_k, GLA_H),
        _heads(gla_v, GLA_H), _heads(log_a, GLA_H), s_gla)
    o_gla = (_head_rms(o_gla) * lw['gla_norm_w']).reshape(B, L, GLA_W) * jax.nn.silu(gla_g)

    prev = jnp.concatenate([s_shift.astype(f32)[:, None, :], rw_cols[:, :-1]], axis=1)
    rw_mix = rw_cols + (prev - rw_cols) * lw['rw_mu']
    r, w_d, k, v, a_d, g_d = _split(rw_mix, RW_WIDTHS)
    log_w = -math.exp(-0.5) * jax.nn.sigmoid(lw['rw_w0'] + jnp.tanh(w_d) @ lw['rw_w2'])
    a = jax.nn.sigmoid(lw['rw_a0'] + a_d @ lw['rw_a2'])
    g = jax.nn.sigmoid(g_d) @ lw['rw_g2']
    kk = _heads(k * lw['rw_kk'], RW_H)
    kk = kk / jnp.maximum(jnp.sqrt(jnp.sum(kk * kk, axis=-1, keepdims=True)), 1e-12)
    k = k * (1.0 + (a - 1.0) * lw['rw_ka'])
    r_h, k_h, v_h = _heads(r, RW_H), _heads(k, RW_H), _heads(v, RW_H)
    o_rw, n_rw = rwkv7_recurrence(r_h, _heads(log_w, RW_H), k_h, v_h, kk, _heads(a, RW_H), s_rw)
    bonus = jnp.sum(r_h * k_h * lw['rw_rk'].reshape(RW_H, RW_N), axis=-1, keepdims=True) * v_h
    o_rw = (_head_layernorm(o_rw, lw['rw_ln_w'], lw['rw_ln_b']) + bonus).reshape(B, L, RW_W) * g

    q_r = _rotary(_heads(ret_q, RET_H), pos)
    k_r = _rotary(_heads(ret_k, RET_H), pos) * RET_DK ** -0.5
    log_gamma = jnp.log1p(-jnp.exp2(-5.0 - jnp.arange(RET_H, dtype=f32)))
    log_gamma = jnp.broadcast_to(log_gamma[:, None], (B, L, RET_H, 1))
    o_ret, n_ret = gated_linear_attention_chunked(q_r, k_r, _heads(ret_v, RET_H), log_gamma, s_ret)
    o_ret = _head_rms(o_ret).reshape(B, L, RET_W) * jax.nn.silu(ret_g)

    branches = jnp.stack([o_hg, o_gla, o_rw, o_ret], axis=2).astype(h.dtype)
    up = jnp.einsum('blnc,ncd->blnd', branches, lw['w_branch'])
    gates = jax.nn.sigmoid(gate_logits.reshape(B, L, N_BRANCH, D_MODEL))
    merged = jnp.sum(gates * up, axis=2).astype(h.dtype)
    y = merged @ lw['w_out']
    dt = h.dtype
    new_states = (n_hg.astype(dt), n_gla.astype(dt), n_rw.astype(dt),
                  rw_cols[:, -1].astype(dt), n_ret.astype(dt))
    return y, new_states


def decoder_layer(x, pos, states, lw):
    y, new_states = token_mixing(_rmsnorm(x, lw['attn_norm']), pos, states, lw)
    x = x + y.astype(x.dtype)
    h = _rmsnorm(x, lw['ffn_norm'])
    g, u = jnp.split(h @ lw['w_ffn_in'], 2, axis=-1)
    x = x + ((jax.nn.silu(g) * u) @ lw['w_ffn_out']).astype(x.dtype)
    return x, new_states


def setup_inputs(seed: int = 0) -> dict:
    key = jax.random.key(seed)
    keys = iter(jax.random.split(key, 40))

    def nrm(shape, scale):
        return scale * jax.random.normal(next(keys), shape, jnp.float32)

    return {
        'x_prompt': nrm((BATCH, SEQ, D_MODEL), 1.0),
        'x_sample': nrm((DEC_BATCH, DEC_SEQ, D_MODEL), 1.0),
        'state_hgrn': nrm((DEPTH, DEC_BATCH, HG_H, HG_DK, HG_DV), 0.5),
        'state_gla': nrm((DEPTH, DEC_BATCH, GLA_H, GLA_DK, GLA_DV), 0.5),
        'state_rwkv': nrm((DEPTH, DEC_BATCH, RW_H, RW_N, RW_N), 0.5),
        'state_rwkv_shift': nrm((DEPTH, DEC_BATCH, RW_COLS), 1.0),
        'state_ret': nrm((DEPTH, DEC_BATCH, RET_H, RET_DK, RET_DV), 0.5),
        'attn_norm_w': 1.0 + nrm((DEPTH, D_MODEL), 0.1),
        'w_in': nrm((DEPTH, D_MODEL, N_IN), D_MODEL ** -0.5),
        'hg_lb_logits': nrm((DEPTH, HG_H * HG_DK), 0.5),
        'hg_norm_w': 1.0 + nrm((DEPTH, HG_DV), 0.1),
        'gla_wa2': nrm((DEPTH, GLA_RANK, GLA_H * GLA_DK), GLA_RANK ** -0.5),
        'gla_ba': nrm((DEPTH, GLA_H * GLA_DK), 0.1),
        'gla_norm_w': 1.0 + nrm((DEPTH, GLA_DV), 0.1),
        'rw_mu': jax.random.uniform(next(keys), (DEPTH, RW_COLS), jnp.float32),
        'rw_w0': nrm((DEPTH, RW_W), 0.5),
        'rw_w2': nrm((DEPTH, RW_DECAY_RANK, RW_W), 0.1 * RW_DECAY_RANK ** -0.5),
        'rw_a0': nrm((DEPTH, RW_W), 0.1),
        'rw_a2': nrm((DEPTH, RW_A_RANK, RW_W), 0.1 * RW_A_RANK ** -0.5),
        'rw_g2': nrm((DEPTH, RW_G_RANK, RW_W), RW_G_RANK ** -0.5),
        'rw_kk': 0.85 + nrm((DEPTH, RW_W), 0.1),
        'rw_ka': 1.0 + nrm((DEPTH, RW_W), 0.1),
        'rw_rk': nrm((DEPTH, RW_W), 0.1),
        'rw_ln_w': 1.0 + nrm((DEPTH, RW_W), 0.1),
        'rw_ln_b': nrm((DEPTH, RW_W), 0.01),
        'w_branch': nrm((DEPTH, N_BRANCH, BRANCH_W, D_MODEL), BRANCH_W ** -0.5),
        'w_out': nrm((DEPTH, D_MODEL, D_MODEL), D_MODEL ** -0.5),
        'ffn_norm_w': 1.0 + nrm((DEPTH, D_MODEL), 0.1),
        'w_ffn_in': nrm((DEPTH, D_MODEL, 2 * D_FF), D_MODEL ** -0.5),
        'w_ffn_out': nrm((DEPTH, D_FF, D_MODEL), D_FF ** -0.5),
        'final_norm_w': 1.0 + nrm((D_MODEL,), 0.1),
    }


def reference(x_prompt, x_sample, state_hgrn, state_gla, state_rwkv, state_rwkv_shift, state_ret,
              attn_norm_w, w_in, hg_lb_logits, hg_norm_w, gla_wa2, gla_ba, gla_norm_w,
              rw_mu, rw_w0, rw_w2, rw_a0, rw_a2, rw_g2, rw_kk, rw_ka, rw_rk, rw_ln_w, rw_ln_b,
              w_branch, w_out, ffn_norm_w, w_ffn_in, w_ffn_out, final_norm_w):
    f32 = jnp.float32
    lb_p = jax.nn.softmax(hg_lb_logits.astype(f32), axis=0)
    lower_bounds = jnp.cumsum(lb_p, axis=0) - lb_p[0:1]

    Bp, Lp, _ = x_prompt.shape
    Ls = x_sample.shape[1]
    pos_p = jnp.arange(Lp, dtype=f32)
    pos_s = float(PAST_LEN) + jnp.arange(Ls, dtype=f32)
    dt = x_prompt.dtype
    zero_states = (jnp.zeros((Bp, HG_H, HG_DK, HG_DV), dt),
                   jnp.zeros((Bp, GLA_H, GLA_DK, GLA_DV), dt),
                   jnp.zeros((Bp, RW_H, RW_N, RW_N), dt),
                   jnp.zeros((Bp, RW_COLS), dt),
                   jnp.zeros((Bp, RET_H, RET_DK, RET_DV), dt))

    xp, xs = x_prompt, x_sample
    p_states, s_states = [], []
    for l in range(DEPTH):
        lw = {
            'attn_norm': attn_norm_w[l], 'w_in': w_in[l],
            'hg_lb': lower_bounds[l], 'hg_norm_w': hg_norm_w[l],
            'gla_wa2': gla_wa2[l], 'gla_ba': gla_ba[l], 'gla_norm_w': gla_norm_w[l],
            'rw_mu': rw_mu[l], 'rw_w0': rw_w0[l], 'rw_w2': rw_w2[l], 'rw_a0': rw_a0[l],
            'rw_a2': rw_a2[l], 'rw_g2': rw_g2[l], 'rw_kk': rw_kk[l], 'rw_ka': rw_ka[l],
            'rw_rk': rw_rk[l], 'rw_ln_w': rw_ln_w[l], 'rw_ln_b': rw_ln_b[l],
            'w_branch': w_branch[l], 'w_out': w_out[l],
            'ffn_norm': ffn_norm_w[l], 'w_ffn_in': w_ffn_in[l], 'w_ffn_out': w_ffn_out[l],
        }
        xp, sp = decoder_layer(xp, pos_p, zero_states, lw)
        xs, ss = decoder_layer(xs, pos_s, (state_hgrn[l], state_gla[l], state_rwkv[l],
                                           state_rwkv_shift[l], state_ret[l]), lw)
        p_states.append(sp)
        s_states.append(ss)

    y_prompt = _rmsnorm(xp, final_norm_w)
    y_sample = _rmsnorm(xs, final_norm_w)
    prompt_hgrn = jnp.stack([s[0] for s in p_states])
    prompt_gla = jnp.stack([s[1] for s in p_states])
    prompt_rwkv = jnp.stack([s[2] for s in p_states])
    prompt_rwkv_shift = jnp.stack([s[3] for s in p_states])
    prompt_ret = jnp.stack([s[4] for s in p_states])
    sample_hgrn = jnp.stack([s[0] for s in s_states])
    sample_gla = jnp.stack([s[1] for s in s_states])
    sample_rwkv = jnp.stack([s[2] for s in s_states])
    sample_rwkv_shift = jnp.stack([s[3] for s in s_states])
    sample_ret = jnp.stack([s[4] for s in s_states])
    return (y_prompt, y_sample,
            prompt_hgrn, prompt_gla, prompt_rwkv, prompt_rwkv_shift, prompt_ret,
            sample_hgrn, sample_gla, sample_rwkv, sample_rwkv_shift, sample_ret)
```

```python
import contextlib
import math
import os
import numpy as np
import concourse.bass as bass
import concourse.mybir as mybir
from concourse.bass_utils import run_bass_kernel_spmd

F32 = mybir.dt.float32
BF16 = mybir.dt.bfloat16
AF = mybir.ActivationFunctionType
ALU = mybir.AluOpType
AX = mybir.AxisListType

D = 1024
DEPTH = 2
N_CORES = 8
SEQ = 2048
DEC_SEQ = 8
DEC_PER_CORE = 16
PAST_LEN = 16384
N_IN = 7952
D_FF = 2816
NORM_EPS = 1e-6
RW_LN_EPS = 64e-5
C_HG = 0
C_GLA = 1024
C_RW = 1808
C_RET = 2832
C_GATE = 3856

ENGS = ("pe", "dve", "act", "pool", "sp")


class Buf:
    def __init__(self, name, n=1, excl=False):
        self.name = name
        self.n = n
        self.excl = excl
        self.w = [None] * n
        self.r = [dict() for _ in range(n)]

    def __getitem__(self, idx):
        if isinstance(idx, int):
            return (self, (idx,))
        if isinstance(idx, slice):
            return (self, tuple(range(*idx.indices(self.n))))
        return (self, tuple(idx))

    @property
    def all(self):
        return (self, tuple(range(self.n)))


def _cells(x):
    if isinstance(x, Buf):
        return x.all
    return x


class Prog:
    NDMA = {None: 6, "A": 3, "B": 3}

    def __init__(self, nc, es):
        self.nc = nc
        self.sem = {}
        self.cnt = {}
        self.q = {}
        self.seen = {}
        self.dma_sems = {}
        self.dma_next = {}
        for st in (None, "A", "B"):
            tag = st or "m"
            self.q[st] = {e: [] for e in ENGS}
            self.seen[st] = {e: {} for e in ENGS}
            for e in ENGS:
                k = "%s:%s" % (tag, e)
                self.sem[k] = es.enter_context(nc.semaphore("s%s_%s" % (tag, e)))
                self.cnt[k] = 0
            for iss in ("sp", "pool", "act"):
                lst = []
                for i in range(self.NDMA[st]):
                    k = "%s:d_%s%d" % (tag, iss, i)
                    self.sem[k] = es.enter_context(nc.semaphore("d%s_%s%d" % (tag, iss, i)))
                    self.cnt[k] = 0
                    lst.append(k)
                self.dma_sems[(st, iss)] = lst
                self.dma_next[(st, iss)] = 0
        self.cur = None
        self.n_instr = 0
        self.out_tokens = []
        self.glob = {"A": [], "B": []}
        self.act_tbl = None
        self.n_tbl_switch = 0

    def _push(self, eng, emit, deps=None, tok=None, cost=0.4, signals=True, tbl=None):
        if self.cur is None:
            self.q[None][eng].append(emit)
            if tbl is not None:
                self.act_tbl = tbl
        else:
            self.glob[self.cur].append(dict(eng=eng, emit=emit, deps=dict(deps or {}), tok=tok, cost=cost,
                                            signals=signals, tbl=tbl))

    def _note(self, tok, deps):
        pass

    def ek(self, eng):
        return "%s:%s" % (self.cur or "m", eng)

    def _deps(self, reads, writes, eng=None):
        deps = {}
        self._me = self.ek(eng) if eng is not None else None

        def need(tok):
            if tok is None:
                return
            k, v = tok
            if deps.get(k, 0) < v:
                deps[k] = v

        for b, cells in map(_cells, reads):
            for c in cells:
                need(b.w[c])
                if b.excl:
                    for k, v in b.r[c].items():
                        if k != self._me:
                            need((k, v))
        for b, cells in map(_cells, writes):
            for c in cells:
                need(b.w[c])
                for k, v in b.r[c].items():
                    need((k, v))
        return deps

    def _commit(self, tok, reads, writes):
        k, v = tok
        for b, cells in map(_cells, reads):
            for c in cells:
                if b.r[c].get(k, 0) < v:
                    b.r[c][k] = v
        for b, cells in map(_cells, writes):
            for c in cells:
                b.w[c] = tok
                b.r[c] = {}

    def _waits(self, eng, deps):
        ws = []
        seen = self.seen[self.cur][eng]
        me = self.ek(eng)
        for k, v in deps.items():
            if seen.get(k, 0) >= v:
                continue
            if k == me and v > self.cnt[me]:
                continue
            seen[k] = v
            ws.append((self.sem[k], v))
        return ws

    def op(self, eng, fn, reads=(), writes=(), inc=True, cost=0.4, tbl=None):
        deps = self._deps(reads, writes, eng)
        ws = self._waits(eng, deps)
        me = self.ek(eng)
        sem = self.sem[me]
        if inc:
            self.cnt[me] += 1
            tok = (me, self.cnt[me])
        else:
            tok = (me, self.cnt[me] + 1)
        self._commit(tok, reads, writes)
        self._note(tok, deps)
        self.n_instr += 1

        def emit(e, ws=ws, fn=fn, inc=inc, sem=sem):
            for s, v in ws:
                e.wait_ge(s, v)
            ins = fn(e)
            if inc:
                ins.then_inc(sem, 1)
        self._push(eng, emit, deps, tok, cost, signals=inc, tbl=tbl)
        return tok

    def dma(self, iss, out, in_, reads=(), writes=(), is_output=False):
        deps = self._deps(reads, writes)
        key = (self.cur, iss)
        i = self.dma_next[key]
        self.dma_next[key] = (i + 1) % len(self.dma_sems[key])
        k = self.dma_sems[key][i]
        if self.cnt[k] > 0:
            if deps.get(k, 0) < self.cnt[k]:
                deps[k] = self.cnt[k]
        ws = self._waits(iss, deps)
        self.cnt[k] += 16
        tok = (k, self.cnt[k])
        self._commit(tok, reads, writes)
        self._note(tok, deps)
        sem = self.sem[k]
        self.n_instr += 1
        if is_output:
            self.out_tokens.append(tok)

        def emit(e, ws=ws, sem=sem, out=out, in_=in_):
            for s, v in ws:
                e.wait_ge(s, v)
            e.dma_start(out=out, in_=in_).then_inc(sem, 16)
        self._push(iss, emit, deps, tok, 2.5, signals=True)
        return tok

    def merge_streams(self):
        L = {"A": self.glob["A"], "B": self.glob["B"]}
        LAT = float(os.environ.get("MK_LAT", "0.6"))
        TBL = float(os.environ.get("MK_TBL", "1.3"))
        prod = {}
        for st in ("A", "B"):
            for idx, ins in enumerate(L[st]):
                if ins["signals"] and ins["tok"] is not None:
                    prod[ins["tok"]] = (st, idx)
        fin = {"A": [0.0] * len(L["A"]), "B": [0.0] * len(L["B"])}
        placed = {"A": 0, "B": 0}
        t_free = {e: 0.0 for e in ENGS}

        def start_time(st):
            i = placed[st]
            if i >= len(L[st]):
                return None
            ins = L[st][i]
            ready = 0.0
            for tk in ins["deps"].items():
                p = prod.get(tk)
                if p is None:
                    continue
                pst, pidx = p
                if pidx >= placed[pst]:
                    if pst != st:
                        return float("inf")
                    continue
                lat = LAT if L[pst][pidx]["eng"] != ins["eng"] else 0.05
                ready = max(ready, fin[pst][pidx] + lat)
            pen = TBL if (ins["tbl"] is not None and ins["tbl"] != tblstate[0]) else 0.0
            return max(ready, t_free[ins["eng"]]) + pen

        tblstate = [self.act_tbl]
        rem = {}
        for st in ("A", "B"):
            r = [0.0] * (len(L[st]) + 1)
            for i in range(len(L[st]) - 1, -1, -1):
                r[i] = r[i + 1] + L[st][i]["cost"]
            rem[st] = r
        BIAS = float(os.environ.get("MK_BIAS", "0.03"))
        while placed["A"] < len(L["A"]) or placed["B"] < len(L["B"]):
            sa, sb = start_time("A"), start_time("B")
            EF = float(os.environ.get("MK_EF", "1.0"))
            ka = None if sa is None else sa + EF * L["A"][placed["A"]]["cost"] - BIAS * rem["A"][placed["A"]]
            kb = None if sb is None else sb + EF * L["B"][placed["B"]]["cost"] - BIAS * rem["B"][placed["B"]]
            if sb is None or (sa is not None and ka <= kb):
                st, t0 = "A", sa
            else:
                st, t0 = "B", sb
            if t0 == float("inf"):
                st = "B" if st == "A" else "A"
                t0 = start_time(st)
                assert t0 is not None and t0 != float("inf")
            ins = L[st][placed[st]]
            e = ins["eng"]
            if ins["tbl"] is not None and ins["tbl"] != tblstate[0]:
                tblstate[0] = ins["tbl"]
                self.n_tbl_switch += 1
            is_dma = ins["cost"] >= 2.0 and e in ("sp", "pool")
            fin[st][placed[st]] = t0 + ins["cost"]
            t_free[e] = t0 + (0.15 if is_dma else ins["cost"])
            placed[st] += 1
            self.q[None][e].append(ins["emit"])
        self.est_span = max(t_free.values())
        self.act_tbl = tblstate[0]
        self.glob = {"A": [], "B": []}

    def finish(self):
        assert self.cur is None
        deps = {}
        for k, v in self.out_tokens:
            deps[k] = max(deps.get(k, 0), v)
        for k in self.sem:
            if k != "m:sp" and self.cnt[k] > 0:
                deps[k] = max(deps.get(k, 0), self.cnt[k])
        ws = self._waits("sp", deps)

        def emit(e, ws=ws):
            for s, v in ws:
                e.wait_ge(s, v)
        self.q[None]["sp"].append(emit)

    def emit_all(self):
        nc = self.nc
        q = self.q[None]
        with nc.Block() as block:
            @block.tensor
            def _(e):
                for f in q["pe"]:
                    f(e)

            @block.vector
            def _(e):
                for f in q["dve"]:
                    f(e)

            @block.scalar
            def _(e):
                for f in q["act"]:
                    f(e)

            @block.gpsimd
            def _(e):
                for f in q["pool"]:
                    f(e)

            @block.sync
            def _(e):
                for f in q["sp"]:
                    f(e)


class T:
    def __init__(self, ap, d):
        self.ap = ap
        self.d = d


def _dl(lst):
    out = []
    for x in lst:
        if isinstance(x, T):
            out.append(x.d)
        elif x is not None:
            out.append(x)
    return out


_SZ = {F32: 4, BF16: 2}


class Arena:
    def __init__(self, nc, es, name, nbytes, cell=512):
        self.t = es.enter_context(nc.sbuf_tensor(name, [128, nbytes // 2], BF16))
        self.cell = cell
        self.nbytes = nbytes
        self.buf = Buf(name, (nbytes + cell - 1) // cell)
        self.top = 0

    def view(self, off, shape, dt):
        n = 1
        for s in shape:
            n *= s
        nb = n * _SZ[dt]
        assert off % 4 == 0 and off + nb <= self.nbytes, (off, nb, self.nbytes)
        ap = self.t[:, off // 2:(off + nb) // 2]
        if dt == F32:
            ap = ap.bitcast(F32)
        if len(shape) == 2:
            ap = ap.rearrange("p (a b) -> p a b", a=shape[0])
        elif len(shape) == 3:
            ap = ap.rearrange("p (a b c) -> p a b c", a=shape[0], b=shape[1])
        elif len(shape) == 4:
            ap = ap.rearrange("p (a b c d) -> p a b c d", a=shape[0], b=shape[1], c=shape[2])
        cells = tuple(range(off // self.cell, (off + nb + self.cell - 1) // self.cell))
        return T(ap, (self.buf, cells))

    def alloc(self, shape, dt):
        n = 1
        for s in shape:
            n *= s
        nb = n * _SZ[dt]
        off = (self.top + self.cell - 1) // self.cell * self.cell
        self.top = off + nb
        return self.view(off, shape, dt)

    def mark(self):
        return self.top

    def reset(self, m=0):
        self.top = m


class K:
    def __init__(self, nc, es):
        self.nc = nc
        self.es = es
        self.P = Prog(nc, es)
        self.uid = 0

    def sb(self, name, shape, dt=F32, cells=1):
        t = self.es.enter_context(self.nc.sbuf_tensor(name, shape, dt))
        return t, Buf(name, cells)

    @staticmethod
    def _n(ap):
        n = 1
        for d in ap.shape[1:]:
            n *= d
        return n

    def MM(self, out, lhsT, rhs, start=True, stop=True, r=(), w=(), inc=True):
        fp32 = 4.0 if lhsT.dtype == F32 else 1.0
        return self.P.op("pe", lambda e: e.matmul(out, lhsT, rhs, start=start, stop=stop),
                         reads=_dl(r), writes=_dl(w), inc=inc, cost=0.06 + fp32 * self._n(out) / 1500.0)

    def TR(self, out, in_, ident, r=(), w=(), inc=True):
        return self.P.op("pe", lambda e: e.transpose(out, in_, ident), reads=_dl(r), writes=_dl(w), inc=inc,
                         cost=0.25)

    def ACT(self, out, in_, func, r=(), w=(), bias=None, scale=None, accum=None):
        kw = {}
        if bias is not None:
            kw["bias"] = bias
        if scale is not None:
            kw["scale"] = scale
        if accum is not None:
            kw["accum_out"] = accum
        tbl = "T" if func in (AF.Tanh, AF.Silu) else ("L" if func == AF.Ln else None)
        return self.P.op("act", lambda e: e.activation(out=out, in_=in_, func=func, **kw),
                         reads=_dl(r), writes=_dl(w), cost=0.25 + self._n(out) / 1400.0, tbl=tbl)

    def TT(self, out, in0, in1, op, r=(), w=(), eng="dve"):
        return self.P.op(eng, lambda e: e.tensor_tensor(out, in0, in1, op), reads=_dl(r), writes=_dl(w),
                         cost=0.2 + self._n(out) / 1000.0)

    def TS(self, out, in0, s1, op0, s2=None, op1=None, r=(), w=(), eng="dve"):
        if op1 is None:
            return self.P.op(eng, lambda e: e.tensor_scalar(out, in0, s1, None, op0), reads=_dl(r), writes=_dl(w),
                         cost=0.2 + self._n(out) / 1000.0)
        return self.P.op(eng, lambda e: e.tensor_scalar(out, in0, s1, s2, op0, op1), reads=_dl(r), writes=_dl(w),
                         cost=0.2 + self._n(out) / 1000.0)

    def STT(self, out, in0, scalar, in1, op0, op1, r=(), w=(), eng="dve"):
        return self.P.op(eng, lambda e: e.scalar_tensor_tensor(out, in0, scalar, in1, op0, op1),
                         reads=_dl(r), writes=_dl(w), cost=0.2 + self._n(out) / 1000.0)

    def CP(self, out, in_, r=(), w=(), eng="dve"):
        if eng == "act":
            return self.ACT(out, in_, AF.Copy, r=r, w=w)
        return self.P.op(eng, lambda e: e.tensor_copy(out, in_), reads=_dl(r), writes=_dl(w),
                         cost=0.2 + self._n(out) / 1000.0)

    def RSUM(self, out, in_, r=(), w=(), eng="dve"):
        return self.P.op(eng, lambda e: e.reduce_sum(out, in_, axis=AX.X), reads=_dl(r), writes=_dl(w),
                         cost=0.2 + self._n(out) / 1000.0)

    def MEMSET(self, out, val, w=(), eng="dve"):
        return self.P.op(eng, lambda e: e.memset(out, val), writes=_dl(w))

    def DMA(self, out, in_, r=(), w=(), iss="sp", is_output=False):
        return self.P.dma(iss, out, in_, reads=_dl(r), writes=_dl(w), is_output=is_output)


C_ID, C_MCP, C_MCS, C_TIP, C_TIS, C_TSP, C_TSS, C_LP, C_LS = [i * 128 for i in range(9)]
C_INDP = 9 * 128
C_INDS = C_INDP + 4
C_ROWS = C_INDS + 32
C_SHP = C_ROWS + 16
C_SHS = C_SHP + 128
C_CAR = C_SHS + 128
C_SEL = C_CAR + 128
C_BD = C_SEL + 128
NC128 = C_BD + 128


def _chunk_consts(C):
    n = 128 // C
    ch = np.arange(128) // C
    same = ch[:, None] == ch[None, :]
    s = np.arange(128)[:, None]
    t = np.arange(128)[None, :]
    mid = (ch * C + (C // 2 - 1))
    mcum = (same & (s <= t)).astype(np.float32) - (same & (s <= mid[None, :])).astype(np.float32)
    ti = (same & (s <= t)).astype(np.float32)
    tstrict = (same & (s < t)).astype(np.float32)
    low = (same & (s > t)).astype(np.float32)
    ind = np.zeros((128, 2 * n), np.float32)
    for c in range(n):
        rows = np.arange(c * C, (c + 1) * C)
        m = c * C + C // 2 - 1
        ind[rows[rows <= m], 2 * c] = 1.0
        ind[rows[rows > m], 2 * c + 1] = 1.0
    return mcum, ti, tstrict, low, ind


def make_consts(npt):
    c = np.zeros((128, NC128), np.float32)
    c[:, C_ID:C_ID + 128] = np.eye(128, dtype=np.float32)
    mp = _chunk_consts(64)
    ms = _chunk_consts(8)
    c[:, C_MCP:C_MCP + 128], c[:, C_TIP:C_TIP + 128], c[:, C_TSP:C_TSP + 128], c[:, C_LP:C_LP + 128] = mp[:4]
    c[:, C_MCS:C_MCS + 128], c[:, C_TIS:C_TIS + 128], c[:, C_TSS:C_TSS + 128], c[:, C_LS:C_LS + 128] = ms[:4]
    c[:, C_INDP:C_INDP + 4] = mp[4]
    c[:, C_INDS:C_INDS + 32] = ms[4]
    seq = np.arange(128) // 8
    c[:, C_ROWS:C_ROWS + 16] = (seq[:, None] == np.arange(16)[None, :]).astype(np.float32)
    sh = np.zeros((128, 128), np.float32)
    sh[np.arange(127), np.arange(1, 128)] = 1.0
    c[:, C_SHP:C_SHP + 128] = sh
    shs = sh.copy()
    shs[:, np.arange(0, 128, 8)] = 0.0
    c[:, C_SHS:C_SHS + 128] = shs
    c[127, C_CAR] = 1.0
    for q in range(16):
        c[q, C_SEL + 8 * q] = 1.0
    blk = np.arange(128) // 64
    c[:, C_BD:C_BD + 128] = (blk[:, None] == blk[None, :]).astype(np.float32)
    colmask = np.broadcast_to((np.arange(16)[:, None] == seq[None, :]).astype(np.float32)[None], (128, 16, 128))
    colmask = np.ascontiguousarray(colmask).reshape(128, 2048)
    inv = np.power(np.float32(10000.0), -(np.arange(32, dtype=np.float32) / np.float32(32))).astype(np.float32)
    rot = np.zeros((npt + 1, 128, 64), np.float32)
    for i in range(npt + 1):
        if i < npt:
            pos = (np.arange(128) + 128 * i).astype(np.float32)
        else:
            pos = (np.float32(PAST_LEN) + (np.arange(128) % 8).astype(np.float32)).astype(np.float32)
        ang = (pos[:, None] * inv[None, :]).astype(np.float32)
        rot[i, :, :32] = np.cos(ang)
        rot[i, :, 32:] = np.sin(ang)
    return c, colmask, rot


PB_NORM, PB_MU, PB_LB, PB_HGN, PB_GBA, PB_GLN, PB_W0, PB_A0, PB_KK, PB_KA, PB_RK, PB_LNW, PB_LNB = (
    0, 1024, 2048, 2560, 2624, 2752, 2816, 3072, 3328, 3584, 3840, 4096, 4352)
NPB = 4608


def build(npt, passes, mixers=("hg", "gla", "rw", "ret"), depth=DEPTH, debug=None):
    nc = bass.Bass("TRN2", target_bir_lowering=False)
    es = contextlib.ExitStack()
    k = K(nc, es)
    P = k.P
    NT = npt + 1
    NTP = max(len(p) for p in passes)
    NTOKP = NTP * 128

    def din(name, shape):
        return nc.dram_tensor(name, list(shape), F32, kind="ExternalInput").ap()

    def dout(name, shape):
        return nc.dram_tensor(name, list(shape), F32, kind="ExternalOutput").ap()

    xin = din("xin", [NT * 128, D])
    st_in = {"hg": din("st_hg", [DEPTH, 16, 4, 64, 64]), "gla": din("st_gla", [DEPTH, 16, 4, 32, 64]),
             "rw": din("st_rw", [DEPTH, 16, 4, 64, 64]), "ret": din("st_ret", [DEPTH, 16, 4, 64, 64])}
    st_shift = din("st_shift", [DEPTH, 16, 1024])
    W = {}
    for name, shape in (("attn_norm_w", [2, 1024]), ("w_in", [2, 1024, N_IN]), ("hg_lb_logits", [2, 256]),
                        ("hg_norm_w", [2, 64]), ("gla_wa2", [2, 16, 128]), ("gla_ba", [2, 128]),
                        ("gla_norm_w", [2, 64]), ("rw_mu", [2, 1024]), ("rw_w0", [2, 256]),
                        ("rw_w2", [2, 64, 256]), ("rw_a0", [2, 256]), ("rw_a2", [2, 64, 256]),
                        ("rw_g2", [2, 128, 256]), ("rw_kk", [2, 256]), ("rw_ka", [2, 256]), ("rw_rk", [2, 256]),
                        ("rw_ln_w", [2, 256]), ("rw_ln_b", [2, 256]), ("w_branch", [2, 4, 256, 1024]),
                        ("w_out", [2, 1024, 1024]), ("ffn_norm_w", [2, 1024]), ("w_ffn_in", [2, 1024, 2 * D_FF]),
                        ("w_ffn_out", [2, D_FF, 1024]), ("final_norm_w", [1024])):
        W[name] = din(name, shape)
    c128_d = din("c128", [128, NC128])
    colmask_d = din("colmask", [128, 2048])
    rot_d = din("rot", [NT, 128, 64])

    yout = dout("yout", [NT * 128, D])
    p_out = {"hg": dout("p_hg", [DEPTH, 4, 64, 64]), "gla": dout("p_gla", [DEPTH, 4, 32, 64]),
             "rw": dout("p_rw", [DEPTH, 4, 64, 64]), "ret": dout("p_ret", [DEPTH, 4, 64, 64])}
    p_shift = dout("p_shift", [DEPTH, 1024])
    s_out = {"hg": dout("s_hg", [DEPTH, 16, 4, 64, 64]), "gla": dout("s_gla", [DEPTH, 16, 4, 32, 64]),
             "rw": dout("s_rw", [DEPTH, 16, 4, 64, 64]), "ret": dout("s_ret", [DEPTH, 16, 4, 64, 64])}
    s_shift = dout("s_shift", [DEPTH, 16, 1024])
    dbg_out = {}
    if debug:
        for name, shape in debug.items():
            dbg_out[name] = dout("dbg_" + name, shape)

    x_t, x_b = k.sb("x", [128, NTP, D], F32, cells=NTP)
    hT_t, hT_b = k.sb("hT", [128, 8, NTOKP], BF16, cells=NTP)
    oT_t, oT_b = k.sb("oT", [128, 8, NTOKP], BF16, cells=NTP * 4)
    c128_t, c128_b = k.sb("c128s", [128, NC128], F32)
    cm_t, cm_b = k.sb("colmask_s", [128, 16, 128], BF16)
    idb_t, idb_b = k.sb("identb", [128, 128], BF16)
    zl_t, zl_b = k.sb("zerol", [128, 128], BF16)
    pb_t, pb_b = k.sb("pbc", [128, NPB], F32, cells=16)
    lbc_t, lbc_b = k.sb("lbc", [128, 2, 256], F32)
    cst_t, cst_b = k.sb("cst", [128, 8], F32)
    rwl_t, rwl_b = k.sb("rwl", [128, 3, 256], BF16)
    gwa_t, gwa_b = k.sb("gwa", [32, 128], BF16)
    S_t, S_b = k.sb("Sst", [128, DEPTH * 4, 2, 64], F32, cells=DEPTH * 4)
    rwp_t, rwp_b = k.sb("rwprev", [128, DEPTH, 1024], F32, cells=DEPTH)
    gconst_t, gconst_b = k.sb("gconst", [128, 256], F32)
    rot_t, rot_b = k.sb("rots", [128, 2, 64], F32, cells=2)
    wa = Arena(nc, es, "warena", 32768, cell=2048)
    sc = Arena(nc, es, "scratch", 57344, cell=256)

    psA = es.enter_context(nc.psum_tensor("psA", [128, 8, 512], F32))
    psA_b = Buf("psA", 8, excl=True)
    rr = {}
    pools = {None: dict(big=[0, 1], small=[4, 5, 6, 7]), "A": dict(big=[0], small=[4, 5]),
             "B": dict(big=[1], small=[6, 7])}

    def ps_big():
        st = P.cur
        lst = pools[st]["big"]
        i = rr.get((st, "big"), 0)
        rr[(st, "big")] = (i + 1) % len(lst)
        b = lst[i]
        ap = psA[:, 2 * b:2 * b + 2, :].rearrange("p a b -> p (a b)")
        return T(ap, (psA_b, (2 * b, 2 * b + 1)))

    def ps_small():
        st = P.cur
        lst = pools[st]["small"]
        i = rr.get((st, "small"), 0)
        rr[(st, "small")] = (i + 1) % len(lst)
        b = lst[i]
        return T(psA[:, b, :], (psA_b, (b,)))

    def ps_tb():
        t = ps_small()
        return T(t.ap.bitcast(BF16), t.d)

    ident = T(idb_t[:], idb_b.all)

    def cst(i):
        return cst_t[:, i:i + 1]

    k.DMA(c128_t[:], c128_d[:, :], w=[c128_b])
    k.DMA(cm_t[:].rearrange("p a b -> p (a b)"), colmask_d[:, :], w=[cm_b], iss="pool")
    k.DMA(idb_t[:], c128_d[:, C_ID:C_ID + 128], w=[idb_b], iss="pool")
    k.TS(zl_t[:], idb_t[:], 0.0, ALU.mult, r=[idb_b], w=[zl_b])
    CST_EPSD, CST_ONE, CST_LNH, CST_EPS64, CST_EPSLN, CST_TINY = 0, 1, 2, 3, 4, 5
    k.MEMSET(cst_t[:, 0:1], D * NORM_EPS, w=[cst_b])
    k.MEMSET(cst_t[:, 1:2], 1.0, w=[cst_b])
    k.MEMSET(cst_t[:, 2:3], math.log(0.5), w=[cst_b])
    k.MEMSET(cst_t[:, 3:4], NORM_EPS, w=[cst_b])
    k.MEMSET(cst_t[:, 4:5], RW_LN_EPS, w=[cst_b])
    k.MEMSET(cst_t[:, 5:6], 1e-24, w=[cst_b])
    for h in range(4):
        k.MEMSET(gconst_t[:, 64 * h:64 * h + 64], math.log1p(-2.0 ** (-5.0 - h)), w=[gconst_b])
    k.MEMSET(S_t[:].rearrange("p a b c -> p (a b c)"), 0.0, w=[S_b])
    k.MEMSET(rwp_t[:].rearrange("p a b -> p (a b)"), 0.0, w=[rwp_b])
    cmat = {"p": dict(mcum=C_MCP, ti=C_TIP, ts=C_TSP, low=C_LP, ind=C_INDP, nind=4),
            "s": dict(mcum=C_MCS, ti=C_TIS, ts=C_TSS, low=C_LS, ind=C_INDS, nind=32)}

    def cm(kind, name):
        o = cmat[kind][name]
        n = cmat[kind]["nind"] if name == "ind" else 128
        return c128_t[:, o:o + n]

    ctx = dict(nc=nc, k=k, P=P, npt=npt, NTP=NTP, NTOKP=NTOKP, W=W, st_in=st_in, st_shift=st_shift,
               p_out=p_out, p_shift=p_shift, s_out=s_out, s_shift=s_shift, x_t=x_t, x_b=x_b, hT_t=hT_t, hT_b=hT_b,
               oT_t=oT_t, oT_b=oT_b, c128_t=c128_t, c128_b=c128_b, cm_t=cm_t, cm_b=cm_b, ident=ident, zl=T(zl_t[:], zl_b.all),
               pb_t=pb_t, pb_b=pb_b, lbc_t=lbc_t, lbc_b=lbc_b, cst=cst, cst_b=cst_b, rwl_t=rwl_t, rwl_b=rwl_b,
               gwa_t=gwa_t, gwa_b=gwa_b, S_t=S_t, S_b=S_b, rwp_t=rwp_t, rwp_b=rwp_b, gconst_t=gconst_t,
               gconst_b=gconst_b, rot_t=rot_t, rot_b=rot_b, rot_d=rot_d, wa=wa, sc=sc, ps_big=ps_big,
               ps_small=ps_small, ps_tb=ps_tb, cm=cm, dbg_out=dbg_out, xin=xin, yout=yout,
               CST=dict(EPSD=0, ONE=1, LNH=2, EPS64=3, EPSLN=4, TINY=5), mixers=mixers, depth=depth)
    g = Gen(ctx)

    for pi, tiles in enumerate(passes):
        g.load_x(tiles)
        g.final_done = False
        g.attn_norm_done = False
        for l in range(depth):
            g.layer(l, tiles, pi)
        if not g.final_done:
            g.final(tiles)
    P.finish()
    P.emit_all()
    es.close()
    return nc


class Gen:
    def __init__(self, ctx):
        self.__dict__.update(ctx)

    def tile_kind(self, t):
        return "p" if t < self.npt else "s"

    def pbc(self, off, n):
        return T(self.pb_t[:, off:off + n], (self.pb_b, tuple(range(off // 288, (off + n - 1) // 288 + 1))))

    def load_pb(self, off, src_row):
        n = src_row.shape[0]
        t = self.pbc(off, n)
        self.k.DMA(t.ap, src_row.partition_broadcast(128), w=[t])
        return t

    def load_x(self, tiles):
        k = self.k
        for j, t in enumerate(tiles):
            k.DMA(self.x_t[:, j, :], self.xin[t * 128:(t + 1) * 128, :], w=[self.x_b[j]])

    def rms_stats(self, j, sc_junk, rstd):
        k = self.k
        ss = self.sc.alloc([1], F32)
        lnv = self.sc.alloc([1], F32)
        k.ACT(sc_junk.ap, self.x_t[:, j, :], AF.Square, r=[self.x_b[j]], w=[sc_junk, ss], accum=ss.ap)
        k.ACT(lnv.ap, ss.ap, AF.Ln, r=[ss, self.cst_b], w=[lnv], bias=self.cst(self.CST["EPSD"]))
        k.ACT(rstd.ap, lnv.ap, AF.Exp, r=[lnv], w=[rstd], scale=-0.5)

    def norm_prep(self, wrow):
        k = self.k
        wt = self.load_pb(PB_NORM, wrow)
        k.TS(wt.ap, wt.ap, float(math.sqrt(D)), ALU.mult, r=[wt], w=[wt])
        return wt

    def norm_alloc(self):
        return [dict(junk=self.sc.alloc([D], F32), hb=self.sc.alloc([D], BF16), rstd=self.sc.alloc([1], F32),
                     ss=self.sc.alloc([1], F32), lnv=self.sc.alloc([1], F32), y=None) for _ in range(2)]

    def rms_stats2(self, j, s):
        k = self.k
        k.ACT(s["junk"].ap, self.x_t[:, j, :], AF.Square, r=[self.x_b[j]], w=[s["junk"], s["ss"]], accum=s["ss"].ap)
        k.ACT(s["lnv"].ap, s["ss"].ap, AF.Ln, r=[s["ss"], self.cst_b], w=[s["lnv"]], bias=self.cst(self.CST["EPSD"]))
        k.ACT(s["rstd"].ap, s["lnv"].ap, AF.Exp, r=[s["lnv"]], w=[s["rstd"]], scale=-0.5)

    def norm_tile(self, j, wt, s, part="ab"):
        k = self.k
        if "a" in part:
            self.rms_stats2(j, s)
            k.STT(s["hb"].ap, self.x_t[:, j, :], s["rstd"].ap[:, 0:1], wt.ap, ALU.mult, ALU.mult,
                  r=[self.x_b[j], s["rstd"], wt], w=[s["hb"]])
        if "b" not in part:
            return
        tb = self.ps_tb()
        for kt in range(8):
            k.TR(tb.ap[:, kt * 128:(kt + 1) * 128], s["hb"].ap[:, kt * 128:(kt + 1) * 128], self.ident.ap,
                 r=[s["hb"], self.ident], w=[tb], inc=(kt == 7))
        k.CP(self.hT_t[:, :, j * 128:(j + 1) * 128], tb.ap.rearrange("p (a b) -> p a b", a=8), r=[tb],
             w=[self.hT_b[j]], eng=("act" if j % 2 == 0 else "dve"))

    def lagged_norm(self, wt, sets):
        prev = [None]

        def step(j):
            self.norm_tile(j, wt, sets[j % 2], part="a")
            if prev[0] is not None:
                self.norm_tile(prev[0], wt, sets[prev[0] % 2], part="b")
            prev[0] = j

        def flush():
            if prev[0] is not None:
                self.norm_tile(prev[0], wt, sets[prev[0] % 2], part="b")
                prev[0] = None
        return step, flush

    def norm_phase(self, tiles, wrow):
        wt = self.norm_prep(wrow)
        m = self.sc.mark()
        sets = self.norm_alloc()
        for j, t in enumerate(tiles):
            self.norm_tile(j, wt, sets[j % 2])
        self.sc.reset(m)

    def final_tile(self, j, t, wt, s):
        k = self.k
        self.rms_stats2(j, s)
        y = s["junk"]
        k.STT(y.ap, self.x_t[:, j, :], s["rstd"].ap[:, 0:1], wt.ap, ALU.mult, ALU.mult,
              r=[self.x_b[j], s["rstd"], wt], w=[y])
        k.DMA(self.yout[t * 128:(t + 1) * 128, :], y.ap, r=[y], is_output=True)

    def final(self, tiles):
        wt = self.norm_prep(self.W["final_norm_w"])
        m = self.sc.mark()
        sets = self.norm_alloc()
        for j, t in enumerate(tiles):
            self.final_tile(j, t, wt, sets[j % 2])
        self.sc.reset(m)

    def tgroups(self, ntiles):
        ng = (ntiles + 3) // 4
        base, extra = divmod(ntiles, ng)
        gs = []
        j = 0
        for i in range(ng):
            n = base + (1 if i < extra else 0)
            gs.append((j, n))
            j += n
        return gs

    def ffn_phase(self, l, tiles, after=None):
        k = self.k
        nt = len(tiles)
        if not self.ffn_norm_done:
            self.norm_phase(tiles, self.W["ffn_norm_w"][l])
        m = self.sc.mark()
        after_fn = after() if after is not None else None
        sgs = [self.sc.alloc([512], F32) for _ in range(2)]
        acts = [self.sc.alloc([2, 512], BF16) for _ in range(3)]
        nchunk = D_FF // 256
        wfi = self.W["w_ffn_in"][l].rearrange("(kt p) n -> p kt n", p=128)
        wfo = self.W["w_ffn_out"][l].rearrange("(s p) n -> p s n", p=128)
        cnt = 0
        pending = None

        def emit_y(act, wo, j0, n, last):
            for j in range(j0, j0 + n):
                yp = self.ps_big()
                cc = (j - j0) * 128
                for half in range(2):
                    for s in range(2):
                        k.MM(yp.ap[:, half * 512:(half + 1) * 512], act.ap[:, s, cc:cc + 128],
                             wo.ap[:, s, half * 512:(half + 1) * 512], start=(s == 0), stop=(s == 1),
                             r=[act, wo], w=[yp], inc=(half == 1 and s == 1))
                k.TT(self.x_t[:, j, :], self.x_t[:, j, :], yp.ap, ALU.add, r=[self.x_b[j], yp], w=[self.x_b[j]])
                if last and after_fn is not None:
                    after_fn(j)

        for c in range(nchunk):
            slot = c % 2
            base = slot * 12288
            wg = self.wa.view(base, [8, 256], BF16)
            wu = self.wa.view(base + 4096, [8, 256], BF16)
            wo = self.wa.view(base + 8192, [2, 1024], BF16)
            k.DMA(wg.ap, wfi[:, :, c * 256:(c + 1) * 256], w=[wg], iss="pool")
            k.DMA(wu.ap, wfi[:, :, D_FF + c * 256:D_FF + (c + 1) * 256], w=[wu], iss="pool")
            k.DMA(wo.ap, wfo[:, 2 * c:2 * c + 2, :], w=[wo], iss="pool")
            for (j0, n) in self.tgroups(nt):
                ntok = n * 128
                c0 = j0 * 128
                hdeps = [self.hT_b[j] for j in range(j0, j0 + n)]
                act = acts[cnt % 3]
                cnt += 1
                for s in range(2):
                    gp = self.ps_small()
                    up = self.ps_small()
                    for kt in range(8):
                        k.MM(gp.ap[:, :ntok], wg.ap[:, kt, s * 128:(s + 1) * 128], self.hT_t[:, kt, c0:c0 + ntok],
                             start=(kt == 0), stop=(kt == 7), r=[wg] + hdeps, w=[gp], inc=(kt == 7))
                    for kt in range(8):
                        k.MM(up.ap[:, :ntok], wu.ap[:, kt, s * 128:(s + 1) * 128], self.hT_t[:, kt, c0:c0 + ntok],
                             start=(kt == 0), stop=(kt == 7), r=[wu] + hdeps, w=[up], inc=(kt == 7))
                    sg = sgs[s]
                    k.ACT(sg.ap[:, :ntok], gp.ap[:, :ntok], AF.Silu, r=[gp], w=[sg])
                    k.TT(act.ap[:, s, :ntok], sg.ap[:, :ntok], up.ap[:, :ntok], ALU.mult, r=[sg, up], w=[act])
                if pending is not None:
                    emit_y(*pending)
                pending = (act, wo, j0, n, c == nchunk - 1)
        if pending is not None:
            emit_y(*pending)
        if getattr(self, "after_flush", None) is not None:
            self.after_flush()
            self.after_flush = None
        self.sc.reset(m)

    def merge_phase(self, l, tiles):
        k = self.k
        nt = len(tiles)
        m = self.sc.mark()
        mT = self.sc.alloc([8, nt * 128], BF16)
        sgs = [self.sc.alloc([512], F32) for _ in range(2)]
        tmps = [self.sc.alloc([512], F32) for _ in range(2)]
        accs = [self.sc.alloc([512], F32) for _ in range(2)]
        w_in = self.W["w_in"][l]
        wgate_src = w_in[:, C_GATE:C_GATE + 4096].rearrange("(kt p) (b d) -> p kt b d", p=128, b=4)
        wbr_src = self.W["w_branch"][l].rearrange("b (ct p) d -> p ct b d", p=128)
        cnt = 0
        for ds in range(8):
            base = (ds % 2) * 12288
            wg = self.wa.view(base, [8, 4, 128], BF16)
            wb = self.wa.view(base + 8192, [2, 4, 128], BF16)
            for b in range(4):
                k.DMA(wg.ap[:, :, b, :], wgate_src[:, :, b, ds * 128:(ds + 1) * 128], w=[wg], iss="pool")
                k.DMA(wb.ap[:, :, b, :], wbr_src[:, :, b, ds * 128:(ds + 1) * 128], w=[wb], iss="pool")
            for (j0, n) in self.tgroups(nt):
                ntok = n * 128
                c0 = j0 * 128
                hdeps = [self.hT_b[j] for j in range(j0, j0 + n)]
                acc = accs[cnt % 2]
                cnt += 1
                for b in range(4):
                    odeps = [self.oT_b[j * 4 + b] for j in range(j0, j0 + n)]
                    gp = self.ps_small()
                    for kt in range(8):
                        k.MM(gp.ap[:, :ntok], wg.ap[:, kt, b, :], self.hT_t[:, kt, c0:c0 + ntok],
                             start=(kt == 0), stop=(kt == 7), r=[wg] + hdeps, w=[gp], inc=(kt == 7))
                    up = self.ps_small()
                    for ct in range(2):
                        k.MM(up.ap[:, :ntok], wb.ap[:, ct, b, :], self.oT_t[:, 2 * b + ct, c0:c0 + ntok],
                             start=(ct == 0), stop=(ct == 1), r=[wb] + odeps, w=[up], inc=(ct == 1))
                    sg = sgs[b % 2]
                    k.ACT(sg.ap[:, :ntok], gp.ap[:, :ntok], AF.Tanh, r=[gp], w=[sg], scale=0.5)
                    if b == 0:
                        k.STT(acc.ap[:, :ntok], sg.ap[:, :ntok], 1.0, up.ap[:, :ntok], ALU.add, ALU.mult,
                              r=[sg, up], w=[acc])
                    else:
                        tmp = tmps[b % 2]
                        k.STT(tmp.ap[:, :ntok], sg.ap[:, :ntok], 1.0, up.ap[:, :ntok], ALU.add, ALU.mult,
                              r=[sg, up], w=[tmp])
                        k.TT(acc.ap[:, :ntok], acc.ap[:, :ntok], tmp.ap[:, :ntok], ALU.add, r=[acc, tmp], w=[acc])
                k.ACT(mT.ap[:, ds, c0:c0 + ntok], acc.ap[:, :ntok], AF.Copy, r=[acc], w=[mT], scale=0.5)
        wt_n = self.norm_prep(self.W["ffn_norm_w"][l])
        nstep, nflush = self.lagged_norm(wt_n, self.norm_alloc())
        wo = self.wa.view(0, [8, 1024], BF16)
        k.DMA(wo.ap, self.W["w_out"][l].rearrange("(kt p) n -> p kt n", p=128), w=[wo], iss="pool")
        for j in range(nt):
            yp = self.ps_big()
            for half in range(2):
                for kt in range(8):
                    k.MM(yp.ap[:, half * 512:(half + 1) * 512], mT.ap[:, kt, j * 128:(j + 1) * 128],
                         wo.ap[:, kt, half * 512:(half + 1) * 512], start=(kt == 0), stop=(kt == 7),
                         r=[mT, wo], w=[yp], inc=(half == 1 and kt == 7))
            k.TT(self.x_t[:, j, :], self.x_t[:, j, :], yp.ap, ALU.add, r=[self.x_b[j], yp], w=[self.x_b[j]])
            nstep(j)
        nflush()
        self.ffn_norm_done = True
        self.sc.reset(m)

    def layer(self, l, tiles, pi):
        k = self.k
        ph = os.environ.get("PHASES", "nlmgf")
        self.ffn_norm_done = False
        if "n" in ph and not getattr(self, "attn_norm_done", False):
            self.norm_phase(tiles, self.W["attn_norm_w"][l])
        self.attn_norm_done = False
        if "l" in ph:
            self.load_layer_params(l)
        if "m" not in ph:
            if "g" in ph:
                self.merge_phase(l, tiles)
            if "f" in ph:
                self.ffn_phase(l, tiles)
            return
        jts = list(enumerate(tiles))
        pj = [(j, t) for (j, t) in jts if t < self.npt]
        sj = [(j, t) for (j, t) in jts if t >= self.npt]
        for mi, name in enumerate(("hg", "gla", "rw", "ret")):
            if name not in self.mixers:
                for j in range(len(tiles)):
                    k.TS(self.oT_t[:, 2 * mi:2 * mi + 2, j * 128:(j + 1) * 128], self.hT_t[:, 0:2, j * 128:(j + 1) * 128],
                         0.0, ALU.mult, r=[self.hT_b[j]], w=[self.oT_b[j * 4 + mi]])
        two_stream = all(n in self.mixers for n in ("hg", "gla", "rw", "ret")) and len(pj) > 0 \
            and os.environ.get("NO_TWO_STREAM") is None
        if two_stream:
            m0 = self.sc.mark()
            P = self.k.P
            P.cur = "A"
            for _ in self.mixer_rw(l, pj, 2, wslot=1):
                pass
            P.cur = "B"
            self.sc.top = self.rw_top
            for mname, mi_ in (("hg", 0), ("gla", 1), ("ret", 3)):
                for _ in getattr(self, "mixer_" + mname)(l, pj, mi_, nsets=1, wslot=0):
                    pass
            P.cur = None
            P.merge_streams()
            self.sc.reset(m0)
            rest = sj
        else:
            rest = jts
        if rest:
            for mi, name in enumerate(("hg", "gla", "rw", "ret")):
                if name in self.mixers:
                    for _ in getattr(self, "mixer_" + name)(l, rest, mi, nsets=(1 if two_stream else 2)):
                        pass
        self.merge_phase(l, tiles)

        def after():
            last = (l == self.depth - 1)
            wt = self.norm_prep(self.W["final_norm_w"] if last else self.W["attn_norm_w"][l + 1])
            sets = self.norm_alloc()
            if last:
                self.final_done = True
                return lambda j: self.final_tile(j, tiles[j], wt, sets[j % 2])
            self.attn_norm_done = True
            step, flush = self.lagged_norm(wt, sets)
            self.after_flush = flush
            return step
        self.ffn_phase(l, tiles, after=after)


    def load_layer_params(self, l):
        k = self.k
        W = self.W
        pb = {}
        pb["mu"] = self.load_pb(PB_MU, W["rw_mu"][l])
        k.TS(pb["mu"].ap, pb["mu"].ap, -1.0, ALU.mult, 1.0, ALU.add, r=[pb["mu"]], w=[pb["mu"]])
        pb["hgn"] = self.load_pb(PB_HGN, W["hg_norm_w"][l])
        pb["gba"] = self.load_pb(PB_GBA, W["gla_ba"][l])
        pb["gln"] = self.load_pb(PB_GLN, W["gla_norm_w"][l])
        for nm, off in (("rw_w0", PB_W0), ("rw_a0", PB_A0), ("rw_kk", PB_KK), ("rw_ka", PB_KA), ("rw_rk", PB_RK),
                        ("rw_ln_w", PB_LNW), ("rw_ln_b", PB_LNB)):
            pb[nm] = self.load_pb(off, W[nm][l])
        self.pb = pb
        lbc = T(self.lbc_t[:], self.lbc_b.all)
        if l == 0:
            k.MEMSET(self.lbc_t[:].rearrange("p a b -> p (a b)"), 0.5, w=[lbc])
        else:
            assert DEPTH == 2
            l0 = self.load_pb(PB_LB, W["hg_lb_logits"][0])
            l1 = self.load_pb(PB_LB + 256, W["hg_lb_logits"][1])
            k.TT(l0.ap, l1.ap, l0.ap, ALU.subtract, r=[l0, l1], w=[l0])
            k.ACT(l0.ap, l0.ap, AF.Tanh, r=[l0], w=[l0], scale=0.5)
            k.TS(self.lbc_t[:, 0, :], l0.ap, 0.25, ALU.mult, 0.75, ALU.add, r=[l0], w=[lbc])
            k.TS(self.lbc_t[:, 1, :], l0.ap, -0.25, ALU.mult, 0.25, ALU.add, r=[l0], w=[lbc])
        rwl = T(self.rwl_t[:], self.rwl_b.all)
        k.DMA(self.rwl_t[0:64, 0, :], W["rw_w2"][l], w=[rwl], iss="pool")
        k.DMA(self.rwl_t[64:128, 1, :], W["rw_a2"][l], w=[rwl], iss="pool")
        k.DMA(self.rwl_t[:, 2, :], W["rw_g2"][l], w=[rwl], iss="pool")
        k.TS(self.gwa_t[:], self.ident.ap[0:32, :], 0.0, ALU.mult, r=[self.ident], w=[self.gwa_b])
        k.DMA(self.gwa_t[0:16, :], W["gla_wa2"][l], w=[self.gwa_b], iss="pool")

    def load_mixer_w(self, l, mi, c0, ncols):
        wv = self.wa.view((mi % 2) * 16384, [8, 1024], BF16)
        src = self.W["w_in"][l][:, c0:c0 + ncols].rearrange("(kt p) n -> p kt n", p=128)
        self.k.DMA(wv.ap[:, :, 0:ncols], src, w=[wv], iss="pool")
        return wv

    def project(self, wv, j, ncols):
        k = self.k
        pp = self.ps_big()
        c = 0
        while c < ncols:
            n = min(512, ncols - c)
            for kt in range(8):
                k.MM(pp.ap[:, c:c + n], self.hT_t[:, kt, j * 128:(j + 1) * 128], wv.ap[:, kt, c:c + n],
                     start=(kt == 0), stop=(kt == 7), r=[self.hT_b[j], wv], w=[pp], inc=(kt == 7))
            c += n
        return pp

    def gla_sets(self, n):
        sets = []
        for _ in range(n):
            d = {}
            for nm in ("a0", "a1", "a2", "a3", "a4", "a5", "g", "Ep", "Em", "gsb", "tg"):
                d[nm] = self.sc.alloc([256], F32)
            d["dd"] = self.sc.alloc([2, 32], F32)
            d["KVd"] = self.sc.alloc([2, 2, 64], F32)
            d["Smid"] = self.sc.alloc([2, 64], F32)
            d["tmp"] = self.sc.alloc([2, 64], F32)
            d["st4"] = self.sc.alloc([8], F32)
            for nm in ("qt", "kt", "vbf", "ob"):
                d[nm] = self.sc.alloc([256], BF16)
            d["qkT"] = self.sc.alloc([4, 128], BF16)
            d["AT"] = self.sc.alloc([4, 128], BF16)
            sets.append(d)
        return sets

    def S_view(self, l, mi):
        return T(self.S_t[:, l * 4 + mi, :, :], self.S_b[l * 4 + mi])

    def gla_A(self, s, kind, q, kk, g, S, smidB, cidx, samp=None):
        k = self.k
        nind = 4 if kind == "p" else 32
        cst = [T(self.c128_t[:], self.c128_b.all)]
        bp = self.ps_small()
        k.MM(bp.ap[:, 0:256], self.cm(kind, "mcum"), g.ap, r=cst + [g], w=[bp], inc=False)
        for jj in range(2):
            k.MM(bp.ap[:, 256 + jj * nind:256 + (jj + 1) * nind], g.ap[:, jj * 128:(jj + 1) * 128],
                 self.cm(kind, "ind"), r=cst + [g], w=[bp], inc=(jj == 1))
        yield
        k.ACT(s["Ep"].ap, bp.ap[:, 0:256], AF.Exp, r=[bp], w=[s["Ep"]])
        k.ACT(s["Em"].ap, bp.ap[:, 0:256], AF.Exp, r=[bp], w=[s["Em"]], scale=-1.0)
        dd = s["dd"]
        k.ACT(dd.ap[:, :, 0:nind], bp.ap[:, 256:256 + 2 * nind].rearrange("p (a b) -> p a b", a=2), AF.Exp,
              r=[bp], w=[dd])
        k.TT(s["qt"].ap, q.ap, s["Ep"].ap, ALU.mult, r=[q, s["Ep"]], w=[s["qt"]])
        k.TT(s["kt"].ap, kk.ap, s["Em"].ap, ALU.mult, r=[kk, s["Em"]], w=[s["kt"]])
        yield
        tb = self.ps_tb()
        for i, src in enumerate((s["qt"], s["qt"], s["kt"], s["kt"])):
            k.TR(tb.ap[:, i * 128:(i + 1) * 128], src.ap[:, (i % 2) * 128:(i % 2 + 1) * 128], self.ident.ap,
                 r=[src, self.ident], w=[tb], inc=(i == 3))
        qkT = s["qkT"]
        k.CP(qkT.ap, tb.ap[:, 0:512].rearrange("p (a b) -> p a b", a=4), r=[tb], w=[qkT], eng="act")
        yield
        scps = [self.ps_small(), self.ps_small()]
        for hl in range(2):
            for jj in range(2):
                k.MM(scps[hl].ap[:, jj * 128:(jj + 1) * 128], qkT.ap[64 * hl:64 * hl + 64, 2 + jj, :],
                     qkT.ap[64 * hl:64 * hl + 64, jj, :], r=[qkT], w=[scps[hl]], inc=(jj == 1))
        mask = self.cm(kind, "ti")
        AT4 = s["AT"].ap.rearrange("p (j h) t -> p j h t", j=2)
        for hl in range(2):
            k.TT(AT4[:, :, hl, :], scps[hl].ap[:, 0:256].rearrange("p (a b) -> p a b", a=2),
                 mask.unsqueeze(1).to_broadcast([128, 2, 128]), ALU.mult, r=[scps[hl]] + cst, w=[s["AT"]])
        kt, vbf = s["kt"], s["vbf"]
        yield
        if kind == "p":
            KVd = s["KVd"]
            kvps = [self.ps_small(), self.ps_small()]
            for c in range(2):
                for jj in range(2):
                    k.MM(kvps[c].ap[:, jj * 128:(jj + 1) * 128],
                         kt.ap[64 * c:64 * c + 64, jj * 128:(jj + 1) * 128],
                         vbf.ap[64 * c:64 * c + 64, jj * 128:(jj + 1) * 128], r=[kt, vbf], w=[kvps[c]],
                         inc=(jj == 1))
            for c in range(2):
                kv2 = kvps[c].ap[:, 0:256].rearrange("p (a b) -> p a b", a=2)
                k.CP(KVd.ap[0:64, c], kv2[0:64, :, 0:64], r=[kvps[c]], w=[KVd])
                k.CP(KVd.ap[64:128, c], kv2[64:128, :, 64:128], r=[kvps[c]], w=[KVd], eng="act")
            yield
            for c in range(2):
                d1 = dd.ap[:, :, 2 * c:2 * c + 1].to_broadcast([128, 2, 64])
                d2 = dd.ap[:, :, 2 * c + 1:2 * c + 2].to_broadcast([128, 2, 64])
                k.TT(s["Smid"].ap, S.ap, d1, ALU.mult, r=[S, dd], w=[s["Smid"]])
                k.CP(smidB.ap[:, cidx + c], s["Smid"].ap, r=[s["Smid"]], w=[smidB], eng="act")
                k.TT(s["tmp"].ap, s["Smid"].ap, KVd.ap[:, c], ALU.add, r=[s["Smid"], KVd], w=[s["tmp"]])
                k.TT(S.ap, s["tmp"].ap, d2, ALU.mult, r=[s["tmp"], dd], w=[S])
        else:
            S0, KVd, SB = samp["S0"], samp["KVd"], samp["SmidB"]
            ddv = dd.ap.rearrange("p j (q t) -> p q j t", t=2)
            ktm = samp["ktm"]
            for hf in range(2):
                q0 = 8 * hf
                d1 = ddv[:, q0:q0 + 8, :, 0:1].to_broadcast([128, 8, 2, 64])
                d2 = ddv[:, q0:q0 + 8, :, 1:2].to_broadcast([128, 8, 2, 64])
                samp["load"](hf)
                k.TT(S0.ap, S0.ap, d1, ALU.mult, r=[S0, dd], w=[S0])
                k.CP(SB.ap[:, q0:q0 + 8], S0.ap, r=[S0], w=[SB], eng="act")
                for q2 in range(4):
                    kvp = self.ps_small()
                    for u in range(2):
                        sq = q0 + 2 * q2 + u
                        km = ktm[sq % len(ktm)]
                        k.TS(km.ap, kt.ap, self.c128_t[:, C_ROWS + sq:C_ROWS + sq + 1], ALU.mult, r=[kt] + cst,
                             w=[km])
                        for jj in range(2):
                            k.MM(kvp.ap[:, (u * 2 + jj) * 128:(u * 2 + jj + 1) * 128],
                                 km.ap[:, jj * 128:(jj + 1) * 128], vbf.ap[:, jj * 128:(jj + 1) * 128],
                                 r=[km, vbf], w=[kvp], inc=(u == 1 and jj == 1))
                    kv4 = kvp.ap.rearrange("p (a b) -> p a b", a=4)
                    kd4 = KVd.ap[:, 2 * q2:2 * q2 + 2].rearrange("p a b c -> p (a b) c")
                    k.CP(kd4[0:64], kv4[0:64, :, 0:64], r=[kvp], w=[KVd])
                    k.CP(kd4[64:128], kv4[64:128, :, 64:128], r=[kvp], w=[KVd], eng="act")
                k.TT(KVd.ap, KVd.ap, S0.ap, ALU.add, r=[KVd, S0], w=[KVd])
                k.TT(KVd.ap, KVd.ap, d2, ALU.mult, r=[KVd, dd], w=[KVd])
                samp["store"](hf)

    def gla_B(self, s, kind, smidB, cidx, samp=None):
        k = self.k
        op_ = self.ps_small()
        AT, vbf, qkT = s["AT"], s["vbf"], s["qkT"]
        if kind == "p":
            for c in range(2):
                for h in range(4):
                    jj, hl = h // 2, h % 2
                    out = op_.ap[64 * c:64 * c + 64, h * 64:(h + 1) * 64]
                    k.MM(out, AT.ap[:, h, 64 * c:64 * c + 64], vbf.ap[:, h * 64:(h + 1) * 64], start=True, stop=False,
                         r=[AT, vbf], w=[op_], inc=False)
                    k.MM(out, qkT.ap[64 * hl:64 * hl + 64, jj, 64 * c:64 * c + 64],
                         smidB.ap[64 * hl:64 * hl + 64, cidx + c, jj, :], start=False, stop=True,
                         r=[qkT, smidB], w=[op_], inc=(c == 1 and h == 3))
        else:
            SB, qtm = samp["SmidB"], samp["qtm"]
            cmd = [T(self.cm_t[:], self.cm_b.all)]
            n = 0
            for h in range(4):
                jj, hl = h // 2, h % 2
                out = op_.ap[:, h * 64:(h + 1) * 64]
                k.MM(out, AT.ap[:, h, :], vbf.ap[:, h * 64:(h + 1) * 64], start=True, stop=False,
                     r=[AT, vbf], w=[op_], inc=False)
                for sq in range(16):
                    qm = qtm[n % len(qtm)]
                    n += 1
                    k.TT(qm.ap[64 * hl:64 * hl + 64, :], qkT.ap[64 * hl:64 * hl + 64, jj, :],
                         self.cm_t[64 * hl:64 * hl + 64, sq, :], ALU.mult, r=[qkT] + cmd, w=[qm])
                    k.MM(out, qm.ap[64 * hl:64 * hl + 64, :], SB.ap[64 * hl:64 * hl + 64, sq, jj, :], start=False,
                         stop=(sq == 15), r=[qm, SB], w=[op_], inc=True)
        return op_

    def samp_alloc(self, name, l, dk):
        k = self.k
        d = dict(S0=self.sc.alloc([8, 2, 64], F32), KVd=self.sc.alloc([8, 2, 64], F32),
                 SmidB=self.sc.alloc([16, 2, 64], BF16), ktm=[self.sc.alloc([256], BF16) for _ in range(3)],
                 qtm=[self.sc.alloc([128], BF16) for _ in range(4)])
        S0, Sn = d["S0"], d["KVd"]
        src = self.st_in[name][l]
        dst = self.s_out[name][l]

        def load(hf):
            if dk < 64:
                k.MEMSET(S0.ap.rearrange("p a b c -> p (a b c)"), 0.0, w=[S0])
            for hl in range(2):
                for jj in range(2):
                    k.DMA(S0.ap[64 * hl:64 * hl + dk, :, jj, :],
                          src[8 * hf:8 * hf + 8, 2 * jj + hl].rearrange("s k v -> k s v"), w=[S0])

        def store(hf):
            for hl in range(2):
                for jj in range(2):
                    k.DMA(dst[8 * hf:8 * hf + 8, 2 * jj + hl].rearrange("s k v -> k s v"),
                          Sn.ap[64 * hl:64 * hl + dk, :, jj, :], r=[Sn], is_output=True)
        d["load"], d["store"] = load, store
        return d

    def store_state_p(self, name, l, S, dk):
        k = self.k
        dst = self.p_out[name][l]
        for hl in range(2):
            for jj in range(2):
                k.DMA(dst[2 * jj + hl], S.ap[64 * hl:64 * hl + dk, jj, :], r=[S], is_output=True)

    def post_rms(self, s, o_ps, normw, j, mi):
        k = self.k
        osb, sq, on, tg = s["a0"], s["a1"], s["a2"], s["tg"]
        st = s["st4"]
        gate_ap, gate_dep = s["gsb"].ap, s["gsb"]
        k.CP(osb.ap, o_ps.ap[:, 0:256], r=[o_ps], w=[osb], eng="act")
        k.TT(sq.ap, osb.ap, osb.ap, ALU.mult, r=[osb], w=[sq])
        k.RSUM(st.ap[:, 0:4], sq.ap.rearrange("p (a b) -> p a b", a=4), r=[sq], w=[st])
        k.ACT(st.ap[:, 4:8], st.ap[:, 0:4], AF.Ln, r=[st, self.cst_b], w=[st], bias=self.cst(self.CST["EPS64"]),
              scale=1.0 / 64)
        k.ACT(st.ap[:, 0:4], st.ap[:, 4:8], AF.Exp, r=[st, self.cst_b], w=[st], bias=self.cst(self.CST["LNH"]),
              scale=-0.5)
        yield
        k.TT(on.ap.rearrange("p (a b) -> p a b", a=4), osb.ap.rearrange("p (a b) -> p a b", a=4),
             st.ap[:, 0:4].unsqueeze(2).to_broadcast([128, 4, 64]), ALU.mult, r=[osb, st], w=[on])
        if normw is not None:
            k.TT(on.ap.rearrange("p (a b) -> p a b", a=4), on.ap.rearrange("p (a b) -> p a b", a=4),
                 normw.ap.unsqueeze(1).to_broadcast([128, 4, 64]), ALU.mult, r=[on, normw], w=[on])
        k.STT(tg.ap, tg.ap, 1.0, gate_ap, ALU.add, ALU.mult, r=[tg, gate_dep], w=[tg])
        k.TT(s["ob"].ap, on.ap, tg.ap, ALU.mult, r=[on, tg], w=[s["ob"]])
        yield
        self.to_oT(s["ob"], j, mi)

    def to_oT(self, ob, j, mi):
        k = self.k
        tb = self.ps_tb()
        for i in range(2):
            k.TR(tb.ap[:, i * 128:(i + 1) * 128], ob.ap[:, i * 128:(i + 1) * 128], self.ident.ap,
                 r=[ob, self.ident], w=[tb], inc=(i == 1))
        k.CP(self.oT_t[:, 2 * mi:2 * mi + 2, j * 128:(j + 1) * 128], tb.ap[:, 0:256].rearrange("p (a b) -> p a b", a=2),
             r=[tb], w=[self.oT_b[j * 4 + mi]])

    def gla_stream(self, name, l, jts, mi, c0, ncols, dk, pre, post, init=None, nsets=2, wslot=None):
        k = self.k
        m = self.sc.mark()
        wv = self.load_mixer_w(l, mi if wslot is None else wslot, c0, ncols)
        sets = self.gla_sets(nsets)
        if init is not None:
            init(sets)
        tiles = [t for (_, t) in jts]
        smidB = self.sc.alloc([2 * nsets, 2, 64], BF16)
        has_s = any(t >= self.npt for t in tiles)
        samp = None
        if has_s:
            samp = self.samp_alloc(name, l, dk)
        S = self.S_view(l, mi)

        def tile_gen(j, t, s, si):
            kind = self.tile_kind(t)
            pp = self.project(wv, j, ncols)
            yield
            q, kk, g = pre(s, pp, j, t, l, wv)
            yield
            yield from self.gla_A(s, kind, q, kk, g, S, smidB, 2 * si, samp if kind == "s" else None)
            if kind == "p" and t == self.npt - 1:
                self.store_state_p(name, l, S, dk)
            yield
            o_ps = self.gla_B(s, kind, smidB, 2 * si, samp if kind == "s" else None)
            yield
            yield from post(s, o_ps, pp, j, mi, l)

        active = []
        nxt = 0
        free = list(range(nsets))
        while nxt < len(jts) or active:
            if nxt < len(jts) and free:
                si = free.pop(0)
                active.append((tile_gen(jts[nxt][0], jts[nxt][1], sets[si], si), si))
                nxt += 1
            still = []
            for gen, si in active:
                try:
                    next(gen)
                    still.append((gen, si))
                except StopIteration:
                    free.append(si)
            active = still
            yield
        self.sc.reset(m)

    def mixer_hg(self, l, jts, mi, nsets=2, wslot=None):
        k = self.k
        lbc = T(self.lbc_t[:], self.lbc_b.all)

        def pre(s, pp, j, t, l, wv):
            th, fg, kf, qf, tq = s["a0"], s["a1"], s["a2"], s["a3"], s["a4"]
            P_ = pp.ap
            k.ACT(th.ap, P_[:, 256:512], AF.Tanh, r=[pp], w=[th], scale=0.5)
            k.ACT(tq.ap, P_[:, 0:256], AF.Tanh, r=[pp], w=[tq], scale=0.5)
            k.ACT(s["tg"].ap, P_[:, 768:1024], AF.Tanh, r=[pp], w=[s["tg"]], scale=0.5)
            k.TT(fg.ap, th.ap, self.lbc_t[:, 1, :], ALU.mult, r=[th, lbc], w=[fg])
            k.TT(fg.ap, fg.ap, self.lbc_t[:, 0, :], ALU.add, r=[fg, lbc], w=[fg])
            k.TS(kf.ap, fg.ap, -1.0, ALU.mult, 1.0, ALU.add, r=[fg], w=[kf])
            k.TS(fg.ap, fg.ap, 1e-30, ALU.max, r=[fg], w=[fg])
            k.ACT(s["g"].ap, fg.ap, AF.Ln, r=[fg], w=[s["g"]])
            k.STT(qf.ap, tq.ap, 1.0, P_[:, 0:256], ALU.add, ALU.mult, r=[tq, pp], w=[qf])
            k.TS(qf.ap, qf.ap, 0.5, ALU.mult, r=[qf], w=[qf])
            k.CP(s["vbf"].ap, P_[:, 512:768], r=[pp], w=[s["vbf"]], eng="act")
            k.CP(s["gsb"].ap, P_[:, 768:1024], r=[pp], w=[s["gsb"]], eng="act")
            return qf, kf, s["g"]

        def post(s, o_ps, pp, j, mi, l):
            yield from self.post_rms(s, o_ps, self.pb["hgn"], j, mi)

        return self.gla_stream("hg", l, jts, mi, C_HG, 1024, 64, pre, post, nsets=nsets, wslot=wslot)


    def mixer_ret(self, l, jts, mi, nsets=2, wslot=None):
        k = self.k
        gc = T(self.gconst_t[:], self.gconst_b.all)

        def pre(s, pp, j, t, l, wv):
            par = j % 2
            rt = T(self.rot_t[:, par, :], self.rot_b[par])
            k.DMA(rt.ap, self.rot_d[t], w=[rt])
            cosb = self.rot_t[:, par, 0:32].unsqueeze(1).to_broadcast([128, 8, 32])
            sinb = self.rot_t[:, par, 32:64].unsqueeze(1).to_broadcast([128, 8, 32])
            v8 = lambda ap: ap.rearrange("p (h i) -> p h i", h=8)
            v4 = lambda ap: ap.rearrange("p (h a i) -> p h a i", h=4, a=2)
            outs = []
            for (c0, t1, t2, o, scale) in ((0, s["a0"], s["a1"], s["a2"], 1.0), (256, s["a3"], s["a4"], s["a5"], 0.125)):
                src = v8(pp.ap[:, c0:c0 + 256])
                k.STT(v8(t1.ap), src, scale, cosb, ALU.mult, ALU.mult, r=[pp, rt], w=[t1])
                k.STT(v8(t2.ap), src, scale, sinb, ALU.mult, ALU.mult, r=[pp, rt], w=[t2])
                k.TT(v4(o.ap)[:, :, 0, :], v4(t1.ap)[:, :, 0, :], v4(t2.ap)[:, :, 1, :], ALU.subtract,
                     r=[t1, t2], w=[o])
                k.TT(v4(o.ap)[:, :, 1, :], v4(t2.ap)[:, :, 0, :], v4(t1.ap)[:, :, 1, :], ALU.add,
                     r=[t1, t2], w=[o])
                outs.append(o)
            k.CP(s["vbf"].ap, pp.ap[:, 512:768], r=[pp], w=[s["vbf"]], eng="act")
            k.CP(s["gsb"].ap, pp.ap[:, 768:1024], r=[pp], w=[s["gsb"]], eng="act")
            k.ACT(s["tg"].ap, pp.ap[:, 768:1024], AF.Tanh, r=[pp], w=[s["tg"]], scale=0.5)
            return outs[0], outs[1], gc

        def post(s, o_ps, pp, j, mi, l):
            yield from self.post_rms(s, o_ps, None, j, mi)

        return self.gla_stream("ret", l, jts, mi, C_RET, 1024, 64, pre, post, nsets=nsets, wslot=wslot)

    def mixer_gla(self, l, jts, mi, nsets=2, wslot=None):
        k = self.k

        def init(sets):
            for s in sets:
                for nm in ("a4", "a5", "g"):
                    k.MEMSET(s[nm].ap, 0.0, w=[s[nm]])

        def pre(s, pp, j, t, l, wv):
            v3 = lambda ap: ap.rearrange("p (h i) -> p h i", h=4)
            k.ACT(s["tg"].ap, pp.ap[:, 528:784], AF.Tanh, r=[pp], w=[s["tg"]], scale=0.5)
            ap_ = self.ps_small()
            for kt in range(8):
                k.MM(ap_.ap[0:32, 0:128], wv.ap[:, kt, 512:544], self.hT_t[:, kt, j * 128:(j + 1) * 128],
                     start=(kt == 0), stop=(kt == 7), r=[wv, self.hT_b[j]], w=[ap_], inc=(kt == 7))
            adT = s["ob"]
            k.CP(adT.ap[0:32, 0:128], ap_.ap[0:32, 0:128], r=[ap_], w=[adT])
            zp = self.ps_small()
            k.MM(zp.ap[:, 0:128], adT.ap[0:32, 0:128], self.gwa_t[:], r=[adT, self.gwa_b], w=[zp])
            z = s["a0"]
            k.TT(z.ap[:, 0:128], zp.ap[:, 0:128], self.pb["gba"].ap, ALU.add, r=[zp, self.pb["gba"]], w=[z])
            k.ACT(z.ap[:, 0:128], z.ap[:, 0:128], AF.Exp, r=[z], w=[z], scale=-1.0)
            k.ACT(z.ap[:, 0:128], z.ap[:, 0:128], AF.Ln, r=[z, self.cst_b], w=[z], bias=self.cst(self.CST["ONE"]))
            k.TS(v3(s["g"].ap)[:, :, 0:32], v3(z.ap[:, 0:128]), -1.0 / 16.0, ALU.mult, r=[z], w=[s["g"]])
            k.TS(v3(s["a4"].ap)[:, :, 0:32], v3(pp.ap[:, 0:128]), 32.0 ** -0.5, ALU.mult, r=[pp], w=[s["a4"]])
            k.CP(v3(s["a5"].ap)[:, :, 0:32], v3(pp.ap[:, 128:256]), r=[pp], w=[s["a5"]])
            k.CP(s["vbf"].ap, pp.ap[:, 256:512], r=[pp], w=[s["vbf"]], eng="act")
            k.CP(s["gsb"].ap, pp.ap[:, 528:784], r=[pp], w=[s["gsb"]], eng="act")
            return s["a4"], s["a5"], s["g"]

        def post(s, o_ps, pp, j, mi, l):
            yield from self.post_rms(s, o_ps, self.pb["gln"], j, mi)

        return self.gla_stream("gla", l, jts, mi, C_GLA, 784, 32, pre, post, init=init, nsets=nsets, wslot=wslot)


    def mixer_rw(self, l, jts, mi, nsets=1, wslot=None):
        k = self.k
        m = self.sc.mark()
        wv = self.load_mixer_w(l, mi if wslot is None else wslot, C_RW, 1024)
        tiles = [t for (_, t) in jts]
        A = self.sc.alloc
        c128 = self.c128_t
        cst = [T(self.c128_t[:], self.c128_b.all)]
        pb = self.pb
        rw_sb = A([1024], F32)
        f = {nm: A([256], F32) for nm in ("a", "lw", "gsb", "kk", "kp", "bv", "bon", "t0", "t1", "NTAVf")}
        f["t2"] = f["NTAVf"]
        st = A([16], F32)
        X = A([256], BF16)
        XT = A([2, 128], BF16)
        tb16 = {nm: A([256], BF16) for nm in ("rt", "kkt", "vbf", "AV", "NTAVb", "NU", "ob")}
        KB3 = A([3, 256], BF16)
        T8 = A([8, 128], BF16)
        RT2 = A([2, 2, 128], BF16)
        SM = A([4, 4, 128], BF16)
        Pa, Pb, Qa, Qb, Xc = [A([4, 128], BF16) for _ in range(5)]
        mask4 = A([4, 128], BF16)
        NG = A([2, 2, 128], BF16)
        Hd = A([2, 2, 64], F32)
        Smid = A([2, 64], F32)
        dd_buf = A([2, 32], F32)
        tmp = A([2, 64], F32)
        SmidB = A([2, 2, 64], BF16)
        SmidX = A([2, 2, 128], BF16)
        has_s = any(t >= self.npt for t in tiles)
        if has_s:
            sp = dict(S0=A([8, 2, 64], F32), Sn=A([8, 2, 64], F32), KBm=[A([3, 256], BF16) for _ in range(2)],
                      NGs=[A([2, 128], BF16) for _ in range(2)], SmB=[A([2, 64], BF16) for _ in range(2)],
                      SmX=[A([2, 128], BF16) for _ in range(2)], RTm=[A([2, 2, 128], BF16) for _ in range(2)])
        zsrc = self.cm_t[:, 0:4, :].rearrange("p a b -> p (a b)")
        zr = [T(self.cm_t[:], self.cm_b.all)]
        k.TS(SmidX.ap.rearrange("p a b c -> p (a b c)"), zsrc, 0.0, ALU.mult, r=zr, w=[SmidX])
        if has_s:
            for b_ in sp["SmX"]:
                k.TS(b_.ap.rearrange("p a b -> p (a b)"), zsrc[:, 0:256], 0.0, ALU.mult, r=zr, w=[b_])
        mu = pb["mu"]
        S = self.S_view(l, mi)
        rwp = T(self.rwp_t[:, l, :], self.rwp_b[l])
        c1 = -0.5 * math.exp(-0.5)
        v3 = lambda ap: ap.rearrange("p (a b) -> p a b", a=4)
        cur_kind = [None]

        def set_masks(kind):
            if cur_kind[0] == kind:
                return
            cur_kind[0] = kind
            ts_, ti_ = self.cm(kind, "ts"), self.cm(kind, "ti")
            k.CP(mask4.ap[:, 0, :], ts_, r=cst, w=[mask4])
            k.CP(mask4.ap[:, 1, :], ti_, r=cst, w=[mask4])
            k.TS(mask4.ap[:, 2, :], ts_, -1.0, ALU.mult, r=cst, w=[mask4])
            k.CP(mask4.ap[:, 3, :], ti_, r=cst, w=[mask4])

        for j, t in jts:
            kind = self.tile_kind(t)
            nind = 4 if kind == "p" else 32
            set_masks(kind)
            pp = self.project(wv, j, 1024)
            k.CP(rw_sb.ap, pp.ap, r=[pp], w=[rw_sb], eng="act")
            if kind == "p" and t == self.npt - 1:
                k.DMA(self.p_shift[l:l + 1, :], rw_sb.ap[127:128, :], r=[rw_sb], is_output=True)
            if kind == "s":
                for q in range(16):
                    k.DMA(self.s_shift[l, q:q + 1, :], rw_sb.ap[8 * q + 7:8 * q + 8, :], r=[rw_sb], is_output=True)
                k.DMA(self.rwp_t[0:16, l, :], self.st_shift[l], w=[rwp])
            yield
            pv = self.ps_big()
            shm = C_SHP if kind == "p" else C_SHS
            carry = (kind == "s") or (t > 0)
            for half in range(2):
                hs = slice(half * 512, (half + 1) * 512)
                k.MM(pv.ap[:, hs], c128[:, shm:shm + 128], rw_sb.ap[:, hs], start=True, stop=not carry,
                     r=cst + [rw_sb], w=[pv], inc=(not carry and half == 1))
                if carry and kind == "p":
                    k.MM(pv.ap[:, hs], c128[:, C_CAR:C_CAR + 128], self.rwp_t[:, l, hs], start=False, stop=True,
                         r=cst + [rwp], w=[pv], inc=(half == 1))
                elif carry:
                    k.MM(pv.ap[:, hs], c128[0:32, C_SEL:C_SEL + 128], self.rwp_t[0:32, l, hs], start=False,
                         stop=True, r=cst + [rwp], w=[pv], inc=(half == 1))
            if kind == "p":
                k.CP(self.rwp_t[64:128, l, :], rw_sb.ap[64:128, :], r=[rw_sb], w=[rwp], eng="act")
            k.TT(rw_sb.ap, rw_sb.ap, pv.ap, ALU.subtract, r=[rw_sb, pv], w=[rw_sb])
            k.TT(rw_sb.ap, rw_sb.ap, mu.ap, ALU.mult, r=[rw_sb, mu], w=[rw_sb])
            k.TT(rw_sb.ap, rw_sb.ap, pv.ap, ALU.add, r=[rw_sb, pv], w=[rw_sb])
            mx = rw_sb.ap
            r_, wd_, kx_, v_, ad_, gd_ = mx[:, 0:256], mx[:, 256:320], mx[:, 320:576], mx[:, 576:832], mx[:, 832:896], mx[:, 896:1024]
            yield
            k.ACT(X.ap[:, 0:64], wd_, AF.Tanh, r=[rw_sb], w=[X])
            k.CP(X.ap[:, 64:128], ad_, r=[rw_sb], w=[X])
            k.ACT(f["t0"].ap[:, 0:128], gd_, AF.Tanh, r=[rw_sb], w=[f["t0"]], scale=0.5)
            k.TS(X.ap[:, 128:256], f["t0"].ap[:, 0:128], 0.5, ALU.mult, 0.5, ALU.add, r=[f["t0"]], w=[X])
            tb = self.ps_tb()
            for i in range(2):
                k.TR(tb.ap[:, i * 128:(i + 1) * 128], X.ap[:, i * 128:(i + 1) * 128], self.ident.ap,
                     r=[X, self.ident], w=[tb], inc=(i == 1))
            k.CP(XT.ap, tb.ap[:, 0:256].rearrange("p (a b) -> p a b", a=2), r=[tb], w=[XT], eng="act")
            rwl = T(self.rwl_t[:], self.rwl_b.all)
            lwb = self.ps_small()
            lab = self.ps_small()
            k.MM(lwb.ap[:, 0:256], XT.ap[0:64, 0, :], self.rwl_t[0:64, 0, :], r=[XT, rwl], w=[lwb], inc=False)
            k.MM(lab.ap[:, 0:256], XT.ap[64:128, 0, :], self.rwl_t[64:128, 1, :], r=[XT, rwl], w=[lab])
            k.MM(lwb.ap[:, 256:512], XT.ap[:, 1, :], self.rwl_t[:, 2, :], r=[XT, rwl], w=[lwb])
            k.TT(f["t0"].ap, lwb.ap[:, 0:256], pb["rw_w0"].ap, ALU.add, r=[lwb, pb["rw_w0"]], w=[f["t0"]])
            k.ACT(f["t0"].ap, f["t0"].ap, AF.Tanh, r=[f["t0"]], w=[f["t0"]], scale=0.5)
            k.TS(f["lw"].ap, f["t0"].ap, c1, ALU.mult, c1, ALU.add, r=[f["t0"]], w=[f["lw"]])
            k.TT(f["t1"].ap, lab.ap[:, 0:256], pb["rw_a0"].ap, ALU.add, r=[lab, pb["rw_a0"]], w=[f["t1"]])
            k.ACT(f["t1"].ap, f["t1"].ap, AF.Tanh, r=[f["t1"]], w=[f["t1"]], scale=0.5)
            k.TS(f["a"].ap, f["t1"].ap, 0.5, ALU.mult, 0.5, ALU.add, r=[f["t1"]], w=[f["a"]])
            k.CP(f["gsb"].ap, lwb.ap[:, 256:512], r=[lwb], w=[f["gsb"]], eng="act")
            yield
            k.TT(f["kk"].ap, kx_, pb["rw_kk"].ap, ALU.mult, r=[rw_sb, pb["rw_kk"]], w=[f["kk"]])
            k.TT(f["t0"].ap, f["kk"].ap, f["kk"].ap, ALU.mult, r=[f["kk"]], w=[f["t0"]])
            k.RSUM(st.ap[:, 0:4], v3(f["t0"].ap), r=[f["t0"]], w=[st])
            k.ACT(st.ap[:, 4:8], st.ap[:, 0:4], AF.Ln, r=[st, self.cst_b], w=[st], bias=self.cst(self.CST["TINY"]))
            k.ACT(st.ap[:, 0:4], st.ap[:, 4:8], AF.Exp, r=[st], w=[st], scale=-0.5)
            k.TT(v3(f["kk"].ap), v3(f["kk"].ap), st.ap[:, 0:4].unsqueeze(2).to_broadcast([128, 4, 64]), ALU.mult,
                 r=[f["kk"], st], w=[f["kk"]])
            k.STT(f["t0"].ap, f["a"].ap, -1.0, pb["rw_ka"].ap, ALU.add, ALU.mult, r=[f["a"], pb["rw_ka"]], w=[f["t0"]])
            k.STT(f["kp"].ap, f["t0"].ap, 1.0, kx_, ALU.add, ALU.mult, r=[f["t0"], rw_sb], w=[f["kp"]])
            k.TT(f["bv"].ap, f["a"].ap, f["kk"].ap, ALU.mult, r=[f["a"], f["kk"]], w=[f["bv"]])
            k.TT(f["t0"].ap, r_, f["kp"].ap, ALU.mult, r=[rw_sb, f["kp"]], w=[f["t0"]])
            k.TT(f["t0"].ap, f["t0"].ap, pb["rw_rk"].ap, ALU.mult, r=[f["t0"], pb["rw_rk"]], w=[f["t0"]])
            k.RSUM(st.ap[:, 8:12], v3(f["t0"].ap), r=[f["t0"]], w=[st])
            k.TT(v3(f["bon"].ap), v3(v_), st.ap[:, 8:12].unsqueeze(2).to_broadcast([128, 4, 64]), ALU.mult,
                 r=[rw_sb, st], w=[f["bon"]])
            vbf = tb16["vbf"]
            k.CP(vbf.ap, v_, r=[rw_sb], w=[vbf], eng="act")
            yield
            bp = self.ps_small()
            k.MM(bp.ap[:, 0:256], self.cm(kind, "mcum"), f["lw"].ap, r=cst + [f["lw"]], w=[bp], inc=False)
            for jj in range(2):
                k.MM(bp.ap[:, 256 + jj * nind:256 + (jj + 1) * nind], f["lw"].ap[:, jj * 128:(jj + 1) * 128],
                     self.cm(kind, "ind"), r=cst + [f["lw"]], w=[bp], inc=(jj == 1))
            dd = dd_buf
            k.ACT(f["t0"].ap, bp.ap[:, 0:256], AF.Exp, r=[bp], w=[f["t0"]])
            k.ACT(f["t1"].ap, bp.ap[:, 0:256], AF.Exp, r=[bp], w=[f["t1"]], scale=-1.0)
            k.ACT(dd.ap[:, :, 0:nind], bp.ap[:, 256:256 + 2 * nind].rearrange("p (a b) -> p a b", a=2), AF.Exp,
                  r=[bp], w=[dd])
            k.TT(f["t2"].ap, bp.ap[:, 0:256], f["lw"].ap, ALU.subtract, r=[bp, f["lw"]], w=[f["t2"]])
            k.ACT(f["t2"].ap, f["t2"].ap, AF.Exp, r=[f["t2"]], w=[f["t2"]])
            rt, kkt = tb16["rt"], tb16["kkt"]
            k.TT(rt.ap, r_, f["t0"].ap, ALU.mult, r=[rw_sb, f["t0"]], w=[rt])
            k.TT(kkt.ap, f["kk"].ap, f["t2"].ap, ALU.mult, r=[f["kk"], f["t2"]], w=[kkt])
            k.TT(KB3.ap[:, 0, :], f["kp"].ap, f["t1"].ap, ALU.mult, r=[f["kp"], f["t1"]], w=[KB3])
            k.TT(KB3.ap[:, 1, :], f["bv"].ap, f["t1"].ap, ALU.mult, r=[f["bv"], f["t1"]], w=[KB3])
            yield
            srcs = [(kkt.ap, kkt, 0), (rt.ap, rt, 0), (kkt.ap, kkt, 1), (rt.ap, rt, 1),
                    (KB3.ap[:, 0, :], KB3, 0), (KB3.ap[:, 0, :], KB3, 1), (KB3.ap[:, 1, :], KB3, 0),
                    (KB3.ap[:, 1, :], KB3, 1)]
            tb = self.ps_tb()
            for i, (ap_, dep_, jj) in enumerate(srcs):
                k.TR(tb.ap[:, i * 128:(i + 1) * 128], ap_[:, jj * 128:(jj + 1) * 128], self.ident.ap,
                     r=[dep_, self.ident], w=[tb], inc=(i == 7))
            k.CP(T8.ap, tb.ap.rearrange("p (a b) -> p a b", a=8), r=[tb], w=[T8], eng="act")
            k.CP(RT2.ap[:, :, 1, :], tb.ap[:, 0:512].rearrange("p (a b c) -> p a b c", a=2, b=2)[:, :, 1, :],
                 r=[tb], w=[RT2])
            yield
            for h in range(4):
                jj, hl = h // 2, h % 2
                R = slice(64 * hl, 64 * hl + 64)
                sb_ = self.ps_small()
                rhs = T8.ap[R, 2 * jj:2 * jj + 2, :]
                k.MM(sb_.ap[:, 0:256].rearrange("p (a b) -> p a b", a=2), T8.ap[R, 4 + jj, :], rhs, r=[T8],
                     w=[sb_], inc=False)
                k.MM(sb_.ap[:, 256:512].rearrange("p (a b) -> p a b", a=2), T8.ap[R, 6 + jj, :], rhs, r=[T8],
                     w=[sb_])
                k.TT(SM.ap[:, h], sb_.ap.rearrange("p (a b) -> p a b", a=4), mask4.ap, ALU.mult,
                     r=[sb_, mask4], w=[SM])
            Lb = [self.ps_small(), self.ps_small()]
            for hl in range(2):
                R = slice(64 * hl, 64 * hl + 64)
                for jj in range(2):
                    k.MM(Lb[hl].ap[:, jj * 128:(jj + 1) * 128], T8.ap[R, 2 * jj, :], T8.ap[R, 6 + jj, :], r=[T8],
                         w=[Lb[hl]], inc=(jj == 1))
            low = self.cm(kind, "low")
            Pa4 = Pa.ap.rearrange("p (j h) t -> p j h t", j=2)
            for hl in range(2):
                k.STT(Pa4[:, :, hl, :], Lb[hl].ap[:, 0:256].rearrange("p (a b) -> p a b", a=2), -1.0,
                      low.unsqueeze(1).to_broadcast([128, 2, 128]), ALU.mult, ALU.mult, r=[Lb[hl]] + cst, w=[Pa])
            Q0 = T(SM.ap[:, :, 2, :], SM.d)
            k.TT(Xc.ap, Q0.ap, self.ident.ap.unsqueeze(1).to_broadcast([128, 4, 128]), ALU.add,
                 r=[SM, self.ident], w=[Xc])
            yield
            nsteps = 5 if kind == "p" else 2
            Pc, Qc = Pa, Q0
            for step in range(1, nsteps + 1):
                last = step == nsteps
                Pn = Pb if Pc is Pa else Pa
                Qn = Qb if (Qc is Qa or Qc is Q0) else Qa
                pb_ = self.ps_small()
                for h in range(4):
                    k.MM(pb_.ap[:, h * 128:(h + 1) * 128], Qc.ap[:, h], Pc.ap[:, h], r=[Qc, Pc], w=[pb_],
                         inc=(h == 3))
                if not last:
                    qb_ = self.ps_small()
                    for h in range(4):
                        k.MM(qb_.ap[:, h * 128:(h + 1) * 128], Pc.ap[:, h], Qc.ap[:, h], r=[Qc, Pc], w=[qb_],
                             inc=(h == 3))
                k.CP(Pn.ap, pb_.ap.rearrange("p (a b) -> p a b", a=4), r=[pb_], w=[Pn], eng="act")
                if not last:
                    k.CP(Qn.ap, qb_.ap.rearrange("p (a b) -> p a b", a=4), r=[qb_], w=[Qn], eng="act")
                xb_ = self.ps_small()
                for h in range(4):
                    k.MM(xb_.ap[:, h * 128:(h + 1) * 128], Pn.ap[:, h], Xc.ap[:, h], r=[Pn, Xc], w=[xb_],
                         inc=(h == 3))
                k.TT(Xc.ap, Xc.ap, xb_.ap.rearrange("p (a b) -> p a b", a=4), ALU.add, r=[Xc, xb_], w=[Xc])
                Pc, Qc = Pn, Qn
                yield
            yield
            tk = self.ps_small()
            for h in range(4):
                jj, hl = h // 2, h % 2
                k.MM(tk.ap[64 * hl:64 * hl + 64, jj * 128:(jj + 1) * 128], kkt.ap[:, h * 64:(h + 1) * 64], Xc.ap[:, h],
                     r=[kkt, Xc], w=[tk], inc=False)
            for h in range(4):
                k.MM(tk.ap[:, 256 + h * 64:256 + (h + 1) * 64], Xc.ap[:, h], kkt.ap[:, h * 64:(h + 1) * 64],
                     r=[kkt, Xc], w=[tk], inc=(h == 3))
            k.CP(RT2.ap[:, :, 0, :], tk.ap[:, 0:256].rearrange("p (a b) -> p a b", a=2), r=[tk], w=[RT2], eng="act")
            k.CP(KB3.ap[:, 2, :], tk.ap[:, 256:512], r=[tk], w=[KB3])
            av = self.ps_small()
            AV, NTAVb, NU = tb16["AV"], tb16["NTAVb"], tb16["NU"]
            for h in range(4):
                k.MM(av.ap[:, h * 64:(h + 1) * 64], SM.ap[:, h, 0, :], vbf.ap[:, h * 64:(h + 1) * 64], r=[SM, vbf],
                     w=[av], inc=(h == 3))
            k.CP(AV.ap, av.ap[:, 0:256], r=[av], w=[AV], eng="act")
            for h in range(4):
                k.MM(av.ap[:, 256 + h * 64:256 + (h + 1) * 64], Xc.ap[:, h], AV.ap[:, h * 64:(h + 1) * 64],
                     r=[Xc, AV], w=[av], inc=(h == 3))
            k.TS(f["NTAVf"].ap, av.ap[:, 256:512], -1.0, ALU.mult, r=[av], w=[f["NTAVf"]])
            k.CP(NTAVb.ap, f["NTAVf"].ap, r=[f["NTAVf"]], w=[NTAVb], eng="act")
            bd = c128[:, C_BD:C_BD + 128].unsqueeze(1).to_broadcast([128, 2, 128])
            yield
            if kind == "p":
                Ys = [self.ps_small(), self.ps_small()]
                for c in range(2):
                    Rc = slice(64 * c, 64 * c + 64)
                    for jj in range(2):
                        cs = slice(jj * 128, (jj + 1) * 128)
                        k.MM(Ys[c].ap[:, cs], KB3.ap[Rc, 2, cs], KB3.ap[Rc, 1, cs], r=[KB3], w=[Ys[c]], inc=False)
                    for jj in range(2):
                        cs = slice(jj * 128, (jj + 1) * 128)
                        co = slice(256 + jj * 128, 256 + (jj + 1) * 128)
                        k.MM(Ys[c].ap[:, co], KB3.ap[Rc, 0, cs], vbf.ap[Rc, cs], start=True, stop=False,
                             r=[KB3, vbf], w=[Ys[c]], inc=False)
                        k.MM(Ys[c].ap[:, co], KB3.ap[Rc, 1, cs], NTAVb.ap[Rc, cs], start=False, stop=True,
                             r=[KB3, NTAVb], w=[Ys[c]], inc=(jj == 1))
                for c in range(2):
                    k.STT(NG.ap[:, c], Ys[c].ap[:, 0:256].rearrange("p (a b) -> p a b", a=2), -1.0, bd, ALU.mult,
                          ALU.mult, r=[Ys[c]] + cst, w=[NG])
                    y2 = Ys[c].ap[:, 256:512].rearrange("p (a b) -> p a b", a=2)
                    k.CP(Hd.ap[0:64, c], y2[0:64, :, 0:64], r=[Ys[c]], w=[Hd])
                    k.CP(Hd.ap[64:128, c], y2[64:128, :, 64:128], r=[Ys[c]], w=[Hd], eng="act")
                for c in range(2):
                    d1 = dd.ap[:, :, 2 * c:2 * c + 1].to_broadcast([128, 2, 64])
                    d2 = dd.ap[:, :, 2 * c + 1:2 * c + 2].to_broadcast([128, 2, 64])
                    k.TT(Smid.ap, S.ap, d1, ALU.mult, r=[S, dd], w=[Smid])
                    k.CP(SmidB.ap[:, c], Smid.ap, r=[Smid], w=[SmidB], eng="act")
                    k.CP(SmidX.ap[0:64, c, :, 0:64], Smid.ap[0:64], r=[Smid], w=[SmidX])
                    k.CP(SmidX.ap[64:128, c, :, 64:128], Smid.ap[64:128], r=[Smid], w=[SmidX], eng="act")
                    Z = self.ps_small()
                    for jj in range(2):
                        k.MM(Z.ap[:, jj * 64:(jj + 1) * 64], NG.ap[:, c, jj, :], SmidB.ap[:, c, jj, :],
                             r=[NG, SmidB], w=[Z], inc=(jj == 1))
                    k.TT(tmp.ap, Smid.ap, Z.ap[:, 0:128].rearrange("p (a b) -> p a b", a=2), ALU.add,
                         r=[Smid, Z], w=[tmp])
                    k.TT(tmp.ap, tmp.ap, Hd.ap[:, c], ALU.add, r=[tmp, Hd], w=[tmp])
                    k.TT(S.ap, tmp.ap, d2, ALU.mult, r=[tmp, dd], w=[S])
                if t == self.npt - 1:
                    self.store_state_p("rw", l, S, 64)
                yield
                ub = self.ps_small()
                for c in range(2):
                    Rc = slice(64 * c, 64 * c + 64)
                    for jj in range(2):
                        k.MM(ub.ap[Rc, jj * 128:(jj + 1) * 128], RT2.ap[:, jj, 0, Rc], SmidX.ap[:, c, jj, :],
                             r=[RT2, SmidX], w=[ub], inc=(c == 1 and jj == 1))
                k.STT(NU.ap, ub.ap[:, 0:256], -1.0, f["NTAVf"].ap, ALU.mult, ALU.add, r=[ub, f["NTAVf"]], w=[NU])
                yield
                ob_ = self.ps_small()
                for c in range(2):
                    Rc = slice(64 * c, 64 * c + 64)
                    for h in range(4):
                        jj, hl = h // 2, h % 2
                        Hc = slice(h * 64, (h + 1) * 64)
                        out = ob_.ap[Rc, Hc]
                        k.MM(out, SM.ap[:, h, 1, Rc], vbf.ap[:, Hc], start=True, stop=False, r=[SM, vbf], w=[ob_],
                             inc=False)
                        k.MM(out, SM.ap[:, h, 3, Rc], NU.ap[:, Hc], start=False, stop=False, r=[SM, NU], w=[ob_],
                             inc=False)
                        k.MM(out, RT2.ap[:, jj, 1, Rc], SmidX.ap[:, c, jj, hl * 64:(hl + 1) * 64], start=False,
                             stop=True, r=[RT2, SmidX], w=[ob_], inc=(c == 1 and h == 3))
            else:
                S0, Sn = sp["S0"], sp["Sn"]
                src = self.st_in["rw"][l]
                dst = self.s_out["rw"][l]
                big = self.ps_big()
                ub = T(big.ap[:, 0:512], (big.d[0], (big.d[1][0],)))
                ob_ = T(big.ap[:, 512:1024], (big.d[0], (big.d[1][1],)))
                ddv = dd.ap.rearrange("p j (q t) -> p q j t", t=2)
                cmd = [T(self.cm_t[:], self.cm_b.all)]
                zrhs = self.cm_t[:, 0:2, :].rearrange("p a b -> p (a b)")
                for bk in (ub, ob_):
                    k.MM(bk.ap[:, 0:256], self.zl.ap, zrhs, start=True, stop=False, r=[self.zl] + cmd, w=[bk])
                for hf in range(2):
                    q0 = 8 * hf
                    d1 = ddv[:, q0:q0 + 8, :, 0:1].to_broadcast([128, 8, 2, 64])
                    d2 = ddv[:, q0:q0 + 8, :, 1:2].to_broadcast([128, 8, 2, 64])
                    for hl in range(2):
                        for jj in range(2):
                            k.DMA(S0.ap[64 * hl:64 * hl + 64, :, jj, :],
                                  src[q0:q0 + 8, 2 * jj + hl].rearrange("s k v -> k s v"), w=[S0])
                    k.TT(S0.ap, S0.ap, d1, ALU.mult, r=[S0, dd], w=[S0])
                    for q in range(8):
                        sq = q0 + q
                        KBm, NGs, SmB, SmX, RTm = [sp[n_][sq % 2] for n_ in ("KBm", "NGs", "SmB", "SmX", "RTm")]
                        k.TS(KBm.ap.rearrange("p a b -> p (a b)"), KB3.ap.rearrange("p a b -> p (a b)"),
                             c128[:, C_ROWS + sq:C_ROWS + sq + 1], ALU.mult, r=[KB3] + cst, w=[KBm])
                        Y = self.ps_small()
                        for jj in range(2):
                            cs = slice(jj * 128, (jj + 1) * 128)
                            k.MM(Y.ap[:, cs], KBm.ap[:, 2, cs], KB3.ap[:, 1, cs], r=[KBm, KB3], w=[Y], inc=False)
                        for jj in range(2):
                            cs = slice(jj * 128, (jj + 1) * 128)
                            co = slice(256 + jj * 128, 256 + (jj + 1) * 128)
                            k.MM(Y.ap[:, co], KBm.ap[:, 0, cs], vbf.ap[:, cs], start=True, stop=False,
                                 r=[KBm, vbf], w=[Y], inc=False)
                            k.MM(Y.ap[:, co], KBm.ap[:, 1, cs], NTAVb.ap[:, cs], start=False, stop=True,
                                 r=[KBm, NTAVb], w=[Y], inc=(jj == 1))
                        k.STT(NGs.ap, Y.ap[:, 0:256].rearrange("p (a b) -> p a b", a=2), -1.0, bd, ALU.mult,
                              ALU.mult, r=[Y] + cst, w=[NGs])
                        y2 = Y.ap[:, 256:512].rearrange("p (a b) -> p a b", a=2)
                        k.CP(Sn.ap[0:64, q], y2[0:64, :, 0:64], r=[Y], w=[Sn])
                        k.CP(Sn.ap[64:128, q], y2[64:128, :, 64:128], r=[Y], w=[Sn], eng="act")
                        k.CP(SmB.ap, S0.ap[:, q], r=[S0], w=[SmB], eng="act")
                        k.CP(SmX.ap[0:64, :, 0:64], S0.ap[0:64, q], r=[S0], w=[SmX])
                        k.CP(SmX.ap[64:128, :, 64:128], S0.ap[64:128, q], r=[S0], w=[SmX], eng="act")
                        Z = self.ps_small()
                        for jj in range(2):
                            k.MM(Z.ap[:, jj * 64:(jj + 1) * 64], NGs.ap[:, jj, :], SmB.ap[:, jj, :], r=[NGs, SmB],
                                 w=[Z], inc=(jj == 1))
                        k.TT(Sn.ap[:, q], Sn.ap[:, q], Z.ap[:, 0:128].rearrange("p (a b) -> p a b", a=2), ALU.add,
                             r=[Sn, Z], w=[Sn])
                        k.TT(RTm.ap.rearrange("p a b c -> p (a b) c"), RT2.ap.rearrange("p a b c -> p (a b) c"),
                             self.cm_t[:, sq:sq + 1, :].to_broadcast([128, 4, 128]), ALU.mult, r=[RT2] + cmd,
                             w=[RTm])
                        for jj in range(2):
                            k.MM(ub.ap[:, jj * 128:(jj + 1) * 128], RTm.ap[:, jj, 0, :], SmX.ap[:, jj, :],
                                 start=False, stop=False, r=[RTm, SmX], w=[ub], inc=False)
                        for h in range(4):
                            jj, hl = h // 2, h % 2
                            k.MM(ob_.ap[:, h * 64:(h + 1) * 64], RTm.ap[:, jj, 1, :],
                                 SmX.ap[:, jj, hl * 64:(hl + 1) * 64], start=False, stop=False,
                                 r=[RTm, SmX], w=[ob_], inc=(h == 3))
                    k.TT(Sn.ap, Sn.ap, S0.ap, ALU.add, r=[Sn, S0], w=[Sn])
                    k.TT(Sn.ap, Sn.ap, d2, ALU.mult, r=[Sn, dd], w=[Sn])
                    for hl in range(2):
                        for jj in range(2):
                            k.DMA(dst[q0:q0 + 8, 2 * jj + hl].rearrange("s k v -> k s v"),
                                  Sn.ap[64 * hl:64 * hl + 64, :, jj, :], r=[Sn], is_output=True)
                k.MM(ub.ap[:, 0:256], self.zl.ap, zrhs, start=False, stop=True, r=[self.zl] + cmd, w=[ub])
                k.STT(NU.ap, ub.ap[:, 0:256], -1.0, f["NTAVf"].ap, ALU.mult, ALU.add, r=[ub, f["NTAVf"]], w=[NU])
                for h in range(4):
                    Hc = slice(h * 64, (h + 1) * 64)
                    k.MM(ob_.ap[:, Hc], SM.ap[:, h, 1, :], vbf.ap[:, Hc], start=False, stop=False, r=[SM, vbf],
                         w=[ob_], inc=False)
                    k.MM(ob_.ap[:, Hc], SM.ap[:, h, 3, :], NU.ap[:, Hc], start=False, stop=False, r=[SM, NU],
                         w=[ob_], inc=False)
                k.MM(ob_.ap[:, 0:256], self.zl.ap, zrhs, start=False, stop=True, r=[self.zl] + cmd, w=[ob_])
            yield
            osb, t0 = f["t1"], f["t0"]
            k.CP(osb.ap, ob_.ap[:, 0:256], r=[ob_], w=[osb], eng="act")
            k.RSUM(st.ap[:, 0:4], v3(osb.ap), r=[osb], w=[st])
            k.TT(t0.ap, osb.ap, osb.ap, ALU.mult, r=[osb], w=[t0])
            k.RSUM(st.ap[:, 4:8], v3(t0.ap), r=[t0], w=[st])
            k.TS(st.ap[:, 8:12], st.ap[:, 0:4], 1.0 / 64, ALU.mult, r=[st], w=[st])
            k.TT(st.ap[:, 12:16], st.ap[:, 8:12], st.ap[:, 8:12], ALU.mult, r=[st], w=[st])
            k.STT(st.ap[:, 4:8], st.ap[:, 4:8], 1.0 / 64, st.ap[:, 12:16], ALU.mult, ALU.subtract, r=[st], w=[st])
            k.ACT(st.ap[:, 0:4], st.ap[:, 4:8], AF.Ln, r=[st, self.cst_b], w=[st], bias=self.cst(self.CST["EPSLN"]))
            k.ACT(st.ap[:, 0:4], st.ap[:, 0:4], AF.Exp, r=[st], w=[st], scale=-0.5)
            k.TT(v3(t0.ap), v3(osb.ap), st.ap[:, 8:12].unsqueeze(2).to_broadcast([128, 4, 64]), ALU.subtract,
                 r=[osb, st], w=[t0])
            k.TT(v3(t0.ap), v3(t0.ap), st.ap[:, 0:4].unsqueeze(2).to_broadcast([128, 4, 64]), ALU.mult,
                 r=[t0, st], w=[t0])
            k.TT(t0.ap, t0.ap, pb["rw_ln_w"].ap, ALU.mult, r=[t0, pb["rw_ln_w"]], w=[t0])
            k.TT(t0.ap, t0.ap, pb["rw_ln_b"].ap, ALU.add, r=[t0, pb["rw_ln_b"]], w=[t0])
            k.TT(t0.ap, t0.ap, f["bon"].ap, ALU.add, r=[t0, f["bon"]], w=[t0])
            k.TT(tb16["ob"].ap, t0.ap, f["gsb"].ap, ALU.mult, r=[t0, f["gsb"]], w=[tb16["ob"]])
            self.to_oT(tb16["ob"], j, mi)
            yield
        self.rw_top = self.sc.top
        self.sc.reset(m)


_WNAMES = ("attn_norm_w", "w_in", "hg_lb_logits", "hg_norm_w", "gla_wa2", "gla_ba", "gla_norm_w", "rw_mu", "rw_w0",
           "rw_w2", "rw_a0", "rw_a2", "rw_g2", "rw_kk", "rw_ka", "rw_rk", "rw_ln_w", "rw_ln_b", "w_branch", "w_out",
           "ffn_norm_w", "w_ffn_in", "w_ffn_out", "final_norm_w")


def make_in_map(inputs, core, npt, consts):
    f = lambda a: np.ascontiguousarray(np.asarray(a, dtype=np.float32))
    xp = f(inputs["x_prompt"])[core, :npt * 128]
    s0, s1 = core * DEC_PER_CORE, (core + 1) * DEC_PER_CORE
    xs = f(inputs["x_sample"])[s0:s1].reshape(DEC_PER_CORE * DEC_SEQ, D)
    m = {"xin": np.ascontiguousarray(np.concatenate([xp, xs], 0)),
         "st_hg": f(inputs["state_hgrn"])[:, s0:s1], "st_gla": f(inputs["state_gla"])[:, s0:s1],
         "st_rw": f(inputs["state_rwkv"])[:, s0:s1], "st_ret": f(inputs["state_ret"])[:, s0:s1],
         "st_shift": f(inputs["state_rwkv_shift"])[:, s0:s1]}
    for n in _WNAMES:
        m[n] = f(inputs[n])
    m["c128"], m["colmask"], m["rot"] = consts
    return {k_: np.ascontiguousarray(v) for k_, v in m.items()}


_NC_CACHE = {}


def kernel(**inputs):
    npt = SEQ // 128
    passes = [list(range(0, 8)), list(range(8, 17))]
    key = (npt, str(passes))
    if key not in _NC_CACHE:
        _NC_CACHE[key] = build(npt, passes)
    nc = _NC_CACHE[key]
    consts = make_consts(npt)
    in_maps = [make_in_map(inputs, c, npt, consts) for c in range(N_CORES)]
    res = run_bass_kernel_spmd(nc, in_maps, core_ids=list(range(N_CORES))).results
    y_prompt = np.stack([r["yout"][:npt * 128] for r in res], 0)
    y_sample = np.concatenate([r["yout"][npt * 128:].reshape(DEC_PER_CORE, DEC_SEQ, D) for r in res], 0)
    outs = [y_prompt.astype(np.float32), y_sample.astype(np.float32)]
    for nm in ("p_hg", "p_gla", "p_rw", "p_shift", "p_ret"):
        outs.append(np.stack([r[nm] for r in res], 1).astype(np.float32))
    for nm in ("s_hg", "s_gla", "s_rw", "s_shift", "s_ret"):
        outs.append(np.concatenate([r[nm] for r in res], 1).astype(np.float32))
    return tuple(outs)
```

```python
import contextlib
import math
import os
import numpy as np
import concourse.bass as bass
import concourse.mybir as mybir
from concourse.bass_utils import run_bass_kernel_spmd

F32 = mybir.dt.float32
BF16 = mybir.dt.bfloat16
AF = mybir.ActivationFunctionType
ALU = mybir.AluOpType
AX = mybir.AxisListType

D = 1024
DEPTH = 2
N_CORES = 8
SEQ = 2048
DEC_SEQ = 8
DEC_PER_CORE = 16
PAST_LEN = 16384
N_IN = 7952
D_FF = 2816
NORM_EPS = 1e-6
RW_LN_EPS = 64e-5
C_HG = 0
C_GLA = 1024
C_RW = 1808
C_RET = 2832
C_GATE = 3856

ENGS = ("pe", "dve", "act", "pool", "sp")


class Buf:
    def __init__(self, name, n=1, excl=False):
        self.name = name
        self.n = n
        self.excl = excl
        self.w = [None] * n
        self.r = [dict() for _ in range(n)]

    def __getitem__(self, idx):
        if isinstance(idx, int):
            return (self, (idx,))
        if isinstance(idx, slice):
            return (self, tuple(range(*idx.indices(self.n))))
        return (self, tuple(idx))

    @property
    def all(self):
        return (self, tuple(range(self.n)))


def _cells(x):
    if isinstance(x, Buf):
        return x.all
    return x


class Prog:
    NDMA = {None: 6, "A": 3, "B": 3}

    def __init__(self, nc, es):
        self.nc = nc
        self.sem = {}
        self.cnt = {}
        self.q = {}
        self.seen = {}
        self.dma_sems = {}
        self.dma_next = {}
        for st in (None, "A", "B"):
            tag = st or "m"
            self.q[st] = {e: [] for e in ENGS}
            self.seen[st] = {e: {} for e in ENGS}
            for e in ENGS:
                k = "%s:%s" % (tag, e)
                self.sem[k] = es.enter_context(nc.semaphore("s%s_%s" % (tag, e)))
                self.cnt[k] = 0
            for iss in ("sp", "pool", "act"):
                lst = []
                for i in range(self.NDMA[st]):
                    k = "%s:d_%s%d" % (tag, iss, i)
                    self.sem[k] = es.enter_context(nc.semaphore("d%s_%s%d" % (tag, iss, i)))
                    self.cnt[k] = 0
                    lst.append(k)
                self.dma_sems[(st, iss)] = lst
                self.dma_next[(st, iss)] = 0
        self.cur = None
        self.n_instr = 0
        self.out_tokens = []
        self.glob = {"A": [], "B": []}
        self.act_tbl = None
        self.n_tbl_switch = 0

    def _push(self, eng, emit, deps=None, tok=None, cost=0.4, signals=True, tbl=None):
        if self.cur is None:
            self.q[None][eng].append(emit)
            if tbl is not None:
                self.act_tbl = tbl
        else:
            self.glob[self.cur].append(dict(eng=eng, emit=emit, deps=dict(deps or {}), tok=tok, cost=cost,
                                            signals=signals, tbl=tbl))

    def _note(self, tok, deps):
        pass

    def ek(self, eng):
        return "%s:%s" % (self.cur or "m", eng)

    def _deps(self, reads, writes, eng=None):
        deps = {}
        self._me = self.ek(eng) if eng is not None else None

        def need(tok):
            if tok is None:
                return
            k, v = tok
            if deps.get(k, 0) < v:
                deps[k] = v

        for b, cells in map(_cells, reads):
            for c in cells:
                need(b.w[c])
                if b.excl:
                    for k, v in b.r[c].items():
                        if k != self._me:
                            need((k, v))
        for b, cells in map(_cells, writes):
            for c in cells:
                need(b.w[c])
                for k, v in b.r[c].items():
                    need((k, v))
        return deps

    def _commit(self, tok, reads, writes):
        k, v = tok
        for b, cells in map(_cells, reads):
            for c in cells:
                if b.r[c].get(k, 0) < v:
                    b.r[c][k] = v
        for b, cells in map(_cells, writes):
            for c in cells:
                b.w[c] = tok
                b.r[c] = {}

    def _waits(self, eng, deps):
        ws = []
        seen = self.seen[self.cur][eng]
        me = self.ek(eng)
        for k, v in deps.items():
            if seen.get(k, 0) >= v:
                continue
            if k == me and v > self.cnt[me]:
                continue
            seen[k] = v
            ws.append((self.sem[k], v))
        return ws

    def op(self, eng, fn, reads=(), writes=(), inc=True, cost=0.4, tbl=None):
        deps = self._deps(reads, writes, eng)
        ws = self._waits(eng, deps)
        me = self.ek(eng)
        sem = self.sem[me]
        if inc:
            self.cnt[me] += 1
            tok = (me, self.cnt[me])
        else:
            tok = (me, self.cnt[me] + 1)
        self._commit(tok, reads, writes)
        self._note(tok, deps)
        self.n_instr += 1

        def emit(e, ws=ws, fn=fn, inc=inc, sem=sem):
            for s, v in ws:
                e.wait_ge(s, v)
            ins = fn(e)
            if inc:
                ins.then_inc(sem, 1)
        self._push(eng, emit, deps, tok, cost, signals=inc, tbl=tbl)
        return tok

    def dma(self, iss, out, in_, reads=(), writes=(), is_output=False):
        deps = self._deps(reads, writes)
        key = (self.cur, iss)
        i = self.dma_next[key]
        self.dma_next[key] = (i + 1) % len(self.dma_sems[key])
        k = self.dma_sems[key][i]
        if self.cnt[k] > 0:
            if deps.get(k, 0) < self.cnt[k]:
                deps[k] = self.cnt[k]
        ws = self._waits(iss, deps)
        self.cnt[k] += 16
        tok = (k, self.cnt[k])
        self._commit(tok, reads, writes)
        self._note(tok, deps)
        sem = self.sem[k]
        self.n_instr += 1
        if is_output:
            self.out_tokens.append(tok)

        def emit(e, ws=ws, sem=sem, out=out, in_=in_):
            for s, v in ws:
                e.wait_ge(s, v)
            e.dma_start(out=out, in_=in_).then_inc(sem, 16)
        self._push(iss, emit, deps, tok, 2.5, signals=True)
        return tok

    def merge_streams(self):
        L = {"A": self.glob["A"], "B": self.glob["B"]}
        LAT = float(os.environ.get("MK_LAT", "0.6"))
        TBL = float(os.environ.get("MK_TBL", "1.3"))
        prod = {}
        for st in ("A", "B"):
            for idx, ins in enumerate(L[st]):
                if ins["signals"] and ins["tok"] is not None:
                    prod[ins["tok"]] = (st, idx)
        fin = {"A": [0.0] * len(L["A"]), "B": [0.0] * len(L["B"])}
        placed = {"A": 0, "B": 0}
        t_free = {e: 0.0 for e in ENGS}

        def start_time(st):
            i = placed[st]
            if i >= len(L[st]):
                return None
            ins = L[st][i]
            ready = 0.0
            for tk in ins["deps"].items():
                p = prod.get(tk)
                if p is None:
                    continue
                pst, pidx = p
                if pidx >= placed[pst]:
                    if pst != st:
                        return float("inf")
                    continue
                lat = LAT if L[pst][pidx]["eng"] != ins["eng"] else 0.05
                ready = max(ready, fin[pst][pidx] + lat)
            pen = TBL if (ins["tbl"] is not None and ins["tbl"] != tblstate[0]) else 0.0
            return max(ready, t_free[ins["eng"]]) + pen

        tblstate = [self.act_tbl]
        rem = {}
        for st in ("A", "B"):
            r = [0.0] * (len(L[st]) + 1)
            for i in range(len(L[st]) - 1, -1, -1):
                r[i] = r[i + 1] + L[st][i]["cost"]
            rem[st] = r
        BIAS = float(os.environ.get("MK_BIAS", "0.03"))
        while placed["A"] < len(L["A"]) or placed["B"] < len(L["B"]):
            sa, sb = start_time("A"), start_time("B")
            EF = float(os.environ.get("MK_EF", "1.0"))
            ka = None if sa is None else sa + EF * L["A"][placed["A"]]["cost"] - BIAS * rem["A"][placed["A"]]
            kb = None if sb is None else sb + EF * L["B"][placed["B"]]["cost"] - BIAS * rem["B"][placed["B"]]
            if sb is None or (sa is not None and ka <= kb):
                st, t0 = "A", sa
            else:
                st, t0 = "B", sb
            if t0 == float("inf"):
                st = "B" if st == "A" else "A"
                t0 = start_time(st)
                assert t0 is not None and t0 != float("inf")
            ins = L[st][placed[st]]
            e = ins["eng"]
            if ins["tbl"] is not None and ins["tbl"] != tblstate[0]:
                tblstate[0] = ins["tbl"]
                self.n_tbl_switch += 1
            is_dma = ins["cost"] >= 2.0 and e in ("sp", "pool")
            fin[st][placed[st]] = t0 + ins["cost"]
            t_free[e] = t0 + (0.15 if is_dma else ins["cost"])
            placed[st] += 1
            self.q[None][e].append(ins["emit"])
        self.est_span = max(t_free.values())
        self.act_tbl = tblstate[0]
        self.glob = {"A": [], "B": []}

    def finish(self):
        assert self.cur is None
        deps = {}
        for k, v in self.out_tokens:
            deps[k] = max(deps.get(k, 0), v)
        for k in self.sem:
            if k != "m:sp" and self.cnt[k] > 0:
                deps[k] = max(deps.get(k, 0), self.cnt[k])
        ws = self._waits("sp", deps)

        def emit(e, ws=ws):
            for s, v in ws:
                e.wait_ge(s, v)
        self.q[None]["sp"].append(emit)

    def emit_all(self):
        nc = self.nc
        q = self.q[None]
        with nc.Block() as block:
            @block.tensor
            def _(e):
                for f in q["pe"]:
                    f(e)

            @block.vector
            def _(e):
                for f in q["dve"]:
                    f(e)

            @block.scalar
            def _(e):
                for f in q["act"]:
                    f(e)

            @block.gpsimd
            def _(e):
                for f in q["pool"]:
                    f(e)

            @block.sync
            def _(e):
                for f in q["sp"]:
                    f(e)


class T:
    def __init__(self, ap, d):
        self.ap = ap
        self.d = d


def _dl(lst):
    out = []
    for x in lst:
        if isinstance(x, T):
            out.append(x.d)
        elif x is not None:
            out.append(x)
    return out


_SZ = {F32: 4, BF16: 2}


class Arena:
    def __init__(self, nc, es, name, nbytes, cell=512):
        self.t = es.enter_context(nc.sbuf_tensor(name, [128, nbytes // 2], BF16))
        self.cell = cell
        self.nbytes = nbytes
        self.buf = Buf(name, (nbytes + cell - 1) // cell)
        self.top = 0

    def view(self, off, shape, dt):
        n = 1
        for s in shape:
            n *= s
        nb = n * _SZ[dt]
        assert off % 4 == 0 and off + nb <= self.nbytes, (off, nb, self.nbytes)
        ap = self.t[:, off // 2:(off + nb) // 2]
        if dt == F32:
            ap = ap.bitcast(F32)
        if len(shape) == 2:
            ap = ap.rearrange("p (a b) -> p a b", a=shape[0])
        elif len(shape) == 3:
            ap = ap.rearrange("p (a b c) -> p a b c", a=shape[0], b=shape[1])
        elif len(shape) == 4:
            ap = ap.rearrange("p (a b c d) -> p a b c d", a=shape[0], b=shape[1], c=shape[2])
        cells = tuple(range(off // self.cell, (off + nb + self.cell - 1) // self.cell))
        return T(ap, (self.buf, cells))

    def alloc(self, shape, dt):
        n = 1
        for s in shape:
            n *= s
        nb = n * _SZ[dt]
        off = (self.top + self.cell - 1) // self.cell * self.cell
        self.top = off + nb
        return self.view(off, shape, dt)

    def mark(self):
        return self.top

    def reset(self, m=0):
        self.top = m


class K:
    def __init__(self, nc, es):
        self.nc = nc
        self.es = es
        self.P = Prog(nc, es)
        self.uid = 0

    def sb(self, name, shape, dt=F32, cells=1):
        t = self.es.enter_context(self.nc.sbuf_tensor(name, shape, dt))
        return t, Buf(name, cells)

    @staticmethod
    def _n(ap):
        n = 1
        for d in ap.shape[1:]:
            n *= d
        return n

    def MM(self, out, lhsT, rhs, start=True, stop=True, r=(), w=(), inc=True):
        fp32 = 4.0 if lhsT.dtype == F32 else 1.0
        return self.P.op("pe", lambda e: e.matmul(out, lhsT, rhs, start=start, stop=stop),
                         reads=_dl(r), writes=_dl(w), inc=inc, cost=0.06 + fp32 * self._n(out) / 1500.0)

    def TR(self, out, in_, ident, r=(), w=(), inc=True):
        return self.P.op("pe", lambda e: e.transpose(out, in_, ident), reads=_dl(r), writes=_dl(w), inc=inc,
                         cost=0.25)

    def ACT(self, out, in_, func, r=(), w=(), bias=None, scale=None, accum=None):
        kw = {}
        if bias is not None:
            kw["bias"] = bias
        if scale is not None:
            kw["scale"] = scale
        if accum is not None:
            kw["accum_out"] = accum
        tbl = "T" if func in (AF.Tanh, AF.Silu) else ("L" if func == AF.Ln else None)
        return self.P.op("act", lambda e: e.activation(out=out, in_=in_, func=func, **kw),
                         reads=_dl(r), writes=_dl(w), cost=0.25 + self._n(out) / 1400.0, tbl=tbl)

    def TT(self, out, in0, in1, op, r=(), w=(), eng="dve"):
        return self.P.op(eng, lambda e: e.tensor_tensor(out, in0, in1, op), reads=_dl(r), writes=_dl(w),
                         cost=0.2 + self._n(out) / 1000.0)

    def TS(self, out, in0, s1, op0, s2=None, op1=None, r=(), w=(), eng="dve"):
        if op1 is None:
            return self.P.op(eng, lambda e: e.tensor_scalar(out, in0, s1, None, op0), reads=_dl(r), writes=_dl(w),
                         cost=0.2 + self._n(out) / 1000.0)
        return self.P.op(eng, lambda e: e.tensor_scalar(out, in0, s1, s2, op0, op1), reads=_dl(r), writes=_dl(w),
                         cost=0.2 + self._n(out) / 1000.0)

    def STT(self, out, in0, scalar, in1, op0, op1, r=(), w=(), eng="dve"):
        return self.P.op(eng, lambda e: e.scalar_tensor_tensor(out, in0, scalar, in1, op0, op1),
                         reads=_dl(r), writes=_dl(w), cost=0.2 + self._n(out) / 1000.0)

    def CP(self, out, in_, r=(), w=(), eng="dve"):
        if eng == "act":
            return self.ACT(out, in_, AF.Copy, r=r, w=w)
        return self.P.op(eng, lambda e: e.tensor_copy(out, in_), reads=_dl(r), writes=_dl(w),
                         cost=0.2 + self._n(out) / 1000.0)

    def RSUM(self, out, in_, r=(), w=(), eng="dve"):
        return self.P.op(eng, lambda e: e.reduce_sum(out, in_, axis=AX.X), reads=_dl(r), writes=_dl(w),
                         cost=0.2 + self._n(out) / 1000.0)

    def MEMSET(self, out, val, w=(), eng="dve"):
        return self.P.op(eng, lambda e: e.memset(out, val), writes=_dl(w))

    def DMA(self, out, in_, r=(), w=(), iss="sp", is_output=False):
        return self.P.dma(iss, out, in_, reads=_dl(r), writes=_dl(w), is_output=is_output)


C_ID, C_MCP, C_MCS, C_TIP, C_TIS, C_TSP, C_TSS, C_LP, C_LS = [i * 128 for i in range(9)]
C_INDP = 9 * 128
C_INDS = C_INDP + 4
C_ROWS = C_INDS + 32
C_SHP = C_ROWS + 16
C_SHS = C_SHP + 128
C_CAR = C_SHS + 128
C_SEL = C_CAR + 128
C_BD = C_SEL + 128
NC128 = C_BD + 128


def _chunk_consts(C):
    n = 128 // C
    ch = np.arange(128) // C
    same = ch[:, None] == ch[None, :]
    s = np.arange(128)[:, None]
    t = np.arange(128)[None, :]
    mid = (ch * C + (C // 2 - 1))
    mcum = (same & (s <= t)).astype(np.float32) - (same & (s <= mid[None, :])).astype(np.float32)
    ti = (same & (s <= t)).astype(np.float32)
    tstrict = (same & (s < t)).astype(np.float32)
    low = (same & (s > t)).astype(np.float32)
    ind = np.zeros((128, 2 * n), np.float32)
    for c in range(n):
        rows = np.arange(c * C, (c + 1) * C)
        m = c * C + C // 2 - 1
        ind[rows[rows <= m], 2 * c] = 1.0
        ind[rows[rows > m], 2 * c + 1] = 1.0
    return mcum, ti, tstrict, low, ind


def make_consts(npt):
    c = np.zeros((128, NC128), np.float32)
    c[:, C_ID:C_ID + 128] = np.eye(128, dtype=np.float32)
    mp = _chunk_consts(64)
    ms = _chunk_consts(8)
    c[:, C_MCP:C_MCP + 128], c[:, C_TIP:C_TIP + 128], c[:, C_TSP:C_TSP + 128], c[:, C_LP:C_LP + 128] = mp[:4]
    c[:, C_MCS:C_MCS + 128], c[:, C_TIS:C_TIS + 128], c[:, C_TSS:C_TSS + 128], c[:, C_LS:C_LS + 128] = ms[:4]
    c[:, C_INDP:C_INDP + 4] = mp[4]
    c[:, C_INDS:C_INDS + 32] = ms[4]
    seq = np.arange(128) // 8
    c[:, C_ROWS:C_ROWS + 16] = (seq[:, None] == np.arange(16)[None, :]).astype(np.float32)
    sh = np.zeros((128, 128), np.float32)
    sh[np.arange(127), np.arange(1, 128)] = 1.0
    c[:, C_SHP:C_SHP + 128] = sh
    shs = sh.copy()
    shs[:, np.arange(0, 128, 8)] = 0.0
    c[:, C_SHS:C_SHS + 128] = shs
    c[127, C_CAR] = 1.0
    for q in range(16):
        c[q, C_SEL + 8 * q] = 1.0
    blk = np.arange(128) // 64
    c[:, C_BD:C_BD + 128] = (blk[:, None] == blk[None, :]).astype(np.float32)
    colmask = np.broadcast_to((np.arange(16)[:, None] == seq[None, :]).astype(np.float32)[None], (128, 16, 128))
    colmask = np.ascontiguousarray(colmask).reshape(128, 2048)
    inv = np.power(np.float32(10000.0), -(np.arange(32, dtype=np.float32) / np.float32(32))).astype(np.float32)
    rot = np.zeros((npt + 1, 128, 64), np.float32)
    for i in range(npt + 1):
        if i < npt:
            pos = (np.arange(128) + 128 * i).astype(np.float32)
        else:
            pos = (np.float32(PAST_LEN) + (np.arange(128) % 8).astype(np.float32)).astype(np.float32)
        ang = (pos[:, None] * inv[None, :]).astype(np.float32)
        rot[i, :, :32] = np.cos(ang)
        rot[i, :, 32:] = np.sin(ang)
    return c, colmask, rot


PB_NORM, PB_MU, PB_LB, PB_HGN, PB_GBA, PB_GLN, PB_W0, PB_A0, PB_KK, PB_KA, PB_RK, PB_LNW, PB_LNB = (
    0, 1024, 2048, 2560, 2624, 2752, 2816, 3072, 3328, 3584, 3840, 4096, 4352)
NPB = 4608


def build(npt, passes, mixers=("hg", "gla", "rw", "ret"), depth=DEPTH, debug=None):
    nc = bass.Bass("TRN2", target_bir_lowering=False)
    es = contextlib.ExitStack()
    k = K(nc, es)
    P = k.P
    NT = npt + 1
    NTP = max(len(p) for p in passes)
    NTOKP = NTP * 128

    def din(name, shape):
        return nc.dram_tensor(name, list(shape), F32, kind="ExternalInput").ap()

    def dout(name, shape):
        return nc.dram_tensor(name, list(shape), F32, kind="ExternalOutput").ap()

    xin = din("xin", [NT * 128, D])
    st_in = {"hg": din("st_hg", [DEPTH, 16, 4, 64, 64]), "gla": din("st_gla", [DEPTH, 16, 4, 32, 64]),
             "rw": din("st_rw", [DEPTH, 16, 4, 64, 64]), "ret": din("st_ret", [DEPTH, 16, 4, 64, 64])}
    st_shift = din("st_shift", [DEPTH, 16, 1024])
    W = {}
    for name, shape in (("attn_norm_w", [2, 1024]), ("w_in", [2, 1024, N_IN]), ("hg_lb_logits", [2, 256]),
                        ("hg_norm_w", [2, 64]), ("gla_wa2", [2, 16, 128]), ("gla_ba", [2, 128]),
                        ("gla_norm_w", [2, 64]), ("rw_mu", [2, 1024]), ("rw_w0", [2, 256]),
                        ("rw_w2", [2, 64, 256]), ("rw_a0", [2, 256]), ("rw_a2", [2, 64, 256]),
                        ("rw_g2", [2, 128, 256]), ("rw_kk", [2, 256]), ("rw_ka", [2, 256]), ("rw_rk", [2, 256]),
                        ("rw_ln_w", [2, 256]), ("rw_ln_b", [2, 256]), ("w_branch", [2, 4, 256, 1024]),
                        ("w_out", [2, 1024, 1024]), ("ffn_norm_w", [2, 1024]), ("w_ffn_in", [2, 1024, 2 * D_FF]),
                        ("w_ffn_out", [2, D_FF, 1024]), ("final_norm_w", [1024])):
        W[name] = din(name, shape)
    c128_d = din("c128", [128, NC128])
    colmask_d = din("colmask", [128, 2048])
    rot_d = din("rot", [NT, 128, 64])

    yout = dout("yout", [NT * 128, D])
    p_out = {"hg": dout("p_hg", [DEPTH, 4, 64, 64]), "gla": dout("p_gla", [DEPTH, 4, 32, 64]),
             "rw": dout("p_rw", [DEPTH, 4, 64, 64]), "ret": dout("p_ret", [DEPTH, 4, 64, 64])}
    p_shift = dout("p_shift", [DEPTH, 1024])
    s_out = {"hg": dout("s_hg", [DEPTH, 16, 4, 64, 64]), "gla": dout("s_gla", [DEPTH, 16, 4, 32, 64]),
             "rw": dout("s_rw", [DEPTH, 16, 4, 64, 64]), "ret": dout("s_ret", [DEPTH, 16, 4, 64, 64])}
    s_shift = dout("s_shift", [DEPTH, 16, 1024])
    dbg_out = {}
    if debug:
        for name, shape in debug.items():
            dbg_out[name] = dout("dbg_" + name, shape)

    x_t, x_b = k.sb("x", [128, NTP, D], F32, cells=NTP)
    hT_t, hT_b = k.sb("hT", [128, 8, NTOKP], BF16, cells=NTP)
    oT_t, oT_b = k.sb("oT", [128, 8, NTOKP], BF16, cells=NTP * 4)
    c128_t, c128_b = k.sb("c128s", [128, NC128], F32)
    cm_t, cm_b = k.sb("colmask_s", [128, 16, 128], BF16)
    idb_t, idb_b = k.sb("identb", [128, 128], BF16)
    zl_t, zl_b = k.sb("zerol", [128, 128], BF16)
    pb_t, pb_b = k.sb("pbc", [128, NPB], F32, cells=16)
    lbc_t, lbc_b = k.sb("lbc", [128, 2, 256], F32)
    cst_t, cst_b = k.sb("cst", [128, 8], F32)
    rwl_t, rwl_b = k.sb("rwl", [128, 3, 256], BF16)
    gwa_t, gwa_b = k.sb("gwa", [32, 128], BF16)
    S_t, S_b = k.sb("Sst", [128, DEPTH * 4, 2, 64], F32, cells=DEPTH * 4)
    rwp_t, rwp_b = k.sb("rwprev", [128, DEPTH, 1024], F32, cells=DEPTH)
    gconst_t, gconst_b = k.sb("gconst", [128, 256], F32)
    rot_t, rot_b = k.sb("rots", [128, 2, 64], F32, cells=2)
    wa = Arena(nc, es, "warena", 32768, cell=2048)
    sc = Arena(nc, es, "scratch", 57344, cell=256)

    psA = es.enter_context(nc.psum_tensor("psA", [128, 8, 512], F32))
    psA_b = Buf("psA", 8, excl=True)
    rr = {}
    pools = {None: dict(big=[0, 1], small=[4, 5, 6, 7]), "A": dict(big=[0], small=[4, 5]),
             "B": dict(big=[1], small=[6, 7])}

    def ps_big():
        st = P.cur
        lst = pools[st]["big"]
        i = rr.get((st, "big"), 0)
        rr[(st, "big")] = (i + 1) % len(lst)
        b = lst[i]
        ap = psA[:, 2 * b:2 * b + 2, :].rearrange("p a b -> p (a b)")
        return T(ap, (psA_b, (2 * b, 2 * b + 1)))

    def ps_small():
        st = P.cur
        lst = pools[st]["small"]
        i = rr.get((st, "small"), 0)
        rr[(st, "small")] = (i + 1) % len(lst)
        b = lst[i]
        return T(psA[:, b, :], (psA_b, (b,)))

    def ps_tb():
        t = ps_small()
        return T(t.ap.bitcast(BF16), t.d)

    ident = T(idb_t[:], idb_b.all)

    def cst(i):
        return cst_t[:, i:i + 1]

    k.DMA(c128_t[:], c128_d[:, :], w=[c128_b])
    k.DMA(cm_t[:].rearrange("p a b -> p (a b)"), colmask_d[:, :], w=[cm_b], iss="pool")
    k.DMA(idb_t[:], c128_d[:, C_ID:C_ID + 128], w=[idb_b], iss="pool")
    k.TS(zl_t[:], idb_t[:], 0.0, ALU.mult, r=[idb_b], w=[zl_b])
    CST_EPSD, CST_ONE, CST_LNH, CST_EPS64, CST_EPSLN, CST_TINY = 0, 1, 2, 3, 4, 5
    k.MEMSET(cst_t[:, 0:1], D * NORM_EPS, w=[cst_b])
    k.MEMSET(cst_t[:, 1:2], 1.0, w=[cst_b])
    k.MEMSET(cst_t[:, 2:3], math.log(0.5), w=[cst_b])
    k.MEMSET(cst_t[:, 3:4], NORM_EPS, w=[cst_b])
    k.MEMSET(cst_t[:, 4:5], RW_LN_EPS, w=[cst_b])
    k.MEMSET(cst_t[:, 5:6], 1e-24, w=[cst_b])
    for h in range(4):
        k.MEMSET(gconst_t[:, 64 * h:64 * h + 64], math.log1p(-2.0 ** (-5.0 - h)), w=[gconst_b])
    k.MEMSET(S_t[:].rearrange("p a b c -> p (a b c)"), 0.0, w=[S_b])
    k.MEMSET(rwp_t[:].rearrange("p a b -> p (a b)"), 0.0, w=[rwp_b])
    cmat = {"p": dict(mcum=C_MCP, ti=C_TIP, ts=C_TSP, low=C_LP, ind=C_INDP, nind=4),
            "s": dict(mcum=C_MCS, ti=C_TIS, ts=C_TSS, low=C_LS, ind=C_INDS, nind=32)}

    def cm(kind, name):
        o = cmat[kind][name]
        n = cmat[kind]["nind"] if name == "ind" else 128
        return c128_t[:, o:o + n]

    ctx = dict(nc=nc, k=k, P=P, npt=npt, NTP=NTP, NTOKP=NTOKP, W=W, st_in=st_in, st_shift=st_shift,
               p_out=p_out, p_shift=p_shift, s_out=s_out, s_shift=s_shift, x_t=x_t, x_b=x_b, hT_t=hT_t, hT_b=hT_b,
               oT_t=oT_t, oT_b=oT_b, c128_t=c128_t, c128_b=c128_b, cm_t=cm_t, cm_b=cm_b, ident=ident, zl=T(zl_t[:], zl_b.all),
               pb_t=pb_t, pb_b=pb_b, lbc_t=lbc_t, lbc_b=lbc_b, cst=cst, cst_b=cst_b, rwl_t=rwl_t, rwl_b=rwl_b,
               gwa_t=gwa_t, gwa_b=gwa_b, S_t=S_t, S_b=S_b, rwp_t=rwp_t, rwp_b=rwp_b, gconst_t=gconst_t,
               gconst_b=gconst_b, rot_t=rot_t, rot_b=rot_b, rot_d=rot_d, wa=wa, sc=sc, ps_big=ps_big,
               ps_small=ps_small, ps_tb=ps_tb, cm=cm, dbg_out=dbg_out, xin=xin, yout=yout,
               CST=dict(EPSD=0, ONE=1, LNH=2, EPS64=3, EPSLN=4, TINY=5), mixers=mixers, depth=depth)
    g = Gen(ctx)

    for pi, tiles in enumerate(passes):
        g.load_x(tiles)
        g.final_done = False
        g.attn_norm_done = False
        for l in range(depth):
            g.layer(l, tiles, pi)
        if not g.final_done:
            g.final(tiles)
    P.finish()
    P.emit_all()
    es.close()
    return nc


class Gen:
    def __init__(self, ctx):
        self.__dict__.update(ctx)

    def tile_kind(self, t):
        return "p" if t < self.npt else "s"

    def pbc(self, off, n):
        return T(self.pb_t[:, off:off + n], (self.pb_b, tuple(range(off // 288, (off + n - 1) // 288 + 1))))

    def load_pb(self, off, src_row):
        n = src_row.shape[0]
        t = self.pbc(off, n)
        self.k.DMA(t.ap, src_row.partition_broadcast(128), w=[t])
        return t

    def load_x(self, tiles):
        k = self.k
        for j, t in enumerate(tiles):
            k.DMA(self.x_t[:, j, :], self.xin[t * 128:(t + 1) * 128, :], w=[self.x_b[j]])

    def rms_stats(self, j, sc_junk, rstd):
        k = self.k
        ss = self.sc.alloc([1], F32)
        lnv = self.sc.alloc([1], F32)
        k.ACT(sc_junk.ap, self.x_t[:, j, :], AF.Square, r=[self.x_b[j]], w=[sc_junk, ss], accum=ss.ap)
        k.ACT(lnv.ap, ss.ap, AF.Ln, r=[ss, self.cst_b], w=[lnv], bias=self.cst(self.CST["EPSD"]))
        k.ACT(rstd.ap, lnv.ap, AF.Exp, r=[lnv], w=[rstd], scale=-0.5)

    def norm_prep(self, wrow):
        k = self.k
        wt = self.load_pb(PB_NORM, wrow)
        k.TS(wt.ap, wt.ap, float(math.sqrt(D)), ALU.mult, r=[wt], w=[wt])
        return wt

    def norm_alloc(self):
        return [dict(junk=self.sc.alloc([D], F32), hb=self.sc.alloc([D], BF16), rstd=self.sc.alloc([1], F32),
                     ss=self.sc.alloc([1], F32), lnv=self.sc.alloc([1], F32), y=None) for _ in range(2)]

    def rms_stats2(self, j, s):
        k = self.k
        k.ACT(s["junk"].ap, self.x_t[:, j, :], AF.Square, r=[self.x_b[j]], w=[s["junk"], s["ss"]], accum=s["ss"].ap)
        k.ACT(s["lnv"].ap, s["ss"].ap, AF.Ln, r=[s["ss"], self.cst_b], w=[s["lnv"]], bias=self.cst(self.CST["EPSD"]))
        k.ACT(s["rstd"].ap, s["lnv"].ap, AF.Exp, r=[s["lnv"]], w=[s["rstd"]], scale=-0.5)

    def norm_tile(self, j, wt, s, part="ab"):
        k = self.k
        if "a" in part:
            self.rms_stats2(j, s)
            k.STT(s["hb"].ap, self.x_t[:, j, :], s["rstd"].ap[:, 0:1], wt.ap, ALU.mult, ALU.mult,
                  r=[self.x_b[j], s["rstd"], wt], w=[s["hb"]])
        if "b" not in part:
            return
        tb = self.ps_tb()
        for kt in range(8):
            k.TR(tb.ap[:, kt * 128:(kt + 1) * 128], s["hb"].ap[:, kt * 128:(kt + 1) * 128], self.ident.ap,
                 r=[s["hb"], self.ident], w=[tb], inc=(kt == 7))
        k.CP(self.hT_t[:, :, j * 128:(j + 1) * 128], tb.ap.rearrange("p (a b) -> p a b", a=8), r=[tb],
             w=[self.hT_b[j]], eng=("act" if j % 2 == 0 else "dve"))

    def lagged_norm(self, wt, sets):
        prev = [None]

        def step(j):
            self.norm_tile(j, wt, sets[j % 2], part="a")
            if prev[0] is not None:
                self.norm_tile(prev[0], wt, sets[prev[0] % 2], part="b")
            prev[0] = j

        def flush():
            if prev[0] is not None:
                self.norm_tile(prev[0], wt, sets[prev[0] % 2], part="b")
                prev[0] = None
        return step, flush

    def norm_phase(self, tiles, wrow):
        wt = self.norm_prep(wrow)
        m = self.sc.mark()
        sets = self.norm_alloc()
        for j, t in enumerate(tiles):
            self.norm_tile(j, wt, sets[j % 2])
        self.sc.reset(m)

    def final_tile(self, j, t, wt, s):
        k = self.k
        self.rms_stats2(j, s)
        y = s["junk"]
        k.STT(y.ap, self.x_t[:, j, :], s["rstd"].ap[:, 0:1], wt.ap, ALU.mult, ALU.mult,
              r=[self.x_b[j], s["rstd"], wt], w=[y])
        k.DMA(self.yout[t * 128:(t + 1) * 128, :], y.ap, r=[y], is_output=True)

    def final(self, tiles):
        wt = self.norm_prep(self.W["final_norm_w"])
        m = self.sc.mark()
        sets = self.norm_alloc()
        for j, t in enumerate(tiles):
            self.final_tile(j, t, wt, sets[j % 2])
        self.sc.reset(m)

    def tgroups(self, ntiles):
        ng = (ntiles + 3) // 4
        base, extra = divmod(ntiles, ng)
        gs = []
        j = 0
        for i in range(ng):
            n = base + (1 if i < extra else 0)
            gs.append((j, n))
            j += n
        return gs

    def ffn_phase(self, l, tiles, after=None):
        k = self.k
        nt = len(tiles)
        if not self.ffn_norm_done:
            self.norm_phase(tiles, self.W["ffn_norm_w"][l])
        m = self.sc.mark()
        after_fn = after() if after is not None else None
        sgs = [self.sc.alloc([512], F32) for _ in range(2)]
        acts = [self.sc.alloc([2, 512], BF16) for _ in range(3)]
        nchunk = D_FF // 256
        wfi = self.W["w_ffn_in"][l].rearrange("(kt p) n -> p kt n", p=128)
        wfo = self.W["w_ffn_out"][l].rearrange("(s p) n -> p s n", p=128)
        cnt = 0
        pending = None

        def emit_y(act, wo, j0, n, last):
            for j in range(j0, j0 + n):
                yp = self.ps_big()
                cc = (j - j0) * 128
                for half in range(2):
                    for s in range(2):
                        k.MM(yp.ap[:, half * 512:(half + 1) * 512], act.ap[:, s, cc:cc + 128],
                             wo.ap[:, s, half * 512:(half + 1) * 512], start=(s == 0), stop=(s == 1),
                             r=[act, wo], w=[yp], inc=(half == 1 and s == 1))
                k.TT(self.x_t[:, j, :], self.x_t[:, j, :], yp.ap, ALU.add, r=[self.x_b[j], yp], w=[self.x_b[j]])
                if last and after_fn is not None:
                    after_fn(j)

        for c in range(nchunk):
            slot = c % 2
            base = slot * 12288
            wg = self.wa.view(base, [8, 256], BF16)
            wu = self.wa.view(base + 4096, [8, 256], BF16)
            wo = self.wa.view(base + 8192, [2, 1024], BF16)
            k.DMA(wg.ap, wfi[:, :, c * 256:(c + 1) * 256], w=[wg], iss="pool")
            k.DMA(wu.ap, wfi[:, :, D_FF + c * 256:D_FF + (c + 1) * 256], w=[wu], iss="pool")
            k.DMA(wo.ap, wfo[:, 2 * c:2 * c + 2, :], w=[wo], iss="pool")
            for (j0, n) in self.tgroups(nt):
                ntok = n * 128
                c0 = j0 * 128
                hdeps = [self.hT_b[j] for j in range(j0, j0 + n)]
                act = acts[cnt % 3]
                cnt += 1
                for s in range(2):
                    gp = self.ps_small()
                    up = self.ps_small()
                    for kt in range(8):
                        k.MM(gp.ap[:, :ntok], wg.ap[:, kt, s * 128:(s + 1) * 128], self.hT_t[:, kt, c0:c0 + ntok],
                             start=(kt == 0), stop=(kt == 7), r=[wg] + hdeps, w=[gp], inc=(kt == 7))
                    for kt in range(8):
                        k.MM(up.ap[:, :ntok], wu.ap[:, kt, s * 128:(s + 1) * 128], self.hT_t[:, kt, c0:c0 + ntok],
                             start=(kt == 0), stop=(kt == 7), r=[wu] + hdeps, w=[up], inc=(kt == 7))
                    sg = sgs[s]
                    k.ACT(sg.ap[:, :ntok], gp.ap[:, :ntok], AF.Silu, r=[gp], w=[sg])
                    k.TT(act.ap[:, s, :ntok], sg.ap[:, :ntok], up.ap[:, :ntok], ALU.mult, r=[sg, up], w=[act])
                if pending is not None:
                    emit_y(*pending)
                pending = (act, wo, j0, n, c == nchunk - 1)
        if pending is not None:
            emit_y(*pending)
        if getattr(self, "after_flush", None) is not None:
            self.after_flush()
            self.after_flush = None
        self.sc.reset(m)

    def merge_phase(self, l, tiles):
        k = self.k
        nt = len(tiles)
        m = self.sc.mark()
        mT = self.sc.alloc([8, nt * 128], BF16)
        sgs = [self.sc.alloc([512], F32) for _ in range(2)]
        tmps = [self.sc.alloc([512], F32) for _ in range(2)]
        accs = [self.sc.alloc([512], F32) for _ in range(2)]
        w_in = self.W["w_in"][l]
        wgate_src = w_in[:, C_GATE:C_GATE + 4096].rearrange("(kt p) (b d) -> p kt b d", p=128, b=4)
        wbr_src = self.W["w_branch"][l].rearrange("b (ct p) d -> p ct b d", p=128)
        cnt = 0
        for ds in range(8):
            base = (ds % 2) * 12288
            wg = self.wa.view(base, [8, 4, 128], BF16)
            wb = self.wa.view(base + 8192, [2, 4, 128], BF16)
            for b in range(4):
                k.DMA(wg.ap[:, :, b, :], wgate_src[:, :, b, ds * 128:(ds + 1) * 128], w=[wg], iss="pool")
                k.DMA(wb.ap[:, :, b, :], wbr_src[:, :, b, ds * 128:(ds + 1) * 128], w=[wb], iss="pool")
            for (j0, n) in self.tgroups(nt):
                ntok = n * 128
                c0 = j0 * 128
                hdeps = [self.hT_b[j] for j in range(j0, j0 + n)]
                acc = accs[cnt % 2]
                cnt += 1
                for b in range(4):
                    odeps = [self.oT_b[j * 4 + b] for j in range(j0, j0 + n)]
                    gp = self.ps_small()
                    for kt in range(8):
                        k.MM(gp.ap[:, :ntok], wg.ap[:, kt, b, :], self.hT_t[:, kt, c0:c0 + ntok],
                             start=(kt == 0), stop=(kt == 7), r=[wg] + hdeps, w=[gp], inc=(kt == 7))
                    up = self.ps_small()
                    for ct in range(2):
                        k.MM(up.ap[:, :ntok], wb.ap[:, ct, b, :], self.oT_t[:, 2 * b + ct, c0:c0 + ntok],
                             start=(ct == 0), stop=(ct == 1), r=[wb] + odeps, w=[up], inc=(ct == 1))
                    sg = sgs[b % 2]
                    k.ACT(sg.ap[:, :ntok], gp.ap[:, :ntok], AF.Tanh, r=[gp], w=[sg], scale=0.5)
                    if b == 0:
                        k.STT(acc.ap[:, :ntok], sg.ap[:, :ntok], 1.0, up.ap[:, :ntok], ALU.add, ALU.mult,
                              r=[sg, up], w=[acc])
                    else:
                        tmp = tmps[b % 2]
                        k.STT(tmp.ap[:, :ntok], sg.ap[:, :ntok], 1.0, up.ap[:, :ntok], ALU.add, ALU.mult,
                              r=[sg, up], w=[tmp])
                        k.TT(acc.ap[:, :ntok], acc.ap[:, :ntok], tmp.ap[:, :ntok], ALU.add, r=[acc, tmp], w=[acc])
                k.ACT(mT.ap[:, ds, c0:c0 + ntok], acc.ap[:, :ntok], AF.Copy, r=[acc], w=[mT], scale=0.5)
        wt_n = self.norm_prep(self.W["ffn_norm_w"][l])
        nstep, nflush = self.lagged_norm(wt_n, self.norm_alloc())
        wo = self.wa.view(0, [8, 1024], BF16)
        k.DMA(wo.ap, self.W["w_out"][l].rearrange("(kt p) n -> p kt n", p=128), w=[wo], iss="pool")
        for j in range(nt):
            yp = self.ps_big()
            for half in range(2):
                for kt in range(8):
                    k.MM(yp.ap[:, half * 512:(half + 1) * 512], mT.ap[:, kt, j * 128:(j + 1) * 128],
                         wo.ap[:, kt, half * 512:(half + 1) * 512], start=(kt == 0), stop=(kt == 7),
                         r=[mT, wo], w=[yp], inc=(half == 1 and kt == 7))
            k.TT(self.x_t[:, j, :], self.x_t[:, j, :], yp.ap, ALU.add, r=[self.x_b[j], yp], w=[self.x_b[j]])
            nstep(j)
        nflush()
        self.ffn_norm_done = True
        self.sc.reset(m)

    def layer(self, l, tiles, pi):
        k = self.k
        ph = os.environ.get("PHASES", "nlmgf")
        self.ffn_norm_done = False
        if "n" in ph and not getattr(self, "attn_norm_done", False):
            self.norm_phase(tiles, self.W["attn_norm_w"][l])
        self.attn_norm_done = False
        if "l" in ph:
            self.load_layer_params(l)
        if "m" not in ph:
            if "g" in ph:
                self.merge_phase(l, tiles)
            if "f" in ph:
                self.ffn_phase(l, tiles)
            return
        jts = list(enumerate(tiles))
        pj = [(j, t) for (j, t) in jts if t < self.npt]
        sj = [(j, t) for (j, t) in jts if t >= self.npt]
        for mi, name in enumerate(("hg", "gla", "rw", "ret")):
            if name not in self.mixers:
                for j in range(len(tiles)):
                    k.TS(self.oT_t[:, 2 * mi:2 * mi + 2, j * 128:(j + 1) * 128], self.hT_t[:, 0:2, j * 128:(j + 1) * 128],
                         0.0, ALU.mult, r=[self.hT_b[j]], w=[self.oT_b[j * 4 + mi]])
        two_stream = all(n in self.mixers for n in ("hg", "gla", "rw", "ret")) and len(pj) > 0 \
            and os.environ.get("NO_TWO_STREAM") is None
        if two_stream:
            m0 = self.sc.mark()
            P = self.k.P
            P.cur = "A"
            for _ in self.mixer_rw(l, pj, 2, wslot=1):
                pass
            P.cur = "B"
            self.sc.top = self.rw_top
            for mname, mi_ in (("hg", 0), ("gla", 1), ("ret", 3)):
                for _ in getattr(self, "mixer_" + mname)(l, pj, mi_, nsets=1, wslot=0):
                    pass
            P.cur = None
            P.merge_streams()
            self.sc.reset(m0)
            rest = sj
        else:
            rest = jts
        if rest:
            for mi, name in enumerate(("hg", "gla", "rw", "ret")):
                if name in self.mixers:
                    for _ in getattr(self, "mixer_" + name)(l, rest, mi, nsets=(1 if two_stream else 2)):
                        pass
        self.merge_phase(l, tiles)

        def after():
            last = (l == self.depth - 1)
            wt = self.norm_prep(self.W["final_norm_w"] if last else self.W["attn_norm_w"][l + 1])
            sets = self.norm_alloc()
            if last:
                self.final_done = True
                return lambda j: self.final_tile(j, tiles[j], wt, sets[j % 2])
            self.attn_norm_done = True
            step, flush = self.lagged_norm(wt, sets)
            self.after_flush = flush
            return step
        self.ffn_phase(l, tiles, after=after)


    def load_layer_params(self, l):
        k = self.k
        W = self.W
        pb = {}
        pb["mu"] = self.load_pb(PB_MU, W["rw_mu"][l])
        k.TS(pb["mu"].ap, pb["mu"].ap, -1.0, ALU.mult, 1.0, ALU.add, r=[pb["mu"]], w=[pb["mu"]])
        pb["hgn"] = self.load_pb(PB_HGN, W["hg_norm_w"][l])
        pb["gba"] = self.load_pb(PB_GBA, W["gla_ba"][l])
        pb["gln"] = self.load_pb(PB_GLN, W["gla_norm_w"][l])
        for nm, off in (("rw_w0", PB_W0), ("rw_a0", PB_A0), ("rw_kk", PB_KK), ("rw_ka", PB_KA), ("rw_rk", PB_RK),
                        ("rw_ln_w", PB_LNW), ("rw_ln_b", PB_LNB)):
            pb[nm] = self.load_pb(off, W[nm][l])
        self.pb = pb
        lbc = T(self.lbc_t[:], self.lbc_b.all)
        if l == 0:
            k.MEMSET(self.lbc_t[:].rearrange("p a b -> p (a b)"), 0.5, w=[lbc])
        else:
            assert DEPTH == 2
            l0 = self.load_pb(PB_LB, W["hg_lb_logits"][0])
            l1 = self.load_pb(PB_LB + 256, W["hg_lb_logits"][1])
            k.TT(l0.ap, l1.ap, l0.ap, ALU.subtract, r=[l0, l1], w=[l0])
            k.ACT(l0.ap, l0.ap, AF.Tanh, r=[l0], w=[l0], scale=0.5)
            k.TS(self.lbc_t[:, 0, :], l0.ap, 0.25, ALU.mult, 0.75, ALU.add, r=[l0], w=[lbc])
            k.TS(self.lbc_t[:, 1, :], l0.ap, -0.25, ALU.mult, 0.25, ALU.add, r=[l0], w=[lbc])
        rwl = T(self.rwl_t[:], self.rwl_b.all)
        k.DMA(self.rwl_t[0:64, 0, :], W["rw_w2"][l], w=[rwl], iss="pool")
        k.DMA(self.rwl_t[64:128, 1, :], W["rw_a2"][l], w=[rwl], iss="pool")
        k.DMA(self.rwl_t[:, 2, :], W["rw_g2"][l], w=[rwl], iss="pool")
        k.TS(self.gwa_t[:], self.ident.ap[0:32, :], 0.0, ALU.mult, r=[self.ident], w=[self.gwa_b])
        k.DMA(self.gwa_t[0:16, :], W["gla_wa2"][l], w=[self.gwa_b], iss="pool")

    def load_mixer_w(self, l, mi, c0, ncols):
        wv = self.wa.view((mi % 2) * 16384, [8, 1024], BF16)
        src = self.W["w_in"][l][:, c0:c0 + ncols].rearrange("(kt p) n -> p kt n", p=128)
        self.k.DMA(wv.ap[:, :, 0:ncols], src, w=[wv], iss="pool")
        return wv

    def project(self, wv, j, ncols):
        k = self.k
        pp = self.ps_big()
        c = 0
        while c < ncols:
            n = min(512, ncols - c)
            for kt in range(8):
                k.MM(pp.ap[:, c:c + n], self.hT_t[:, kt, j * 128:(j + 1) * 128], wv.ap[:, kt, c:c + n],
                     start=(kt == 0), stop=(kt == 7), r=[self.hT_b[j], wv], w=[pp], inc=(kt == 7))
            c += n
        return pp

    def gla_sets(self, n):
        sets = []
        for _ in range(n):
            d = {}
            for nm in ("a0", "a1", "a2", "a3", "a4", "a5", "g", "Ep", "Em", "gsb", "tg"):
                d[nm] = self.sc.alloc([256], F32)
            d["dd"] = self.sc.alloc([2, 32], F32)
            d["KVd"] = self.sc.alloc([2, 2, 64], F32)
            d["Smid"] = self.sc.alloc([2, 64], F32)
            d["tmp"] = self.sc.alloc([2, 64], F32)
            d["st4"] = self.sc.alloc([8], F32)
            for nm in ("qt", "kt", "vbf", "ob"):
                d[nm] = self.sc.alloc([256], BF16)
            d["qkT"] = self.sc.alloc([4, 128], BF16)
            d["AT"] = self.sc.alloc([4, 128], BF16)
            sets.append(d)
        return sets

    def S_view(self, l, mi):
        return T(self.S_t[:, l * 4 + mi, :, :], self.S_b[l * 4 + mi])

    def gla_A(self, s, kind, q, kk, g, S, smidB, cidx, samp=None):
        k = self.k
        nind = 4 if kind == "p" else 32
        cst = [T(self.c128_t[:], self.c128_b.all)]
        bp = self.ps_small()
        k.MM(bp.ap[:, 0:256], self.cm(kind, "mcum"), g.ap, r=cst + [g], w=[bp], inc=False)
        for jj in range(2):
            k.MM(bp.ap[:, 256 + jj * nind:256 + (jj + 1) * nind], g.ap[:, jj * 128:(jj + 1) * 128],
                 self.cm(kind, "ind"), r=cst + [g], w=[bp], inc=(jj == 1))
        yield
        k.ACT(s["Ep"].ap, bp.ap[:, 0:256], AF.Exp, r=[bp], w=[s["Ep"]])
        k.ACT(s["Em"].ap, bp.ap[:, 0:256], AF.Exp, r=[bp], w=[s["Em"]], scale=-1.0)
        dd = s["dd"]
        k.ACT(dd.ap[:, :, 0:nind], bp.ap[:, 256:256 + 2 * nind].rearrange("p (a b) -> p a b", a=2), AF.Exp,
              r=[bp], w=[dd])
        k.TT(s["qt"].ap, q.ap, s["Ep"].ap, ALU.mult, r=[q, s["Ep"]], w=[s["qt"]])
        k.TT(s["kt"].ap, kk.ap, s["Em"].ap, ALU.mult, r=[kk, s["Em"]], w=[s["kt"]])
        yield
        tb = self.ps_tb()
        for i, src in enumerate((s["qt"], s["qt"], s["kt"], s["kt"])):
            k.TR(tb.ap[:, i * 128:(i + 1) * 128], src.ap[:, (i % 2) * 128:(i % 2 + 1) * 128], self.ident.ap,
                 r=[src, self.ident], w=[tb], inc=(i == 3))
        qkT = s["qkT"]
        k.CP(qkT.ap, tb.ap[:, 0:512].rearrange("p (a b) -> p a b", a=4), r=[tb], w=[qkT], eng="act")
        yield
        scps = [self.ps_small(), self.ps_small()]
        for hl in range(2):
            for jj in range(2):
                k.MM(scps[hl].ap[:, jj * 128:(jj + 1) * 128], qkT.ap[64 * hl:64 * hl + 64, 2 + jj, :],
                     qkT.ap[64 * hl:64 * hl + 64, jj, :], r=[qkT], w=[scps[hl]], inc=(jj == 1))
        mask = self.cm(kind, "ti")
        AT4 = s["AT"].ap.rearrange("p (j h) t -> p j h t", j=2)
        for hl in range(2):
            k.TT(AT4[:, :, hl, :], scps[hl].ap[:, 0:256].rearrange("p (a b) -> p a b", a=2),
                 mask.unsqueeze(1).to_broadcast([128, 2, 128]), ALU.mult, r=[scps[hl]] + cst, w=[s["AT"]])
        kt, vbf = s["kt"], s["vbf"]
        yield
        if kind == "p":
            KVd = s["KVd"]
            kvps = [self.ps_small(), self.ps_small()]
            for c in range(2):
                for jj in range(2):
                    k.MM(kvps[c].ap[:, jj * 128:(jj + 1) * 128],
                         kt.ap[64 * c:64 * c + 64, jj * 128:(jj + 1) * 128],
                         vbf.ap[64 * c:64 * c + 64, jj * 128:(jj + 1) * 128], r=[kt, vbf], w=[kvps[c]],
                         inc=(jj == 1))
            for c in range(2):
                kv2 = kvps[c].ap[:, 0:256].rearrange("p (a b) -> p a b", a=2)
                k.CP(KVd.ap[0:64, c], kv2[0:64, :, 0:64], r=[kvps[c]], w=[KVd])
                k.CP(KVd.ap[64:128, c], kv2[64:128, :, 64:128], r=[kvps[c]], w=[KVd], eng="act")
            yield
            for c in range(2):
                d1 = dd.ap[:, :, 2 * c:2 * c + 1].to_broadcast([128, 2, 64])
                d2 = dd.ap[:, :, 2 * c + 1:2 * c + 2].to_broadcast([128, 2, 64])
                k.TT(s["Smid"].ap, S.ap, d1, ALU.mult, r=[S, dd], w=[s["Smid"]])
                k.CP(smidB.ap[:, cidx + c], s["Smid"].ap, r=[s["Smid"]], w=[smidB], eng="act")
                k.TT(s["tmp"].ap, s["Smid"].ap, KVd.ap[:, c], ALU.add, r=[s["Smid"], KVd], w=[s["tmp"]])
                k.TT(S.ap, s["tmp"].ap, d2, ALU.mult, r=[s["tmp"], dd], w=[S])
        else:
            S0, KVd, SB = samp["S0"], samp["KVd"], samp["SmidB"]
            ddv = dd.ap.rearrange("p j (q t) -> p q j t", t=2)
            ktm = samp["ktm"]
            for hf in range(2):
                q0 = 8 * hf
                d1 = ddv[:, q0:q0 + 8, :, 0:1].to_broadcast([128, 8, 2, 64])
                d2 = ddv[:, q0:q0 + 8, :, 1:2].to_broadcast([128, 8, 2, 64])
                samp["load"](hf)
                k.TT(S0.ap, S0.ap, d1, ALU.mult, r=[S0, dd], w=[S0])
                k.CP(SB.ap[:, q0:q0 + 8], S0.ap, r=[S0], w=[SB], eng="act")
                for q2 in range(4):
                    kvp = self.ps_small()
                    for u in range(2):
                        sq = q0 + 2 * q2 + u
                        km = ktm[sq % len(ktm)]
                        k.TS(km.ap, kt.ap, self.c128_t[:, C_ROWS + sq:C_ROWS + sq + 1], ALU.mult, r=[kt] + cst,
                             w=[km])
                        for jj in range(2):
                            k.MM(kvp.ap[:, (u * 2 + jj) * 128:(u * 2 + jj + 1) * 128],
                                 km.ap[:, jj * 128:(jj + 1) * 128], vbf.ap[:, jj * 128:(jj + 1) * 128],
                                 r=[km, vbf], w=[kvp], inc=(u == 1 and jj == 1))
                    kv4 = kvp.ap.rearrange("p (a b) -> p a b", a=4)
                    kd4 = KVd.ap[:, 2 * q2:2 * q2 + 2].rearrange("p a b c -> p (a b) c")
                    k.CP(kd4[0:64], kv4[0:64, :, 0:64], r=[kvp], w=[KVd])
                    k.CP(kd4[64:128], kv4[64:128, :, 64:128], r=[kvp], w=[KVd], eng="act")
                k.TT(KVd.ap, KVd.ap, S0.ap, ALU.add, r=[KVd, S0], w=[KVd])
                k.TT(KVd.ap, KVd.ap, d2, ALU.mult, r=[KVd, dd], w=[KVd])
                samp["store"](hf)

    def gla_B(self, s, kind, smidB, cidx, samp=None):
        k = self.k
        op_ = self.ps_small()
        AT, vbf, qkT = s["AT"], s["vbf"], s["qkT"]
        if kind == "p":
            for c in range(2):
                for h in range(4):
                    jj, hl = h // 2, h % 2
                    out = op_.ap[64 * c:64 * c + 64, h * 64:(h + 1) * 64]
                    k.MM(out, AT.ap[:, h, 64 * c:64 * c + 64], vbf.ap[:, h * 64:(h + 1) * 64], start=True, stop=False,
                         r=[AT, vbf], w=[op_], inc=False)
                    k.MM(out, qkT.ap[64 * hl:64 * hl + 64, jj, 64 * c:64 * c + 64],
                         smidB.ap[64 * hl:64 * hl + 64, cidx + c, jj, :], start=False, stop=True,
                         r=[qkT, smidB], w=[op_], inc=(c == 1 and h == 3))
        else:
            SB, qtm = samp["SmidB"], samp["qtm"]
            cmd = [T(self.cm_t[:], self.cm_b.all)]
            n = 0
            for h in range(4):
                jj, hl = h // 2, h % 2
                out = op_.ap[:, h * 64:(h + 1) * 64]
                k.MM(out, AT.ap[:, h, :], vbf.ap[:, h * 64:(h + 1) * 64], start=True, stop=False,
                     r=[AT, vbf], w=[op_], inc=False)
                for sq in range(16):
                    qm = qtm[n % len(qtm)]
                    n += 1
                    k.TT(qm.ap[64 * hl:64 * hl + 64, :], qkT.ap[64 * hl:64 * hl + 64, jj, :],
                         self.cm_t[64 * hl:64 * hl + 64, sq, :], ALU.mult, r=[qkT] + cmd, w=[qm])
                    k.MM(out, qm.ap[64 * hl:64 * hl + 64, :], SB.ap[64 * hl:64 * hl + 64, sq, jj, :], start=False,
                         stop=(sq == 15), r=[qm, SB], w=[op_], inc=True)
        return op_

    def samp_alloc(self, name, l, dk):
        k = self.k
        d = dict(S0=self.sc.alloc([8, 2, 64], F32), KVd=self.sc.alloc([8, 2, 64], F32),
                 SmidB=self.sc.alloc([16, 2, 64], BF16), ktm=[self.sc.alloc([256], BF16) for _ in range(3)],
                 qtm=[self.sc.alloc([128], BF16) for _ in range(4)])
        S0, Sn = d["S0"], d["KVd"]
        src = self.st_in[name][l]
        dst = self.s_out[name][l]

        def load(hf):
            if dk < 64:
                k.MEMSET(S0.ap.rearrange("p a b c -> p (a b c)"), 0.0, w=[S0])
            for hl in range(2):
                for jj in range(2):
                    k.DMA(S0.ap[64 * hl:64 * hl + dk, :, jj, :],
                          src[8 * hf:8 * hf + 8, 2 * jj + hl].rearrange("s k v -> k s v"), w=[S0])

        def store(hf):
            for hl in range(2):
                for jj in range(2):
                    k.DMA(dst[8 * hf:8 * hf + 8, 2 * jj + hl].rearrange("s k v -> k s v"),
                          Sn.ap[64 * hl:64 * hl + dk, :, jj, :], r=[Sn], is_output=True)
        d["load"], d["store"] = load, store
        return d

    def store_state_p(self, name, l, S, dk):
        k = self.k
        dst = self.p_out[name][l]
        for hl in range(2):
            for jj in range(2):
                k.DMA(dst[2 * jj + hl], S.ap[64 * hl:64 * hl + dk, jj, :], r=[S], is_output=True)

    def post_rms(self, s, o_ps, normw, j, mi):
        k = self.k
        osb, sq, on, tg = s["a0"], s["a1"], s["a2"], s["tg"]
        st = s["st4"]
        gate_ap, gate_dep = s["gsb"].ap, s["gsb"]
        k.CP(osb.ap, o_ps.ap[:, 0:256], r=[o_ps], w=[osb], eng="act")
        k.TT(sq.ap, osb.ap, osb.ap, ALU.mult, r=[osb], w=[sq])
        k.RSUM(st.ap[:, 0:4], sq.ap.rearrange("p (a b) -> p a b", a=4), r=[sq], w=[st])
        k.ACT(st.ap[:, 4:8], st.ap[:, 0:4], AF.Ln, r=[st, self.cst_b], w=[st], bias=self.cst(self.CST["EPS64"]),
              scale=1.0 / 64)
        k.ACT(st.ap[:, 0:4], st.ap[:, 4:8], AF.Exp, r=[st, self.cst_b], w=[st], bias=self.cst(self.CST["LNH"]),
              scale=-0.5)
        yield
        k.TT(on.ap.rearrange("p (a b) -> p a b", a=4), osb.ap.rearrange("p (a b) -> p a b", a=4),
             st.ap[:, 0:4].unsqueeze(2).to_broadcast([128, 4, 64]), ALU.mult, r=[osb, st], w=[on])
        if normw is not None:
            k.TT(on.ap.rearrange("p (a b) -> p a b", a=4), on.ap.rearrange("p (a b) -> p a b", a=4),
                 normw.ap.unsqueeze(1).to_broadcast([128, 4, 64]), ALU.mult, r=[on, normw], w=[on])
        k.STT(tg.ap, tg.ap, 1.0, gate_ap, ALU.add, ALU.mult, r=[tg, gate_dep], w=[tg])
        k.TT(s["ob"].ap, on.ap, tg.ap, ALU.mult, r=[on, tg], w=[s["ob"]])
        yield
        self.to_oT(s["ob"], j, mi)

    def to_oT(self, ob, j, mi):
        k = self.k
        tb = self.ps_tb()
        for i in range(2):
            k.TR(tb.ap[:, i * 128:(i + 1) * 128], ob.ap[:, i * 128:(i + 1) * 128], self.ident.ap,
                 r=[ob, self.ident], w=[tb], inc=(i == 1))
        k.CP(self.oT_t[:, 2 * mi:2 * mi + 2, j * 128:(j + 1) * 128], tb.ap[:, 0:256].rearrange("p (a b) -> p a b", a=2),
             r=[tb], w=[self.oT_b[j * 4 + mi]], eng="act")

    def gla_stream(self, name, l, jts, mi, c0, ncols, dk, pre, post, init=None, nsets=2, wslot=None):
        k = self.k
        m = self.sc.mark()
        wv = self.load_mixer_w(l, mi if wslot is None else wslot, c0, ncols)
        sets = self.gla_sets(nsets)
        if init is not None:
            init(sets)
        tiles = [t for (_, t) in jts]
        smidB = self.sc.alloc([2 * nsets, 2, 64], BF16)
        has_s = any(t >= self.npt for t in tiles)
        samp = None
        if has_s:
            samp = self.samp_alloc(name, l, dk)
        S = self.S_view(l, mi)

        def tile_gen(j, t, s, si):
            kind = self.tile_kind(t)
            pp = self.project(wv, j, ncols)
            yield
            q, kk, g = pre(s, pp, j, t, l, wv)
            yield
            yield from self.gla_A(s, kind, q, kk, g, S, smidB, 2 * si, samp if kind == "s" else None)
            if kind == "p" and t == self.npt - 1:
                self.store_state_p(name, l, S, dk)
            yield
            o_ps = self.gla_B(s, kind, smidB, 2 * si, samp if kind == "s" else None)
            yield
            yield from post(s, o_ps, pp, j, mi, l)

        active = []
        nxt = 0
        free = list(range(nsets))
        while nxt < len(jts) or active:
            if nxt < len(jts) and free:
                si = free.pop(0)
                active.append((tile_gen(jts[nxt][0], jts[nxt][1], sets[si], si), si))
                nxt += 1
            still = []
            for gen, si in active:
                try:
                    next(gen)
                    still.append((gen, si))
                except StopIteration:
                    free.append(si)
            active = still
            yield
        self.sc.reset(m)

    def mixer_hg(self, l, jts, mi, nsets=2, wslot=None):
        k = self.k
        lbc = T(self.lbc_t[:], self.lbc_b.all)

        def pre(s, pp, j, t, l, wv):
            th, fg, kf, qf, tq = s["a0"], s["a1"], s["a2"], s["a3"], s["a4"]
            P_ = pp.ap
            k.ACT(th.ap, P_[:, 256:512], AF.Tanh, r=[pp], w=[th], scale=0.5)
            k.ACT(tq.ap, P_[:, 0:256], AF.Tanh, r=[pp], w=[tq], scale=0.5)
            k.ACT(s["tg"].ap, P_[:, 768:1024], AF.Tanh, r=[pp], w=[s["tg"]], scale=0.5)
            k.TT(fg.ap, th.ap, self.lbc_t[:, 1, :], ALU.mult, r=[th, lbc], w=[fg])
            k.TT(fg.ap, fg.ap, self.lbc_t[:, 0, :], ALU.add, r=[fg, lbc], w=[fg])
            k.TS(kf.ap, fg.ap, -1.0, ALU.mult, 1.0, ALU.add, r=[fg], w=[kf])
            k.TS(fg.ap, fg.ap, 1e-30, ALU.max, r=[fg], w=[fg])
            k.ACT(s["g"].ap, fg.ap, AF.Ln, r=[fg], w=[s["g"]])
            k.STT(qf.ap, tq.ap, 1.0, P_[:, 0:256], ALU.add, ALU.mult, r=[tq, pp], w=[qf])
            k.TS(qf.ap, qf.ap, 0.5, ALU.mult, r=[qf], w=[qf])
            k.CP(s["vbf"].ap, P_[:, 512:768], r=[pp], w=[s["vbf"]], eng="act")
            k.CP(s["gsb"].ap, P_[:, 768:1024], r=[pp], w=[s["gsb"]], eng="act")
            return qf, kf, s["g"]

        def post(s, o_ps, pp, j, mi, l):
            yield from self.post_rms(s, o_ps, self.pb["hgn"], j, mi)

        return self.gla_stream("hg", l, jts, mi, C_HG, 1024, 64, pre, post, nsets=nsets, wslot=wslot)


    def mixer_ret(self, l, jts, mi, nsets=2, wslot=None):
        k = self.k
        gc = T(self.gconst_t[:], self.gconst_b.all)

        def pre(s, pp, j, t, l, wv):
            par = j % 2
            rt = T(self.rot_t[:, par, :], self.rot_b[par])
            k.DMA(rt.ap, self.rot_d[t], w=[rt])
            cosb = self.rot_t[:, par, 0:32].unsqueeze(1).to_broadcast([128, 8, 32])
            sinb = self.rot_t[:, par, 32:64].unsqueeze(1).to_broadcast([128, 8, 32])
            v8 = lambda ap: ap.rearrange("p (h i) -> p h i", h=8)
            v4 = lambda ap: ap.rearrange("p (h a i) -> p h a i", h=4, a=2)
            outs = []
            for (c0, t1, t2, o, scale) in ((0, s["a0"], s["a1"], s["a2"], 1.0), (256, s["a3"], s["a4"], s["a5"], 0.125)):
                src = v8(pp.ap[:, c0:c0 + 256])
                k.STT(v8(t1.ap), src, scale, cosb, ALU.mult, ALU.mult, r=[pp, rt], w=[t1])
                k.STT(v8(t2.ap), src, scale, sinb, ALU.mult, ALU.mult, r=[pp, rt], w=[t2])
                k.TT(v4(o.ap)[:, :, 0, :], v4(t1.ap)[:, :, 0, :], v4(t2.ap)[:, :, 1, :], ALU.subtract,
                     r=[t1, t2], w=[o])
                k.TT(v4(o.ap)[:, :, 1, :], v4(t2.ap)[:, :, 0, :], v4(t1.ap)[:, :, 1, :], ALU.add,
                     r=[t1, t2], w=[o])
                outs.append(o)
            k.CP(s["vbf"].ap, pp.ap[:, 512:768], r=[pp], w=[s["vbf"]], eng="act")
            k.CP(s["gsb"].ap, pp.ap[:, 768:1024], r=[pp], w=[s["gsb"]], eng="act")
            k.ACT(s["tg"].ap, pp.ap[:, 768:1024], AF.Tanh, r=[pp], w=[s["tg"]], scale=0.5)
            return outs[0], outs[1], gc

        def post(s, o_ps, pp, j, mi, l):
            yield from self.post_rms(s, o_ps, None, j, mi)

        return self.gla_stream("ret", l, jts, mi, C_RET, 1024, 64, pre, post, nsets=nsets, wslot=wslot)

    def mixer_gla(self, l, jts, mi, nsets=2, wslot=None):
        k = self.k

        def init(sets):
            for s in sets:
                for nm in ("a4", "a5", "g"):
                    k.MEMSET(s[nm].ap, 0.0, w=[s[nm]])

        def pre(s, pp, j, t, l, wv):
            v3 = lambda ap: ap.rearrange("p (h i) -> p h i", h=4)
            k.ACT(s["tg"].ap, pp.ap[:, 528:784], AF.Tanh, r=[pp], w=[s["tg"]], scale=0.5)
            ap_ = self.ps_small()
            for kt in range(8):
                k.MM(ap_.ap[0:32, 0:128], wv.ap[:, kt, 512:544], self.hT_t[:, kt, j * 128:(j + 1) * 128],
                     start=(kt == 0), stop=(kt == 7), r=[wv, self.hT_b[j]], w=[ap_], inc=(kt == 7))
            adT = s["ob"]
            k.CP(adT.ap[0:32, 0:128], ap_.ap[0:32, 0:128], r=[ap_], w=[adT])
            zp = self.ps_small()
            k.MM(zp.ap[:, 0:128], adT.ap[0:32, 0:128], self.gwa_t[:], r=[adT, self.gwa_b], w=[zp])
            z = s["a0"]
            k.TT(z.ap[:, 0:128], zp.ap[:, 0:128], self.pb["gba"].ap, ALU.add, r=[zp, self.pb["gba"]], w=[z])
            k.ACT(z.ap[:, 0:128], z.ap[:, 0:128], AF.Exp, r=[z], w=[z], scale=-1.0)
            k.ACT(z.ap[:, 0:128], z.ap[:, 0:128], AF.Ln, r=[z, self.cst_b], w=[z], bias=self.cst(self.CST["ONE"]))
            k.TS(v3(s["g"].ap)[:, :, 0:32], v3(z.ap[:, 0:128]), -1.0 / 16.0, ALU.mult, r=[z], w=[s["g"]])
            k.TS(v3(s["a4"].ap)[:, :, 0:32], v3(pp.ap[:, 0:128]), 32.0 ** -0.5, ALU.mult, r=[pp], w=[s["a4"]])
            k.CP(v3(s["a5"].ap)[:, :, 0:32], v3(pp.ap[:, 128:256]), r=[pp], w=[s["a5"]])
            k.CP(s["vbf"].ap, pp.ap[:, 256:512], r=[pp], w=[s["vbf"]], eng="act")
            k.CP(s["gsb"].ap, pp.ap[:, 528:784], r=[pp], w=[s["gsb"]], eng="act")
            return s["a4"], s["a5"], s["g"]

        def post(s, o_ps, pp, j, mi, l):
            yield from self.post_rms(s, o_ps, self.pb["gln"], j, mi)

        return self.gla_stream("gla", l, jts, mi, C_GLA, 784, 32, pre, post, init=init, nsets=nsets, wslot=wslot)


    def mixer_rw(self, l, jts, mi, nsets=1, wslot=None):
        k = self.k
        m = self.sc.mark()
        wv = self.load_mixer_w(l, mi if wslot is None else wslot, C_RW, 1024)
        tiles = [t for (_, t) in jts]
        A = self.sc.alloc
        c128 = self.c128_t
        cst = [T(self.c128_t[:], self.c128_b.all)]
        pb = self.pb
        rw_sb = A([1024], F32)
        f = {nm: A([256], F32) for nm in ("a", "lw", "gsb", "kk", "kp", "bv", "bon", "t0", "t1", "NTAVf")}
        f["t2"] = f["NTAVf"]
        st = A([16], F32)
        X = A([256], BF16)
        XT = A([2, 128], BF16)
        tb16 = {nm: A([256], BF16) for nm in ("rt", "kkt", "vbf", "AV", "NTAVb", "NU", "ob")}
        KB3 = A([3, 256], BF16)
        T8 = A([8, 128], BF16)
        RT2 = A([2, 2, 128], BF16)
        SM = A([4, 4, 128], BF16)
        Pa, Pb, Qa, Qb, Xc = [A([4, 128], BF16) for _ in range(5)]
        mask4 = A([4, 128], BF16)
        NG = A([2, 2, 128], BF16)
        Hd = A([2, 2, 64], F32)
        Smid = A([2, 64], F32)
        dd_buf = A([2, 32], F32)
        tmp = A([2, 64], F32)
        SmidB = A([2, 2, 64], BF16)
        SmidX = A([2, 2, 128], BF16)
        has_s = any(t >= self.npt for t in tiles)
        if has_s:
            sp = dict(S0=A([8, 2, 64], F32), Sn=A([8, 2, 64], F32), KBm=[A([3, 256], BF16) for _ in range(2)],
                      NGs=[A([2, 128], BF16) for _ in range(2)], SmB=[A([2, 64], BF16) for _ in range(2)],
                      SmX=[A([2, 128], BF16) for _ in range(2)], RTm=[A([2, 2, 128], BF16) for _ in range(2)])
        zsrc = self.cm_t[:, 0:4, :].rearrange("p a b -> p (a b)")
        zr = [T(self.cm_t[:], self.cm_b.all)]
        k.TS(SmidX.ap.rearrange("p a b c -> p (a b c)"), zsrc, 0.0, ALU.mult, r=zr, w=[SmidX])
        if has_s:
            for b_ in sp["SmX"]:
                k.TS(b_.ap.rearrange("p a b -> p (a b)"), zsrc[:, 0:256], 0.0, ALU.mult, r=zr, w=[b_])
        mu = pb["mu"]
        S = self.S_view(l, mi)
        rwp = T(self.rwp_t[:, l, :], self.rwp_b[l])
        c1 = -0.5 * math.exp(-0.5)
        v3 = lambda ap: ap.rearrange("p (a b) -> p a b", a=4)
        cur_kind = [None]

        def set_masks(kind):
            if cur_kind[0] == kind:
                return
            cur_kind[0] = kind
            ts_, ti_ = self.cm(kind, "ts"), self.cm(kind, "ti")
            k.CP(mask4.ap[:, 0, :], ts_, r=cst, w=[mask4])
            k.CP(mask4.ap[:, 1, :], ti_, r=cst, w=[mask4])
            k.TS(mask4.ap[:, 2, :], ts_, -1.0, ALU.mult, r=cst, w=[mask4])
            k.CP(mask4.ap[:, 3, :], ti_, r=cst, w=[mask4])

        for j, t in jts:
            kind = self.tile_kind(t)
            nind = 4 if kind == "p" else 32
            set_masks(kind)
            pp = self.project(wv, j, 1024)
            k.CP(rw_sb.ap, pp.ap, r=[pp], w=[rw_sb], eng="act")
            if kind == "p" and t == self.npt - 1:
                k.DMA(self.p_shift[l:l + 1, :], rw_sb.ap[127:128, :], r=[rw_sb], is_output=True)
            if kind == "s":
                for q in range(16):
                    k.DMA(self.s_shift[l, q:q + 1, :], rw_sb.ap[8 * q + 7:8 * q + 8, :], r=[rw_sb], is_output=True)
                k.DMA(self.rwp_t[0:16, l, :], self.st_shift[l], w=[rwp])
            yield
            pv = self.ps_big()
            shm = C_SHP if kind == "p" else C_SHS
            carry = (kind == "s") or (t > 0)
            for half in range(2):
                hs = slice(half * 512, (half + 1) * 512)
                k.MM(pv.ap[:, hs], c128[:, shm:shm + 128], rw_sb.ap[:, hs], start=True, stop=not carry,
                     r=cst + [rw_sb], w=[pv], inc=(not carry and half == 1))
                if carry and kind == "p":
                    k.MM(pv.ap[:, hs], c128[:, C_CAR:C_CAR + 128], self.rwp_t[:, l, hs], start=False, stop=True,
                         r=cst + [rwp], w=[pv], inc=(half == 1))
                elif carry:
                    k.MM(pv.ap[:, hs], c128[0:32, C_SEL:C_SEL + 128], self.rwp_t[0:32, l, hs], start=False,
                         stop=True, r=cst + [rwp], w=[pv], inc=(half == 1))
            if kind == "p":
                k.CP(self.rwp_t[64:128, l, :], rw_sb.ap[64:128, :], r=[rw_sb], w=[rwp], eng="act")
            k.TT(rw_sb.ap, rw_sb.ap, pv.ap, ALU.subtract, r=[rw_sb, pv], w=[rw_sb])
            k.TT(rw_sb.ap, rw_sb.ap, mu.ap, ALU.mult, r=[rw_sb, mu], w=[rw_sb])
            k.TT(rw_sb.ap, rw_sb.ap, pv.ap, ALU.add, r=[rw_sb, pv], w=[rw_sb])
            mx = rw_sb.ap
            r_, wd_, kx_, v_, ad_, gd_ = mx[:, 0:256], mx[:, 256:320], mx[:, 320:576], mx[:, 576:832], mx[:, 832:896], mx[:, 896:1024]
            yield
            k.ACT(X.ap[:, 0:64], wd_, AF.Tanh, r=[rw_sb], w=[X])
            k.CP(X.ap[:, 64:128], ad_, r=[rw_sb], w=[X])
            k.ACT(f["t0"].ap[:, 0:128], gd_, AF.Tanh, r=[rw_sb], w=[f["t0"]], scale=0.5)
            k.TS(X.ap[:, 128:256], f["t0"].ap[:, 0:128], 0.5, ALU.mult, 0.5, ALU.add, r=[f["t0"]], w=[X])
            tb = self.ps_tb()
            for i in range(2):
                k.TR(tb.ap[:, i * 128:(i + 1) * 128], X.ap[:, i * 128:(i + 1) * 128], self.ident.ap,
                     r=[X, self.ident], w=[tb], inc=(i == 1))
            k.CP(XT.ap, tb.ap[:, 0:256].rearrange("p (a b) -> p a b", a=2), r=[tb], w=[XT], eng="act")
            rwl = T(self.rwl_t[:], self.rwl_b.all)
            lwb = self.ps_small()
            lab = self.ps_small()
            k.MM(lwb.ap[:, 0:256], XT.ap[0:64, 0, :], self.rwl_t[0:64, 0, :], r=[XT, rwl], w=[lwb], inc=False)
            k.MM(lab.ap[:, 0:256], XT.ap[64:128, 0, :], self.rwl_t[64:128, 1, :], r=[XT, rwl], w=[lab])
            k.MM(lwb.ap[:, 256:512], XT.ap[:, 1, :], self.rwl_t[:, 2, :], r=[XT, rwl], w=[lwb])
            k.TT(f["t0"].ap, lwb.ap[:, 0:256], pb["rw_w0"].ap, ALU.add, r=[lwb, pb["rw_w0"]], w=[f["t0"]])
            k.ACT(f["t0"].ap, f["t0"].ap, AF.Tanh, r=[f["t0"]], w=[f["t0"]], scale=0.5)
            k.TS(f["lw"].ap, f["t0"].ap, c1, ALU.mult, c1, ALU.add, r=[f["t0"]], w=[f["lw"]])
            k.TT(f["t1"].ap, lab.ap[:, 0:256], pb["rw_a0"].ap, ALU.add, r=[lab, pb["rw_a0"]], w=[f["t1"]])
            k.ACT(f["t1"].ap, f["t1"].ap, AF.Tanh, r=[f["t1"]], w=[f["t1"]], scale=0.5)
            k.TS(f["a"].ap, f["t1"].ap, 0.5, ALU.mult, 0.5, ALU.add, r=[f["t1"]], w=[f["a"]])
            k.CP(f["gsb"].ap, lwb.ap[:, 256:512], r=[lwb], w=[f["gsb"]], eng="act")
            yield
            k.TT(f["kk"].ap, kx_, pb["rw_kk"].ap, ALU.mult, r=[rw_sb, pb["rw_kk"]], w=[f["kk"]])
            k.TT(f["t0"].ap, f["kk"].ap, f["kk"].ap, ALU.mult, r=[f["kk"]], w=[f["t0"]])
            k.RSUM(st.ap[:, 0:4], v3(f["t0"].ap), r=[f["t0"]], w=[st])
            k.ACT(st.ap[:, 4:8], st.ap[:, 0:4], AF.Ln, r=[st, self.cst_b], w=[st], bias=self.cst(self.CST["TINY"]))
            k.ACT(st.ap[:, 0:4], st.ap[:, 4:8], AF.Exp, r=[st], w=[st], scale=-0.5)
            k.TT(v3(f["kk"].ap), v3(f["kk"].ap), st.ap[:, 0:4].unsqueeze(2).to_broadcast([128, 4, 64]), ALU.mult,
                 r=[f["kk"], st], w=[f["kk"]])
            k.STT(f["t0"].ap, f["a"].ap, -1.0, pb["rw_ka"].ap, ALU.add, ALU.mult, r=[f["a"], pb["rw_ka"]], w=[f["t0"]])
            k.STT(f["kp"].ap, f["t0"].ap, 1.0, kx_, ALU.add, ALU.mult, r=[f["t0"], rw_sb], w=[f["kp"]])
            k.TT(f["bv"].ap, f["a"].ap, f["kk"].ap, ALU.mult, r=[f["a"], f["kk"]], w=[f["bv"]])
            k.TT(f["t0"].ap, r_, f["kp"].ap, ALU.mult, r=[rw_sb, f["kp"]], w=[f["t0"]])
            k.TT(f["t0"].ap, f["t0"].ap, pb["rw_rk"].ap, ALU.mult, r=[f["t0"], pb["rw_rk"]], w=[f["t0"]])
            k.RSUM(st.ap[:, 8:12], v3(f["t0"].ap), r=[f["t0"]], w=[st])
            k.TT(v3(f["bon"].ap), v3(v_), st.ap[:, 8:12].unsqueeze(2).to_broadcast([128, 4, 64]), ALU.mult,
                 r=[rw_sb, st], w=[f["bon"]])
            vbf = tb16["vbf"]
            k.CP(vbf.ap, v_, r=[rw_sb], w=[vbf], eng="act")
            yield
            bp = self.ps_small()
            k.MM(bp.ap[:, 0:256], self.cm(kind, "mcum"), f["lw"].ap, r=cst + [f["lw"]], w=[bp], inc=False)
            for jj in range(2):
                k.MM(bp.ap[:, 256 + jj * nind:256 + (jj + 1) * nind], f["lw"].ap[:, jj * 128:(jj + 1) * 128],
                     self.cm(kind, "ind"), r=cst + [f["lw"]], w=[bp], inc=(jj == 1))
            dd = dd_buf
            k.ACT(f["t0"].ap, bp.ap[:, 0:256], AF.Exp, r=[bp], w=[f["t0"]])
            k.ACT(f["t1"].ap, bp.ap[:, 0:256], AF.Exp, r=[bp], w=[f["t1"]], scale=-1.0)
            k.ACT(dd.ap[:, :, 0:nind], bp.ap[:, 256:256 + 2 * nind].rearrange("p (a b) -> p a b", a=2), AF.Exp,
                  r=[bp], w=[dd])
            k.TT(f["t2"].ap, bp.ap[:, 0:256], f["lw"].ap, ALU.subtract, r=[bp, f["lw"]], w=[f["t2"]])
            k.ACT(f["t2"].ap, f["t2"].ap, AF.Exp, r=[f["t2"]], w=[f["t2"]])
            rt, kkt = tb16["rt"], tb16["kkt"]
            k.TT(rt.ap, r_, f["t0"].ap, ALU.mult, r=[rw_sb, f["t0"]], w=[rt])
            k.TT(kkt.ap, f["kk"].ap, f["t2"].ap, ALU.mult, r=[f["kk"], f["t2"]], w=[kkt])
            k.TT(KB3.ap[:, 0, :], f["kp"].ap, f["t1"].ap, ALU.mult, r=[f["kp"], f["t1"]], w=[KB3])
            k.TT(KB3.ap[:, 1, :], f["bv"].ap, f["t1"].ap, ALU.mult, r=[f["bv"], f["t1"]], w=[KB3])
            yield
            srcs = [(kkt.ap, kkt, 0), (rt.ap, rt, 0), (kkt.ap, kkt, 1), (rt.ap, rt, 1),
                    (KB3.ap[:, 0, :], KB3, 0), (KB3.ap[:, 0, :], KB3, 1), (KB3.ap[:, 1, :], KB3, 0),
                    (KB3.ap[:, 1, :], KB3, 1)]
            tb = self.ps_tb()
            for i, (ap_, dep_, jj) in enumerate(srcs):
                k.TR(tb.ap[:, i * 128:(i + 1) * 128], ap_[:, jj * 128:(jj + 1) * 128], self.ident.ap,
                     r=[dep_, self.ident], w=[tb], inc=(i == 7))
            k.CP(T8.ap, tb.ap.rearrange("p (a b) -> p a b", a=8), r=[tb], w=[T8], eng="act")
            k.CP(RT2.ap[:, :, 1, :], tb.ap[:, 0:512].rearrange("p (a b c) -> p a b c", a=2, b=2)[:, :, 1, :],
                 r=[tb], w=[RT2])
            yield
            for h in range(4):
                jj, hl = h // 2, h % 2
                R = slice(64 * hl, 64 * hl + 64)
                sb_ = self.ps_small()
                rhs = T8.ap[R, 2 * jj:2 * jj + 2, :]
                k.MM(sb_.ap[:, 0:256].rearrange("p (a b) -> p a b", a=2), T8.ap[R, 4 + jj, :], rhs, r=[T8],
                     w=[sb_], inc=False)
                k.MM(sb_.ap[:, 256:512].rearrange("p (a b) -> p a b", a=2), T8.ap[R, 6 + jj, :], rhs, r=[T8],
                     w=[sb_])
                k.TT(SM.ap[:, h], sb_.ap.rearrange("p (a b) -> p a b", a=4), mask4.ap, ALU.mult,
                     r=[sb_, mask4], w=[SM])
            Lb = [self.ps_small(), self.ps_small()]
            for hl in range(2):
                R = slice(64 * hl, 64 * hl + 64)
                for jj in range(2):
                    k.MM(Lb[hl].ap[:, jj * 128:(jj + 1) * 128], T8.ap[R, 2 * jj, :], T8.ap[R, 6 + jj, :], r=[T8],
                         w=[Lb[hl]], inc=(jj == 1))
            low = self.cm(kind, "low")
            Pa4 = Pa.ap.rearrange("p (j h) t -> p j h t", j=2)
            for hl in range(2):
                k.STT(Pa4[:, :, hl, :], Lb[hl].ap[:, 0:256].rearrange("p (a b) -> p a b", a=2), -1.0,
                      low.unsqueeze(1).to_broadcast([128, 2, 128]), ALU.mult, ALU.mult, r=[Lb[hl]] + cst, w=[Pa])
            Q0 = T(SM.ap[:, :, 2, :], SM.d)
            k.TT(Xc.ap, Q0.ap, self.ident.ap.unsqueeze(1).to_broadcast([128, 4, 128]), ALU.add,
                 r=[SM, self.ident], w=[Xc])
            yield
            nsteps = 5 if kind == "p" else 2
            Pc, Qc = Pa, Q0
            for step in range(1, nsteps + 1):
                last = step == nsteps
                Pn = Pb if Pc is Pa else Pa
                Qn = Qb if (Qc is Qa or Qc is Q0) else Qa
                pb_ = self.ps_small()
                for h in range(4):
                    k.MM(pb_.ap[:, h * 128:(h + 1) * 128], Qc.ap[:, h], Pc.ap[:, h], r=[Qc, Pc], w=[pb_],
                         inc=(h == 3))
                if not last:
                    qb_ = self.ps_small()
                    for h in range(4):
                        k.MM(qb_.ap[:, h * 128:(h + 1) * 128], Pc.ap[:, h], Qc.ap[:, h], r=[Qc, Pc], w=[qb_],
                             inc=(h == 3))
                k.CP(Pn.ap, pb_.ap.rearrange("p (a b) -> p a b", a=4), r=[pb_], w=[Pn], eng="act")
                if not last:
                    k.CP(Qn.ap, qb_.ap.rearrange("p (a b) -> p a b", a=4), r=[qb_], w=[Qn], eng="act")
                xb_ = self.ps_small()
                for h in range(4):
                    k.MM(xb_.ap[:, h * 128:(h + 1) * 128], Pn.ap[:, h], Xc.ap[:, h], r=[Pn, Xc], w=[xb_],
                         inc=(h == 3))
                k.TT(Xc.ap, Xc.ap, xb_.ap.rearrange("p (a b) -> p a b", a=4), ALU.add, r=[Xc, xb_], w=[Xc])
                Pc, Qc = Pn, Qn
                yield
            yield
            tk = self.ps_small()
            for h in range(4):
                jj, hl = h // 2, h % 2
                k.MM(tk.ap[64 * hl:64 * hl + 64, jj * 128:(jj + 1) * 128], kkt.ap[:, h * 64:(h + 1) * 64], Xc.ap[:, h],
                     r=[kkt, Xc], w=[tk], inc=False)
            for h in range(4):
                k.MM(tk.ap[:, 256 + h * 64:256 + (h + 1) * 64], Xc.ap[:, h], kkt.ap[:, h * 64:(h + 1) * 64],
                     r=[kkt, Xc], w=[tk], inc=(h == 3))
            k.CP(RT2.ap[:, :, 0, :], tk.ap[:, 0:256].rearrange("p (a b) -> p a b", a=2), r=[tk], w=[RT2], eng="act")
            k.CP(KB3.ap[:, 2, :], tk.ap[:, 256:512], r=[tk], w=[KB3], eng="act")
            av = self.ps_small()
            AV, NTAVb, NU = tb16["AV"], tb16["NTAVb"], tb16["NU"]
            for h in range(4):
                k.MM(av.ap[:, h * 64:(h + 1) * 64], SM.ap[:, h, 0, :], vbf.ap[:, h * 64:(h + 1) * 64], r=[SM, vbf],
                     w=[av], inc=(h == 3))
            k.CP(AV.ap, av.ap[:, 0:256], r=[av], w=[AV], eng="act")
            for h in range(4):
                k.MM(av.ap[:, 256 + h * 64:256 + (h + 1) * 64], Xc.ap[:, h], AV.ap[:, h * 64:(h + 1) * 64],
                     r=[Xc, AV], w=[av], inc=(h == 3))
            k.TS(f["NTAVf"].ap, av.ap[:, 256:512], -1.0, ALU.mult, r=[av], w=[f["NTAVf"]])
            k.CP(NTAVb.ap, f["NTAVf"].ap, r=[f["NTAVf"]], w=[NTAVb], eng="act")
            bd = c128[:, C_BD:C_BD + 128].unsqueeze(1).to_broadcast([128, 2, 128])
            yield
            if kind == "p":
                Ys = [self.ps_small(), self.ps_small()]
                for c in range(2):
                    Rc = slice(64 * c, 64 * c + 64)
                    for jj in range(2):
                        cs = slice(jj * 128, (jj + 1) * 128)
                        k.MM(Ys[c].ap[:, cs], KB3.ap[Rc, 2, cs], KB3.ap[Rc, 1, cs], r=[KB3], w=[Ys[c]], inc=False)
                    for jj in range(2):
                        cs = slice(jj * 128, (jj + 1) * 128)
                        co = slice(256 + jj * 128, 256 + (jj + 1) * 128)
                        k.MM(Ys[c].ap[:, co], KB3.ap[Rc, 0, cs], vbf.ap[Rc, cs], start=True, stop=False,
                             r=[KB3, vbf], w=[Ys[c]], inc=False)
                        k.MM(Ys[c].ap[:, co], KB3.ap[Rc, 1, cs], NTAVb.ap[Rc, cs], start=False, stop=True,
                             r=[KB3, NTAVb], w=[Ys[c]], inc=(jj == 1))
                for c in range(2):
                    k.STT(NG.ap[:, c], Ys[c].ap[:, 0:256].rearrange("p (a b) -> p a b", a=2), -1.0, bd, ALU.mult,
                          ALU.mult, r=[Ys[c]] + cst, w=[NG])
                    y2 = Ys[c].ap[:, 256:512].rearrange("p (a b) -> p a b", a=2)
                    k.CP(Hd.ap[0:64, c], y2[0:64, :, 0:64], r=[Ys[c]], w=[Hd])
                    k.CP(Hd.ap[64:128, c], y2[64:128, :, 64:128], r=[Ys[c]], w=[Hd], eng="act")
                for c in range(2):
                    d1 = dd.ap[:, :, 2 * c:2 * c + 1].to_broadcast([128, 2, 64])
                    d2 = dd.ap[:, :, 2 * c + 1:2 * c + 2].to_broadcast([128, 2, 64])
                    k.TT(Smid.ap, S.ap, d1, ALU.mult, r=[S, dd], w=[Smid])
                    k.CP(SmidB.ap[:, c], Smid.ap, r=[Smid], w=[SmidB], eng="act")
                    k.CP(SmidX.ap[0:64, c, :, 0:64], Smid.ap[0:64], r=[Smid], w=[SmidX])
                    k.CP(SmidX.ap[64:128, c, :, 64:128], Smid.ap[64:128], r=[Smid], w=[SmidX], eng="act")
                    Z = self.ps_small()
                    for jj in range(2):
                        k.MM(Z.ap[:, jj * 64:(jj + 1) * 64], NG.ap[:, c, jj, :], SmidB.ap[:, c, jj, :],
                             r=[NG, SmidB], w=[Z], inc=(jj == 1))
                    k.TT(tmp.ap, Smid.ap, Z.ap[:, 0:128].rearrange("p (a b) -> p a b", a=2), ALU.add,
                         r=[Smid, Z], w=[tmp])
                    k.TT(tmp.ap, tmp.ap, Hd.ap[:, c], ALU.add, r=[tmp, Hd], w=[tmp])
                    k.TT(S.ap, tmp.ap, d2, ALU.mult, r=[tmp, dd], w=[S])
                if t == self.npt - 1:
                    self.store_state_p("rw", l, S, 64)
                yield
                ub = self.ps_small()
                for c in range(2):
                    Rc = slice(64 * c, 64 * c + 64)
                    for jj in range(2):
                        k.MM(ub.ap[Rc, jj * 128:(jj + 1) * 128], RT2.ap[:, jj, 0, Rc], SmidX.ap[:, c, jj, :],
                             r=[RT2, SmidX], w=[ub], inc=(c == 1 and jj == 1))
                k.STT(NU.ap, ub.ap[:, 0:256], -1.0, f["NTAVf"].ap, ALU.mult, ALU.add, r=[ub, f["NTAVf"]], w=[NU])
                yield
                ob_ = self.ps_small()
                for c in range(2):
                    Rc = slice(64 * c, 64 * c + 64)
                    for h in range(4):
                        jj, hl = h // 2, h % 2
                        Hc = slice(h * 64, (h + 1) * 64)
                        out = ob_.ap[Rc, Hc]
                        k.MM(out, SM.ap[:, h, 1, Rc], vbf.ap[:, Hc], start=True, stop=False, r=[SM, vbf], w=[ob_],
                             inc=False)
                        k.MM(out, SM.ap[:, h, 3, Rc], NU.ap[:, Hc], start=False, stop=False, r=[SM, NU], w=[ob_],
                             inc=False)
                        k.MM(out, RT2.ap[:, jj, 1, Rc], SmidX.ap[:, c, jj, hl * 64:(hl + 1) * 64], start=False,
                             stop=True, r=[RT2, SmidX], w=[ob_], inc=(c == 1 and h == 3))
            else:
                S0, Sn = sp["S0"], sp["Sn"]
                src = self.st_in["rw"][l]
                dst = self.s_out["rw"][l]
                big = self.ps_big()
                ub = T(big.ap[:, 0:512], (big.d[0], (big.d[1][0],)))
                ob_ = T(big.ap[:, 512:1024], (big.d[0], (big.d[1][1],)))
                ddv = dd.ap.rearrange("p j (q t) -> p q j t", t=2)
                cmd = [T(self.cm_t[:], self.cm_b.all)]
                zrhs = self.cm_t[:, 0:2, :].rearrange("p a b -> p (a b)")
                for bk in (ub, ob_):
                    k.MM(bk.ap[:, 0:256], self.zl.ap, zrhs, start=True, stop=False, r=[self.zl] + cmd, w=[bk])
                for hf in range(2):
                    q0 = 8 * hf
                    d1 = ddv[:, q0:q0 + 8, :, 0:1].to_broadcast([128, 8, 2, 64])
                    d2 = ddv[:, q0:q0 + 8, :, 1:2].to_broadcast([128, 8, 2, 64])
                    for hl in range(2):
                        for jj in range(2):
                            k.DMA(S0.ap[64 * hl:64 * hl + 64, :, jj, :],
                                  src[q0:q0 + 8, 2 * jj + hl].rearrange("s k v -> k s v"), w=[S0])
                    k.TT(S0.ap, S0.ap, d1, ALU.mult, r=[S0, dd], w=[S0])
                    for q in range(8):
                        sq = q0 + q
                        KBm, NGs, SmB, SmX, RTm = [sp[n_][sq % 2] for n_ in ("KBm", "NGs", "SmB", "SmX", "RTm")]
                        k.TS(KBm.ap.rearrange("p a b -> p (a b)"), KB3.ap.rearrange("p a b -> p (a b)"),
                             c128[:, C_ROWS + sq:C_ROWS + sq + 1], ALU.mult, r=[KB3] + cst, w=[KBm])
                        Y = self.ps_small()
                        for jj in range(2):
                            cs = slice(jj * 128, (jj + 1) * 128)
                            k.MM(Y.ap[:, cs], KBm.ap[:, 2, cs], KB3.ap[:, 1, cs], r=[KBm, KB3], w=[Y], inc=False)
                        for jj in range(2):
                            cs = slice(jj * 128, (jj + 1) * 128)
                            co = slice(256 + jj * 128, 256 + (jj + 1) * 128)
                            k.MM(Y.ap[:, co], KBm.ap[:, 0, cs], vbf.ap[:, cs], start=True, stop=False,
                                 r=[KBm, vbf], w=[Y], inc=False)
                            k.MM(Y.ap[:, co], KBm.ap[:, 1, cs], NTAVb.ap[:, cs], start=False, stop=True,
                                 r=[KBm, NTAVb], w=[Y], inc=(jj == 1))
                        k.STT(NGs.ap, Y.ap[:, 0:256].rearrange("p (a b) -> p a b", a=2), -1.0, bd, ALU.mult,
                              ALU.mult, r=[Y] + cst, w=[NGs])
                        y2 = Y.ap[:, 256:512].rearrange("p (a b) -> p a b", a=2)
                        k.CP(Sn.ap[0:64, q], y2[0:64, :, 0:64], r=[Y], w=[Sn])
                        k.CP(Sn.ap[64:128, q], y2[64:128, :, 64:128], r=[Y], w=[Sn], eng="act")
                        k.CP(SmB.ap, S0.ap[:, q], r=[S0], w=[SmB], eng="act")
                        k.CP(SmX.ap[0:64, :, 0:64], S0.ap[0:64, q], r=[S0], w=[SmX])
                        k.CP(SmX.ap[64:128, :, 64:128], S0.ap[64:128, q], r=[S0], w=[SmX], eng="act")
                        Z = self.ps_small()
                        for jj in range(2):
                            k.MM(Z.ap[:, jj * 64:(jj + 1) * 64], NGs.ap[:, jj, :], SmB.ap[:, jj, :], r=[NGs, SmB],
                                 w=[Z], inc=(jj == 1))
                        k.TT(Sn.ap[:, q], Sn.ap[:, q], Z.ap[:, 0:128].rearrange("p (a b) -> p a b", a=2), ALU.add,
                             r=[Sn, Z], w=[Sn])
                        k.TT(RTm.ap.rearrange("p a b c -> p (a b) c"), RT2.ap.rearrange("p a b c -> p (a b) c"),
                             self.cm_t[:, sq:sq + 1, :].to_broadcast([128, 4, 128]), ALU.mult, r=[RT2] + cmd,
                             w=[RTm])
                        for jj in range(2):
                            k.MM(ub.ap[:, jj * 128:(jj + 1) * 128], RTm.ap[:, jj, 0, :], SmX.ap[:, jj, :],
                                 start=False, stop=False, r=[RTm, SmX], w=[ub], inc=False)
                        for h in range(4):
                            jj, hl = h // 2, h % 2
                            k.MM(ob_.ap[:, h * 64:(h + 1) * 64], RTm.ap[:, jj, 1, :],
                                 SmX.ap[:, jj, hl * 64:(hl + 1) * 64], start=False, stop=False,
                                 r=[RTm, SmX], w=[ob_], inc=(h == 3))
                    k.TT(Sn.ap, Sn.ap, S0.ap, ALU.add, r=[Sn, S0], w=[Sn])
                    k.TT(Sn.ap, Sn.ap, d2, ALU.mult, r=[Sn, dd], w=[Sn])
                    for hl in range(2):
                        for jj in range(2):
                            k.DMA(dst[q0:q0 + 8, 2 * jj + hl].rearrange("s k v -> k s v"),
                                  Sn.ap[64 * hl:64 * hl + 64, :, jj, :], r=[Sn], is_output=True)
                k.MM(ub.ap[:, 0:256], self.zl.ap, zrhs, start=False, stop=True, r=[self.zl] + cmd, w=[ub])
                k.STT(NU.ap, ub.ap[:, 0:256], -1.0, f["NTAVf"].ap, ALU.mult, ALU.add, r=[ub, f["NTAVf"]], w=[NU])
                for h in range(4):
                    Hc = slice(h * 64, (h + 1) * 64)
                    k.MM(ob_.ap[:, Hc], SM.ap[:, h, 1, :], vbf.ap[:, Hc], start=False, stop=False, r=[SM, vbf],
                         w=[ob_], inc=False)
                    k.MM(ob_.ap[:, Hc], SM.ap[:, h, 3, :], NU.ap[:, Hc], start=False, stop=False, r=[SM, NU],
                         w=[ob_], inc=False)
                k.MM(ob_.ap[:, 0:256], self.zl.ap, zrhs, start=False, stop=True, r=[self.zl] + cmd, w=[ob_])
            yield
            osb, t0 = f["t1"], f["t0"]
            k.CP(osb.ap, ob_.ap[:, 0:256], r=[ob_], w=[osb], eng="act")
            k.RSUM(st.ap[:, 0:4], v3(osb.ap), r=[osb], w=[st])
            k.TT(t0.ap, osb.ap, osb.ap, ALU.mult, r=[osb], w=[t0])
            k.RSUM(st.ap[:, 4:8], v3(t0.ap), r=[t0], w=[st])
            k.TS(st.ap[:, 8:12], st.ap[:, 0:4], 1.0 / 64, ALU.mult, r=[st], w=[st])
            k.TT(st.ap[:, 12:16], st.ap[:, 8:12], st.ap[:, 8:12], ALU.mult, r=[st], w=[st])
            k.STT(st.ap[:, 4:8], st.ap[:, 4:8], 1.0 / 64, st.ap[:, 12:16], ALU.mult, ALU.subtract, r=[st], w=[st])
            k.ACT(st.ap[:, 0:4], st.ap[:, 4:8], AF.Ln, r=[st, self.cst_b], w=[st], bias=self.cst(self.CST["EPSLN"]))
            k.ACT(st.ap[:, 0:4], st.ap[:, 0:4], AF.Exp, r=[st], w=[st], scale=-0.5)
            k.TT(v3(t0.ap), v3(osb.ap), st.ap[:, 8:12].unsqueeze(2).to_broadcast([128, 4, 64]), ALU.subtract,
                 r=[osb, st], w=[t0])
            k.TT(v3(t0.ap), v3(t0.ap), st.ap[:, 0:4].unsqueeze(2).to_broadcast([128, 4, 64]), ALU.mult,
                 r=[t0, st], w=[t0])
            k.TT(t0.ap, t0.ap, pb["rw_ln_w"].ap, ALU.mult, r=[t0, pb["rw_ln_w"]], w=[t0])
            k.TT(t0.ap, t0.ap, pb["rw_ln_b"].ap, ALU.add, r=[t0, pb["rw_ln_b"]], w=[t0])
            k.TT(t0.ap, t0.ap, f["bon"].ap, ALU.add, r=[t0, f["bon"]], w=[t0])
            k.TT(tb16["ob"].ap, t0.ap, f["gsb"].ap, ALU.mult, r=[t0, f["gsb"]], w=[tb16["ob"]])
            self.to_oT(tb16["ob"], j, mi)
            yield
        self.rw_top = self.sc.top
        self.sc.reset(m)


_WNAMES = ("attn_norm_w", "w_in", "hg_lb_logits", "hg_norm_w", "gla_wa2", "gla_ba", "gla_norm_w", "rw_mu", "rw_w0",
           "rw_w2", "rw_a0", "rw_a2", "rw_g2", "rw_kk", "rw_ka", "rw_rk", "rw_ln_w", "rw_ln_b", "w_branch", "w_out",
           "ffn_norm_w", "w_ffn_in", "w_ffn_out", "final_norm_w")


def make_in_map(inputs, core, npt, consts):
    f = lambda a: np.ascontiguousarray(np.asarray(a, dtype=np.float32))
    xp = f(inputs["x_prompt"])[core, :npt * 128]
    s0, s1 = core * DEC_PER_CORE, (core + 1) * DEC_PER_CORE
    xs = f(inputs["x_sample"])[s0:s1].reshape(DEC_PER_CORE * DEC_SEQ, D)
    m = {"xin": np.ascontiguousarray(np.concatenate([xp, xs], 0)),
         "st_hg": f(inputs["state_hgrn"])[:, s0:s1], "st_gla": f(inputs["state_gla"])[:, s0:s1],
         "st_rw": f(inputs["state_rwkv"])[:, s0:s1], "st_ret": f(inputs["state_ret"])[:, s0:s1],
         "st_shift": f(inputs["state_rwkv_shift"])[:, s0:s1]}
    for n in _WNAMES:
        m[n] = f(inputs[n])
    m["c128"], m["colmask"], m["rot"] = consts
    return {k_: np.ascontiguousarray(v) for k_, v in m.items()}


_NC_CACHE = {}


def kernel(**inputs):
    npt = SEQ // 128
    passes = [list(range(0, 8)), list(range(8, 17))]
    key = (npt, str(passes))
    if key not in _NC_CACHE:
        _NC_CACHE[key] = build(npt, passes)
    nc = _NC_CACHE[key]
    consts = make_consts(npt)
    in_maps = [make_in_map(inputs, c, npt, consts) for c in range(N_CORES)]
    res = run_bass_kernel_spmd(nc, in_maps, core_ids=list(range(N_CORES))).results
    y_prompt = np.stack([r["yout"][:npt * 128] for r in res], 0)
    y_sample = np.concatenate([r["yout"][npt * 128:].reshape(DEC_PER_CORE, DEC_SEQ, D) for r in res], 0)
    outs = [y_prompt.astype(np.float32), y_sample.astype(np.float32)]
    for nm in ("p_hg", "p_gla", "p_rw", "p_shift", "p_ret"):
        outs.append(np.stack([r[nm] for r in res], 1).astype(np.float32))
    for nm in ("s_hg", "s_gla", "s_rw", "s_shift", "s_ret"):
        outs.append(np.concatenate([r[nm] for r in res], 1).astype(np.float32))
    return tuple(outs)
```

```python
import contextlib
import math
import os
import numpy as np
import concourse.bass as bass
import concourse.mybir as mybir
from concourse.bass_utils import run_bass_kernel_spmd

F32 = mybir.dt.float32
BF16 = mybir.dt.bfloat16
AF = mybir.ActivationFunctionType
ALU = mybir.AluOpType
AX = mybir.AxisListType

D = 1024
DEPTH = 2
N_CORES = 8
SEQ = 2048
DEC_SEQ = 8
DEC_PER_CORE = 16
PAST_LEN = 16384
N_IN = 7952
D_FF = 2816
NORM_EPS = 1e-6
RW_LN_EPS = 64e-5
C_HG = 0
C_GLA = 1024
C_RW = 1808
C_RET = 2832
C_GATE = 3856

ENGS = ("pe", "dve", "act", "pool", "sp")


class Buf:
    def __init__(self, name, n=1, excl=False):
        self.name = name
        self.n = n
        self.excl = excl
        self.w = [None] * n
        self.r = [dict() for _ in range(n)]

    def __getitem__(self, idx):
        if isinstance(idx, int):
            return (self, (idx,))
        if isinstance(idx, slice):
            return (self, tuple(range(*idx.indices(self.n))))
        return (self, tuple(idx))

    @property
    def all(self):
        return (self, tuple(range(self.n)))


def _cells(x):
    if isinstance(x, Buf):
        return x.all
    return x


class Prog:
    NDMA = {None: 6, "A": 3, "B": 3}

    def __init__(self, nc, es):
        self.nc = nc
        self.sem = {}
        self.cnt = {}
        self.q = {}
        self.seen = {}
        self.dma_sems = {}
        self.dma_next = {}
        for st in (None, "A", "B"):
            tag = st or "m"
            self.q[st] = {e: [] for e in ENGS}
            self.seen[st] = {e: {} for e in ENGS}
            for e in ENGS:
                k = "%s:%s" % (tag, e)
                self.sem[k] = es.enter_context(nc.semaphore("s%s_%s" % (tag, e)))
                self.cnt[k] = 0
            for iss in ("sp", "pool", "act"):
                lst = []
                for i in range(self.NDMA[st]):
                    k = "%s:d_%s%d" % (tag, iss, i)
                    self.sem[k] = es.enter_context(nc.semaphore("d%s_%s%d" % (tag, iss, i)))
                    self.cnt[k] = 0
                    lst.append(k)
                self.dma_sems[(st, iss)] = lst
                self.dma_next[(st, iss)] = 0
        self.cur = None
        self.n_instr = 0
        self.out_tokens = []
        self.glob = {"A": [], "B": []}
        self.act_tbl = None
        self.n_tbl_switch = 0

    def _push(self, eng, emit, deps=None, tok=None, cost=0.4, signals=True, tbl=None):
        if self.cur is None:
            self.q[None][eng].append(emit)
            if tbl is not None:
                self.act_tbl = tbl
        else:
            self.glob[self.cur].append(dict(eng=eng, emit=emit, deps=dict(deps or {}), tok=tok, cost=cost,
                                            signals=signals, tbl=tbl))

    def _note(self, tok, deps):
        pass

    def ek(self, eng):
        return "%s:%s" % (self.cur or "m", eng)

    def _deps(self, reads, writes, eng=None):
        deps = {}
        self._me = self.ek(eng) if eng is not None else None

        def need(tok):
            if tok is None:
                return
            k, v = tok
            if deps.get(k, 0) < v:
                deps[k] = v

        for b, cells in map(_cells, reads):
            for c in cells:
                need(b.w[c])
                if b.excl:
                    for k, v in b.r[c].items():
                        if k != self._me:
                            need((k, v))
        for b, cells in map(_cells, writes):
            for c in cells:
                need(b.w[c])
                for k, v in b.r[c].items():
                    need((k, v))
        return deps

    def _commit(self, tok, reads, writes):
        k, v = tok
        for b, cells in map(_cells, reads):
            for c in cells:
                if b.r[c].get(k, 0) < v:
                    b.r[c][k] = v
        for b, cells in map(_cells, writes):
            for c in cells:
                b.w[c] = tok
                b.r[c] = {}

    def _waits(self, eng, deps):
        ws = []
        seen = self.seen[self.cur][eng]
        me = self.ek(eng)
        for k, v in deps.items():
            if seen.get(k, 0) >= v:
                continue
            if k == me and v > self.cnt[me]:
                continue
            seen[k] = v
            ws.append((self.sem[k], v))
        return ws

    def op(self, eng, fn, reads=(), writes=(), inc=True, cost=0.4, tbl=None):
        deps = self._deps(reads, writes, eng)
        ws = self._waits(eng, deps)
        me = self.ek(eng)
        sem = self.sem[me]
        if inc:
            self.cnt[me] += 1
            tok = (me, self.cnt[me])
        else:
            tok = (me, self.cnt[me] + 1)
        self._commit(tok, reads, writes)
        self._note(tok, deps)
        self.n_instr += 1

        def emit(e, ws=ws, fn=fn, inc=inc, sem=sem):
            for s, v in ws:
                e.wait_ge(s, v)
            ins = fn(e)
            if inc:
                ins.then_inc(sem, 1)
        self._push(eng, emit, deps, tok, cost, signals=inc, tbl=tbl)
        return tok

    def dma(self, iss, out, in_, reads=(), writes=(), is_output=False):
        deps = self._deps(reads, writes)
        key = (self.cur, iss)
        i = self.dma_next[key]
        self.dma_next[key] = (i + 1) % len(self.dma_sems[key])
        k = self.dma_sems[key][i]
        if self.cnt[k] > 0:
            if deps.get(k, 0) < self.cnt[k]:
                deps[k] = self.cnt[k]
        ws = self._waits(iss, deps)
        self.cnt[k] += 16
        tok = (k, self.cnt[k])
        self._commit(tok, reads, writes)
        self._note(tok, deps)
        sem = self.sem[k]
        self.n_instr += 1
        if is_output:
            self.out_tokens.append(tok)

        def emit(e, ws=ws, sem=sem, out=out, in_=in_):
            for s, v in ws:
                e.wait_ge(s, v)
            e.dma_start(out=out, in_=in_).then_inc(sem, 16)
        self._push(iss, emit, deps, tok, 2.5, signals=True)
        return tok

    def merge_streams(self):
        L = {"A": self.glob["A"], "B": self.glob["B"]}
        LAT = float(os.environ.get("MK_LAT", "0.6"))
        TBL = float(os.environ.get("MK_TBL", "1.3"))
        prod = {}
        for st in ("A", "B"):
            for idx, ins in enumerate(L[st]):
                if ins["signals"] and ins["tok"] is not None:
                    prod[ins["tok"]] = (st, idx)
        fin = {"A": [0.0] * len(L["A"]), "B": [0.0] * len(L["B"])}
        placed = {"A": 0, "B": 0}
        t_free = {e: 0.0 for e in ENGS}

        def start_time(st):
            i = placed[st]
            if i >= len(L[st]):
                return None
            ins = L[st][i]
            ready = 0.0
            for tk in ins["deps"].items():
                p = prod.get(tk)
                if p is None:
                    continue
                pst, pidx = p
                if pidx >= placed[pst]:
                    if pst != st:
                        return float("inf")
                    continue
                lat = LAT if L[pst][pidx]["eng"] != ins["eng"] else 0.05
                ready = max(ready, fin[pst][pidx] + lat)
            pen = TBL if (ins["tbl"] is not None and ins["tbl"] != tblstate[0]) else 0.0
            return max(ready, t_free[ins["eng"]]) + pen

        tblstate = [self.act_tbl]
        rem = {}
        for st in ("A", "B"):
            r = [0.0] * (len(L[st]) + 1)
            for i in range(len(L[st]) - 1, -1, -1):
                r[i] = r[i + 1] + L[st][i]["cost"]
            rem[st] = r
        BIAS = float(os.environ.get("MK_BIAS", "0.03"))
        while placed["A"] < len(L["A"]) or placed["B"] < len(L["B"]):
            sa, sb = start_time("A"), start_time("B")
            EF = float(os.environ.get("MK_EF", "1.0"))
            ka = None if sa is None else sa + EF * L["A"][placed["A"]]["cost"] - BIAS * rem["A"][placed["A"]]
            kb = None if sb is None else sb + EF * L["B"][placed["B"]]["cost"] - BIAS * rem["B"][placed["B"]]
            if sb is None or (sa is not None and ka <= kb):
                st, t0 = "A", sa
            else:
                st, t0 = "B", sb
            if t0 == float("inf"):
                st = "B" if st == "A" else "A"
                t0 = start_time(st)
                assert t0 is not None and t0 != float("inf")
            ins = L[st][placed[st]]
            e = ins["eng"]
            if ins["tbl"] is not None and ins["tbl"] != tblstate[0]:
                tblstate[0] = ins["tbl"]
                self.n_tbl_switch += 1
            is_dma = ins["cost"] >= 2.0 and e in ("sp", "pool")
            fin[st][placed[st]] = t0 + ins["cost"]
            t_free[e] = t0 + (0.15 if is_dma else ins["cost"])
            placed[st] += 1
            self.q[None][e].append(ins["emit"])
        self.est_span = max(t_free.values())
        self.act_tbl = tblstate[0]
        self.glob = {"A": [], "B": []}

    def finish(self):
        assert self.cur is None
        deps = {}
        for k, v in self.out_tokens:
            deps[k] = max(deps.get(k, 0), v)
        for k in self.sem:
            if k != "m:sp" and self.cnt[k] > 0:
                deps[k] = max(deps.get(k, 0), self.cnt[k])
        ws = self._waits("sp", deps)

        def emit(e, ws=ws):
            for s, v in ws:
                e.wait_ge(s, v)
        self.q[None]["sp"].append(emit)

    def emit_all(self):
        nc = self.nc
        q = self.q[None]
        with nc.Block() as block:
            @block.tensor
            def _(e):
                for f in q["pe"]:
                    f(e)

            @block.vector
            def _(e):
                for f in q["dve"]:
                    f(e)

            @block.scalar
            def _(e):
                for f in q["act"]:
                    f(e)

            @block.gpsimd
            def _(e):
                for f in q["pool"]:
                    f(e)

            @block.sync
            def _(e):
                for f in q["sp"]:
                    f(e)


class T:
    def __init__(self, ap, d):
        self.ap = ap
        self.d = d


def _dl(lst):
    out = []
    for x in lst:
        if isinstance(x, T):
            out.append(x.d)
        elif x is not None:
            out.append(x)
    return out


_SZ = {F32: 4, BF16: 2}


class Arena:
    def __init__(self, nc, es, name, nbytes, cell=512):
        self.t = es.enter_context(nc.sbuf_tensor(name, [128, nbytes // 2], BF16))
        self.cell = cell
        self.nbytes = nbytes
        self.buf = Buf(name, (nbytes + cell - 1) // cell)
        self.top = 0

    def view(self, off, shape, dt):
        n = 1
        for s in shape:
            n *= s
        nb = n * _SZ[dt]
        assert off % 4 == 0 and off + nb <= self.nbytes, (off, nb, self.nbytes)
        ap = self.t[:, off // 2:(off + nb) // 2]
        if dt == F32:
            ap = ap.bitcast(F32)
        if len(shape) == 2:
            ap = ap.rearrange("p (a b) -> p a b", a=shape[0])
        elif len(shape) == 3:
            ap = ap.rearrange("p (a b c) -> p a b c", a=shape[0], b=shape[1])
        elif len(shape) == 4:
            ap = ap.rearrange("p (a b c d) -> p a b c d", a=shape[0], b=shape[1], c=shape[2])
        cells = tuple(range(off // self.cell, (off + nb + self.cell - 1) // self.cell))
        return T(ap, (self.buf, cells))

    def alloc(self, shape, dt):
        n = 1
        for s in shape:
            n *= s
        nb = n * _SZ[dt]
        off = (self.top + self.cell - 1) // self.cell * self.cell
        self.top = off + nb
        return self.view(off, shape, dt)

    def mark(self):
        return self.top

    def reset(self, m=0):
        self.top = m


class K:
    def __init__(self, nc, es):
        self.nc = nc
        self.es = es
        self.P = Prog(nc, es)
        self.uid = 0

    def sb(self, name, shape, dt=F32, cells=1):
        t = self.es.enter_context(self.nc.sbuf_tensor(name, shape, dt))
        return t, Buf(name, cells)

    @staticmethod
    def _n(ap):
        n = 1
        for d in ap.shape[1:]:
            n *= d
        return n

    def MM(self, out, lhsT, rhs, start=True, stop=True, r=(), w=(), inc=True):
        fp32 = 4.0 if lhsT.dtype == F32 else 1.0
        return self.P.op("pe", lambda e: e.matmul(out, lhsT, rhs, start=start, stop=stop),
                         reads=_dl(r), writes=_dl(w), inc=inc, cost=0.06 + fp32 * self._n(out) / 1500.0)

    def TR(self, out, in_, ident, r=(), w=(), inc=True):
        return self.P.op("pe", lambda e: e.transpose(out, in_, ident), reads=_dl(r), writes=_dl(w), inc=inc,
                         cost=0.25)

    def ACT(self, out, in_, func, r=(), w=(), bias=None, scale=None, accum=None):
        kw = {}
        if bias is not None:
            kw["bias"] = bias
        if scale is not None:
            kw["scale"] = scale
        if accum is not None:
            kw["accum_out"] = accum
        tbl = "T" if func in (AF.Tanh, AF.Silu) else ("L" if func == AF.Ln else None)
        return self.P.op("act", lambda e: e.activation(out=out, in_=in_, func=func, **kw),
                         reads=_dl(r), writes=_dl(w), cost=0.25 + self._n(out) / 1400.0, tbl=tbl)

    def TT(self, out, in0, in1, op, r=(), w=(), eng="dve"):
        return self.P.op(eng, lambda e: e.tensor_tensor(out, in0, in1, op), reads=_dl(r), writes=_dl(w),
                         cost=0.2 + self._n(out) / 1000.0)

    def TS(self, out, in0, s1, op0, s2=None, op1=None, r=(), w=(), eng="dve"):
        if op1 is None:
            return self.P.op(eng, lambda e: e.tensor_scalar(out, in0, s1, None, op0), reads=_dl(r), writes=_dl(w),
                         cost=0.2 + self._n(out) / 1000.0)
        return self.P.op(eng, lambda e: e.tensor_scalar(out, in0, s1, s2, op0, op1), reads=_dl(r), writes=_dl(w),
                         cost=0.2 + self._n(out) / 1000.0)

    def STT(self, out, in0, scalar, in1, op0, op1, r=(), w=(), eng="dve"):
        return self.P.op(eng, lambda e: e.scalar_tensor_tensor(out, in0, scalar, in1, op0, op1),
                         reads=_dl(r), writes=_dl(w), cost=0.2 + self._n(out) / 1000.0)

    def CP(self, out, in_, r=(), w=(), eng="dve"):
        if eng == "act":
            return self.ACT(out, in_, AF.Copy, r=r, w=w)
        return self.P.op(eng, lambda e: e.tensor_copy(out, in_), reads=_dl(r), writes=_dl(w),
                         cost=0.2 + self._n(out) / 1000.0)

    def RSUM(self, out, in_, r=(), w=(), eng="dve"):
        return self.P.op(eng, lambda e: e.reduce_sum(out, in_, axis=AX.X), reads=_dl(r), writes=_dl(w),
                         cost=0.2 + self._n(out) / 1000.0)

    def MEMSET(self, out, val, w=(), eng="dve"):
        return self.P.op(eng, lambda e: e.memset(out, val), writes=_dl(w))

    def DMA(self, out, in_, r=(), w=(), iss="sp", is_output=False):
        return self.P.dma(iss, out, in_, reads=_dl(r), writes=_dl(w), is_output=is_output)


C_ID, C_MCP, C_MCS, C_TIP, C_TIS, C_TSP, C_TSS, C_LP, C_LS = [i * 128 for i in range(9)]
C_INDP = 9 * 128
C_INDS = C_INDP + 4
C_ROWS = C_INDS + 32
C_SHP = C_ROWS + 16
C_SHS = C_SHP + 128
C_CAR = C_SHS + 128
C_SEL = C_CAR + 128
C_BD = C_SEL + 128
NC128 = C_BD + 128


def _chunk_consts(C):
    n = 128 // C
    ch = np.arange(128) // C
    same = ch[:, None] == ch[None, :]
    s = np.arange(128)[:, None]
    t = np.arange(128)[None, :]
    mid = (ch * C + (C // 2 - 1))
    mcum = (same & (s <= t)).astype(np.float32) - (same & (s <= mid[None, :])).astype(np.float32)
    ti = (same & (s <= t)).astype(np.float32)
    tstrict = (same & (s < t)).astype(np.float32)
    low = (same & (s > t)).astype(np.float32)
    ind = np.zeros((128, 2 * n), np.float32)
    for c in range(n):
        rows = np.arange(c * C, (c + 1) * C)
        m = c * C + C // 2 - 1
        ind[rows[rows <= m], 2 * c] = 1.0
        ind[rows[rows > m], 2 * c + 1] = 1.0
    return mcum, ti, tstrict, low, ind


def make_consts(npt):
    c = np.zeros((128, NC128), np.float32)
    c[:, C_ID:C_ID + 128] = np.eye(128, dtype=np.float32)
    mp = _chunk_consts(64)
    ms = _chunk_consts(8)
    c[:, C_MCP:C_MCP + 128], c[:, C_TIP:C_TIP + 128], c[:, C_TSP:C_TSP + 128], c[:, C_LP:C_LP + 128] = mp[:4]
    c[:, C_MCS:C_MCS + 128], c[:, C_TIS:C_TIS + 128], c[:, C_TSS:C_TSS + 128], c[:, C_LS:C_LS + 128] = ms[:4]
    c[:, C_INDP:C_INDP + 4] = mp[4]
    c[:, C_INDS:C_INDS + 32] = ms[4]
    seq = np.arange(128) // 8
    c[:, C_ROWS:C_ROWS + 16] = (seq[:, None] == np.arange(16)[None, :]).astype(np.float32)
    sh = np.zeros((128, 128), np.float32)
    sh[np.arange(127), np.arange(1, 128)] = 1.0
    c[:, C_SHP:C_SHP + 128] = sh
    shs = sh.copy()
    shs[:, np.arange(0, 128, 8)] = 0.0
    c[:, C_SHS:C_SHS + 128] = shs
    c[127, C_CAR] = 1.0
    for q in range(16):
        c[q, C_SEL + 8 * q] = 1.0
    blk = np.arange(128) // 64
    c[:, C_BD:C_BD + 128] = (blk[:, None] == blk[None, :]).astype(np.float32)
    colmask = np.broadcast_to((np.arange(16)[:, None] == seq[None, :]).astype(np.float32)[None], (128, 16, 128))
    colmask = np.ascontiguousarray(colmask).reshape(128, 2048)
    inv = np.power(np.float32(10000.0), -(np.arange(32, dtype=np.float32) / np.float32(32))).astype(np.float32)
    rot = np.zeros((npt + 1, 128, 64), np.float32)
    for i in range(npt + 1):
        if i < npt:
            pos = (np.arange(128) + 128 * i).astype(np.float32)
        else:
            pos = (np.float32(PAST_LEN) + (np.arange(128) % 8).astype(np.float32)).astype(np.float32)
        ang = (pos[:, None] * inv[None, :]).astype(np.float32)
        rot[i, :, :32] = np.cos(ang)
        rot[i, :, 32:] = np.sin(ang)
    return c, colmask, rot


PB_NORM, PB_MU, PB_LB, PB_HGN, PB_GBA, PB_GLN, PB_W0, PB_A0, PB_KK, PB_KA, PB_RK, PB_LNW, PB_LNB = (
    0, 1024, 2048, 2560, 2624, 2752, 2816, 3072, 3328, 3584, 3840, 4096, 4352)
NPB = 4608


def build(npt, passes, mixers=("hg", "gla", "rw", "ret"), depth=DEPTH, debug=None):
    nc = bass.Bass("TRN2", target_bir_lowering=False)
    es = contextlib.ExitStack()
    k = K(nc, es)
    P = k.P
    NT = npt + 1
    NTP = max(len(p) for p in passes)
    NTOKP = NTP * 128

    def din(name, shape):
        return nc.dram_tensor(name, list(shape), F32, kind="ExternalInput").ap()

    def dout(name, shape):
        return nc.dram_tensor(name, list(shape), F32, kind="ExternalOutput").ap()

    xin = din("xin", [NT * 128, D])
    st_in = {"hg": din("st_hg", [DEPTH, 16, 4, 64, 64]), "gla": din("st_gla", [DEPTH, 16, 4, 32, 64]),
             "rw": din("st_rw", [DEPTH, 16, 4, 64, 64]), "ret": din("st_ret", [DEPTH, 16, 4, 64, 64])}
    st_shift = din("st_shift", [DEPTH, 16, 1024])
    W = {}
    for name, shape in (("attn_norm_w", [2, 1024]), ("w_in", [2, 1024, N_IN]), ("hg_lb_logits", [2, 256]),
                        ("hg_norm_w", [2, 64]), ("gla_wa2", [2, 16, 128]), ("gla_ba", [2, 128]),
                        ("gla_norm_w", [2, 64]), ("rw_mu", [2, 1024]), ("rw_w0", [2, 256]),
                        ("rw_w2", [2, 64, 256]), ("rw_a0", [2, 256]), ("rw_a2", [2, 64, 256]),
                        ("rw_g2", [2, 128, 256]), ("rw_kk", [2, 256]), ("rw_ka", [2, 256]), ("rw_rk", [2, 256]),
                        ("rw_ln_w", [2, 256]), ("rw_ln_b", [2, 256]), ("w_branch", [2, 4, 256, 1024]),
                        ("w_out", [2, 1024, 1024]), ("ffn_norm_w", [2, 1024]), ("w_ffn_in", [2, 1024, 2 * D_FF]),
                        ("w_ffn_out", [2, D_FF, 1024]), ("final_norm_w", [1024])):
        W[name] = din(name, shape)
    c128_d = din("c128", [128, NC128])
    colmask_d = din("colmask", [128, 2048])
    rot_d = din("rot", [NT, 128, 64])

    yout = dout("yout", [NT * 128, D])
    p_out = {"hg": dout("p_hg", [DEPTH, 4, 64, 64]), "gla": dout("p_gla", [DEPTH, 4, 32, 64]),
             "rw": dout("p_rw", [DEPTH, 4, 64, 64]), "ret": dout("p_ret", [DEPTH, 4, 64, 64])}
    p_shift = dout("p_shift", [DEPTH, 1024])
    s_out = {"hg": dout("s_hg", [DEPTH, 16, 4, 64, 64]), "gla": dout("s_gla", [DEPTH, 16, 4, 32, 64]),
             "rw": dout("s_rw", [DEPTH, 16, 4, 64, 64]), "ret": dout("s_ret", [DEPTH, 16, 4, 64, 64])}
    s_shift = dout("s_shift", [DEPTH, 16, 1024])
    dbg_out = {}
    if debug:
        for name, shape in debug.items():
            dbg_out[name] = dout("dbg_" + name, shape)

    x_t, x_b = k.sb("x", [128, NTP, D], F32, cells=NTP)
    hT_t, hT_b = k.sb("hT", [128, 8, NTOKP], BF16, cells=NTP)
    oT_t, oT_b = k.sb("oT", [128, 8, NTOKP], BF16, cells=NTP * 4)
    c128_t, c128_b = k.sb("c128s", [128, NC128], F32)
    cm_t, cm_b = k.sb("colmask_s", [128, 16, 128], BF16)
    idb_t, idb_b = k.sb("identb", [128, 128], BF16)
    zl_t, zl_b = k.sb("zerol", [128, 128], BF16)
    pb_t, pb_b = k.sb("pbc", [128, NPB], F32, cells=16)
    lbc_t, lbc_b = k.sb("lbc", [128, 2, 256], F32)
    cst_t, cst_b = k.sb("cst", [128, 8], F32)
    rwl_t, rwl_b = k.sb("rwl", [128, 3, 256], BF16)
    gwa_t, gwa_b = k.sb("gwa", [32, 128], BF16)
    S_t, S_b = k.sb("Sst", [128, DEPTH * 4, 2, 64], F32, cells=DEPTH * 4)
    rwp_t, rwp_b = k.sb("rwprev", [128, DEPTH, 1024], F32, cells=DEPTH)
    gconst_t, gconst_b = k.sb("gconst", [128, 256], F32)
    rot_t, rot_b = k.sb("rots", [128, 2, 64], F32, cells=2)
    wa = Arena(nc, es, "warena", 32768, cell=2048)
    sc = Arena(nc, es, "scratch", 57344, cell=256)

    psA = es.enter_context(nc.psum_tensor("psA", [128, 8, 512], F32))
    psA_b = Buf("psA", 8, excl=True)
    rr = {}
    pools = {None: dict(big=[0, 1], small=[4, 5, 6, 7]), "A": dict(big=[0], small=[4, 5]),
             "B": dict(big=[1], small=[6, 7])}

    def ps_big():
        st = P.cur
        lst = pools[st]["big"]
        i = rr.get((st, "big"), 0)
        rr[(st, "big")] = (i + 1) % len(lst)
        b = lst[i]
        ap = psA[:, 2 * b:2 * b + 2, :].rearrange("p a b -> p (a b)")
        return T(ap, (psA_b, (2 * b, 2 * b + 1)))

    def ps_small():
        st = P.cur
        lst = pools[st]["small"]
        i = rr.get((st, "small"), 0)
        rr[(st, "small")] = (i + 1) % len(lst)
        b = lst[i]
        return T(psA[:, b, :], (psA_b, (b,)))

    def ps_tb():
        t = ps_small()
        return T(t.ap.bitcast(BF16), t.d)

    ident = T(idb_t[:], idb_b.all)

    def cst(i):
        return cst_t[:, i:i + 1]

    k.DMA(c128_t[:], c128_d[:, :], w=[c128_b])
    k.DMA(cm_t[:].rearrange("p a b -> p (a b)"), colmask_d[:, :], w=[cm_b], iss="pool")
    k.DMA(idb_t[:], c128_d[:, C_ID:C_ID + 128], w=[idb_b], iss="pool")
    k.TS(zl_t[:], idb_t[:], 0.0, ALU.mult, r=[idb_b], w=[zl_b])
    CST_EPSD, CST_ONE, CST_LNH, CST_EPS64, CST_EPSLN, CST_TINY = 0, 1, 2, 3, 4, 5
    k.MEMSET(cst_t[:, 0:1], D * NORM_EPS, w=[cst_b])
    k.MEMSET(cst_t[:, 1:2], 1.0, w=[cst_b])
    k.MEMSET(cst_t[:, 2:3], math.log(0.5), w=[cst_b])
    k.MEMSET(cst_t[:, 3:4], NORM_EPS, w=[cst_b])
    k.MEMSET(cst_t[:, 4:5], RW_LN_EPS, w=[cst_b])
    k.MEMSET(cst_t[:, 5:6], 1e-24, w=[cst_b])
    for h in range(4):
        k.MEMSET(gconst_t[:, 64 * h:64 * h + 64], math.log1p(-2.0 ** (-5.0 - h)), w=[gconst_b])
    k.MEMSET(S_t[:].rearrange("p a b c -> p (a b c)"), 0.0, w=[S_b])
    k.MEMSET(rwp_t[:].rearrange("p a b -> p (a b)"), 0.0, w=[rwp_b])
    cmat = {"p": dict(mcum=C_MCP, ti=C_TIP, ts=C_TSP, low=C_LP, ind=C_INDP, nind=4),
            "s": dict(mcum=C_MCS, ti=C_TIS, ts=C_TSS, low=C_LS, ind=C_INDS, nind=32)}

    def cm(kind, name):
        o = cmat[kind][name]
        n = cmat[kind]["nind"] if name == "ind" else 128
        return c128_t[:, o:o + n]

    ctx = dict(nc=nc, k=k, P=P, npt=npt, NTP=NTP, NTOKP=NTOKP, W=W, st_in=st_in, st_shift=st_shift,
               p_out=p_out, p_shift=p_shift, s_out=s_out, s_shift=s_shift, x_t=x_t, x_b=x_b, hT_t=hT_t, hT_b=hT_b,
               oT_t=oT_t, oT_b=oT_b, c128_t=c128_t, c128_b=c128_b, cm_t=cm_t, cm_b=cm_b, ident=ident, zl=T(zl_t[:], zl_b.all),
               pb_t=pb_t, pb_b=pb_b, lbc_t=lbc_t, lbc_b=lbc_b, cst=cst, cst_b=cst_b, rwl_t=rwl_t, rwl_b=rwl_b,
               gwa_t=gwa_t, gwa_b=gwa_b, S_t=S_t, S_b=S_b, rwp_t=rwp_t, rwp_b=rwp_b, gconst_t=gconst_t,
               gconst_b=gconst_b, rot_t=rot_t, rot_b=rot_b, rot_d=rot_d, wa=wa, sc=sc, ps_big=ps_big,
               ps_small=ps_small, ps_tb=ps_tb, cm=cm, dbg_out=dbg_out, xin=xin, yout=yout,
               CST=dict(EPSD=0, ONE=1, LNH=2, EPS64=3, EPSLN=4, TINY=5), mixers=mixers, depth=depth)
    g = Gen(ctx)

    for pi, tiles in enumerate(passes):
        g.load_x(tiles)
        g.final_done = False
        g.attn_norm_done = False
        for l in range(depth):
            g.layer(l, tiles, pi)
        if not g.final_done:
            g.final(tiles)
    P.finish()
    P.emit_all()
    es.close()
    return nc


class Gen:
    def __init__(self, ctx):
        self.__dict__.update(ctx)

    def tile_kind(self, t):
        return "p" if t < self.npt else "s"

    def pbc(self, off, n):
        return T(self.pb_t[:, off:off + n], (self.pb_b, tuple(range(off // 288, (off + n - 1) // 288 + 1))))

    def load_pb(self, off, src_row):
        n = src_row.shape[0]
        t = self.pbc(off, n)
        self.k.DMA(t.ap, src_row.partition_broadcast(128), w=[t])
        return t

    def load_x(self, tiles):
        k = self.k
        for j, t in enumerate(tiles):
            k.DMA(self.x_t[:, j, :], self.xin[t * 128:(t + 1) * 128, :], w=[self.x_b[j]])

    def rms_stats(self, j, sc_junk, rstd):
        k = self.k
        ss = self.sc.alloc([1], F32)
        lnv = self.sc.alloc([1], F32)
        k.ACT(sc_junk.ap, self.x_t[:, j, :], AF.Square, r=[self.x_b[j]], w=[sc_junk, ss], accum=ss.ap)
        k.ACT(lnv.ap, ss.ap, AF.Ln, r=[ss, self.cst_b], w=[lnv], bias=self.cst(self.CST["EPSD"]))
        k.ACT(rstd.ap, lnv.ap, AF.Exp, r=[lnv], w=[rstd], scale=-0.5)

    def norm_prep(self, wrow):
        k = self.k
        wt = self.load_pb(PB_NORM, wrow)
        k.TS(wt.ap, wt.ap, float(math.sqrt(D)), ALU.mult, r=[wt], w=[wt])
        return wt

    def norm_alloc(self):
        return [dict(junk=self.sc.alloc([D], F32), hb=self.sc.alloc([D], BF16), rstd=self.sc.alloc([1], F32),
                     ss=self.sc.alloc([1], F32), lnv=self.sc.alloc([1], F32), y=None) for _ in range(2)]

    def rms_stats2(self, j, s):
        k = self.k
        k.ACT(s["junk"].ap, self.x_t[:, j, :], AF.Square, r=[self.x_b[j]], w=[s["junk"], s["ss"]], accum=s["ss"].ap)
        k.ACT(s["lnv"].ap, s["ss"].ap, AF.Ln, r=[s["ss"], self.cst_b], w=[s["lnv"]], bias=self.cst(self.CST["EPSD"]))
        k.ACT(s["rstd"].ap, s["lnv"].ap, AF.Exp, r=[s["lnv"]], w=[s["rstd"]], scale=-0.5)

    def norm_tile(self, j, wt, s, part="ab"):
        k = self.k
        if "a" in part:
            self.rms_stats2(j, s)
            k.STT(s["hb"].ap, self.x_t[:, j, :], s["rstd"].ap[:, 0:1], wt.ap, ALU.mult, ALU.mult,
                  r=[self.x_b[j], s["rstd"], wt], w=[s["hb"]])
        if "b" not in part:
            return
        tb = self.ps_tb()
        for kt in range(8):
            k.TR(tb.ap[:, kt * 128:(kt + 1) * 128], s["hb"].ap[:, kt * 128:(kt + 1) * 128], self.ident.ap,
                 r=[s["hb"], self.ident], w=[tb], inc=(kt == 7))
        k.CP(self.hT_t[:, :, j * 128:(j + 1) * 128], tb.ap.rearrange("p (a b) -> p a b", a=8), r=[tb],
             w=[self.hT_b[j]], eng=("act" if j % 2 == 0 else "dve"))

    def lagged_norm(self, wt, sets):
        prev = [None]

        def step(j):
            self.norm_tile(j, wt, sets[j % 2], part="a")
            if prev[0] is not None:
                self.norm_tile(prev[0], wt, sets[prev[0] % 2], part="b")
            prev[0] = j

        def flush():
            if prev[0] is not None:
                self.norm_tile(prev[0], wt, sets[prev[0] % 2], part="b")
                prev[0] = None
        return step, flush

    def norm_phase(self, tiles, wrow):
        wt = self.norm_prep(wrow)
        m = self.sc.mark()
        sets = self.norm_alloc()
        for j, t in enumerate(tiles):
            self.norm_tile(j, wt, sets[j % 2])
        self.sc.reset(m)

    def final_tile(self, j, t, wt, s):
        k = self.k
        self.rms_stats2(j, s)
        y = s["junk"]
        k.STT(y.ap, self.x_t[:, j, :], s["rstd"].ap[:, 0:1], wt.ap, ALU.mult, ALU.mult,
              r=[self.x_b[j], s["rstd"], wt], w=[y])
        k.DMA(self.yout[t * 128:(t + 1) * 128, :], y.ap, r=[y], is_output=True)

    def final(self, tiles):
        wt = self.norm_prep(self.W["final_norm_w"])
        m = self.sc.mark()
        sets = self.norm_alloc()
        for j, t in enumerate(tiles):
            self.final_tile(j, t, wt, sets[j % 2])
        self.sc.reset(m)

    def tgroups(self, ntiles):
        ng = (ntiles + 3) // 4
        base, extra = divmod(ntiles, ng)
        gs = []
        j = 0
        for i in range(ng):
            n = base + (1 if i < extra else 0)
            gs.append((j, n))
            j += n
        return gs

    def ffn_phase(self, l, tiles, after=None):
        k = self.k
        nt = len(tiles)
        if not self.ffn_norm_done:
            self.norm_phase(tiles, self.W["ffn_norm_w"][l])
        m = self.sc.mark()
        after_fn = after() if after is not None else None
        sgs = [self.sc.alloc([512], F32) for _ in range(2)]
        acts = [self.sc.alloc([2, 512], BF16) for _ in range(3)]
        nchunk = D_FF // 256
        wfi = self.W["w_ffn_in"][l].rearrange("(kt p) n -> p kt n", p=128)
        wfo = self.W["w_ffn_out"][l].rearrange("(s p) n -> p s n", p=128)
        cnt = 0
        pending = None

        def emit_y(act, wo, j0, n, last, lo=0, hi=None):
            hi = n if hi is None else hi
            for j in range(j0 + lo, j0 + hi):
                yp = self.ps_big()
                cc = (j - j0) * 128
                for half in range(2):
                    for s in range(2):
                        k.MM(yp.ap[:, half * 512:(half + 1) * 512], act.ap[:, s, cc:cc + 128],
                             wo.ap[:, s, half * 512:(half + 1) * 512], start=(s == 0), stop=(s == 1),
                             r=[act, wo], w=[yp], inc=(half == 1 and s == 1))
                k.TT(self.x_t[:, j, :], self.x_t[:, j, :], yp.ap, ALU.add, r=[self.x_b[j], yp], w=[self.x_b[j]])
                if last and after_fn is not None:
                    after_fn(j)

        for c in range(nchunk):
            slot = c % 2
            base = slot * 12288
            wg = self.wa.view(base, [8, 256], BF16)
            wu = self.wa.view(base + 4096, [8, 256], BF16)
            wo = self.wa.view(base + 8192, [2, 1024], BF16)
            k.DMA(wg.ap, wfi[:, :, c * 256:(c + 1) * 256], w=[wg], iss="pool")
            k.DMA(wu.ap, wfi[:, :, D_FF + c * 256:D_FF + (c + 1) * 256], w=[wu], iss="pool")
            k.DMA(wo.ap, wfo[:, 2 * c:2 * c + 2, :], w=[wo], iss="pool")
            for (j0, n) in self.tgroups(nt):
                ntok = n * 128
                c0 = j0 * 128
                hdeps = [self.hT_b[j] for j in range(j0, j0 + n)]
                act = acts[cnt % 3]
                cnt += 1
                for s in range(2):
                    gp = self.ps_small()
                    up = self.ps_small()
                    for kt in range(8):
                        k.MM(gp.ap[:, :ntok], wg.ap[:, kt, s * 128:(s + 1) * 128], self.hT_t[:, kt, c0:c0 + ntok],
                             start=(kt == 0), stop=(kt == 7), r=[wg] + hdeps, w=[gp], inc=(kt == 7))
                    for kt in range(8):
                        k.MM(up.ap[:, :ntok], wu.ap[:, kt, s * 128:(s + 1) * 128], self.hT_t[:, kt, c0:c0 + ntok],
                             start=(kt == 0), stop=(kt == 7), r=[wu] + hdeps, w=[up], inc=(kt == 7))
                    sg = sgs[s]
                    k.ACT(sg.ap[:, :ntok], gp.ap[:, :ntok], AF.Silu, r=[gp], w=[sg])
                    k.TT(act.ap[:, s, :ntok], sg.ap[:, :ntok], up.ap[:, :ntok], ALU.mult, r=[sg, up], w=[act])
                    if pending is not None:
                        hp = (pending[3] + 1) // 2
                        if s == 0:
                            emit_y(*pending, lo=0, hi=hp)
                        else:
                            emit_y(*pending, lo=hp, hi=pending[3])
                pending = (act, wo, j0, n, c == nchunk - 1)
        if pending is not None:
            emit_y(*pending)
        if getattr(self, "after_flush", None) is not None:
            self.after_flush()
            self.after_flush = None
        self.sc.reset(m)

    def merge_phase(self, l, tiles):
        k = self.k
        nt = len(tiles)
        m = self.sc.mark()
        mT = self.sc.alloc([8, nt * 128], BF16)
        sgs = [self.sc.alloc([512], F32) for _ in range(2)]
        tmps = [self.sc.alloc([512], F32) for _ in range(2)]
        accs = [self.sc.alloc([512], F32) for _ in range(2)]
        w_in = self.W["w_in"][l]
        wgate_src = w_in[:, C_GATE:C_GATE + 4096].rearrange("(kt p) (b d) -> p kt b d", p=128, b=4)
        wbr_src = self.W["w_branch"][l].rearrange("b (ct p) d -> p ct b d", p=128)
        cnt = 0
        for ds in range(8):
            base = (ds % 2) * 12288
            wg = self.wa.view(base, [8, 4, 128], BF16)
            wb = self.wa.view(base + 8192, [2, 4, 128], BF16)
            for b in range(4):
                k.DMA(wg.ap[:, :, b, :], wgate_src[:, :, b, ds * 128:(ds + 1) * 128], w=[wg], iss="pool")
                k.DMA(wb.ap[:, :, b, :], wbr_src[:, :, b, ds * 128:(ds + 1) * 128], w=[wb], iss="pool")
            for (j0, n) in self.tgroups(nt):
                ntok = n * 128
                c0 = j0 * 128
                hdeps = [self.hT_b[j] for j in range(j0, j0 + n)]
                acc = accs[cnt % 2]
                cnt += 1
                for b in range(4):
                    odeps = [self.oT_b[j * 4 + b] for j in range(j0, j0 + n)]
                    gp = self.ps_small()
                    for kt in range(8):
                        k.MM(gp.ap[:, :ntok], wg.ap[:, kt, b, :], self.hT_t[:, kt, c0:c0 + ntok],
                             start=(kt == 0), stop=(kt == 7), r=[wg] + hdeps, w=[gp], inc=(kt == 7))
                    up = self.ps_small()
                    for ct in range(2):
                        k.MM(up.ap[:, :ntok], wb.ap[:, ct, b, :], self.oT_t[:, 2 * b + ct, c0:c0 + ntok],
                             start=(ct == 0), stop=(ct == 1), r=[wb] + odeps, w=[up], inc=(ct == 1))
                    sg = sgs[b % 2]
                    k.ACT(sg.ap[:, :ntok], gp.ap[:, :ntok], AF.Tanh, r=[gp], w=[sg], scale=0.5)
                    if b == 0:
                        k.STT(acc.ap[:, :ntok], sg.ap[:, :ntok], 1.0, up.ap[:, :ntok], ALU.add, ALU.mult,
                              r=[sg, up], w=[acc])
                    else:
                        tmp = tmps[b % 2]
                        k.STT(tmp.ap[:, :ntok], sg.ap[:, :ntok], 1.0, up.ap[:, :ntok], ALU.add, ALU.mult,
                              r=[sg, up], w=[tmp])
                        k.TT(acc.ap[:, :ntok], acc.ap[:, :ntok], tmp.ap[:, :ntok], ALU.add, r=[acc, tmp], w=[acc])
                k.ACT(mT.ap[:, ds, c0:c0 + ntok], acc.ap[:, :ntok], AF.Copy, r=[acc], w=[mT], scale=0.5)
        wt_n = self.norm_prep(self.W["ffn_norm_w"][l])
        nstep, nflush = self.lagged_norm(wt_n, self.norm_alloc())
        wo = self.wa.view(0, [8, 1024], BF16)
        k.DMA(wo.ap, self.W["w_out"][l].rearrange("(kt p) n -> p kt n", p=128), w=[wo], iss="pool")
        for j in range(nt):
            yp = self.ps_big()
            for half in range(2):
                for kt in range(8):
                    k.MM(yp.ap[:, half * 512:(half + 1) * 512], mT.ap[:, kt, j * 128:(j + 1) * 128],
                         wo.ap[:, kt, half * 512:(half + 1) * 512], start=(kt == 0), stop=(kt == 7),
                         r=[mT, wo], w=[yp], inc=(half == 1 and kt == 7))
            k.TT(self.x_t[:, j, :], self.x_t[:, j, :], yp.ap, ALU.add, r=[self.x_b[j], yp], w=[self.x_b[j]])
            nstep(j)
        nflush()
        self.ffn_norm_done = True
        self.sc.reset(m)

    def layer(self, l, tiles, pi):
        k = self.k
        ph = os.environ.get("PHASES", "nlmgf")
        self.ffn_norm_done = False
        if "n" in ph and not getattr(self, "attn_norm_done", False):
            self.norm_phase(tiles, self.W["attn_norm_w"][l])
        self.attn_norm_done = False
        if "l" in ph:
            self.load_layer_params(l)
        if "m" not in ph:
            if "g" in ph:
                self.merge_phase(l, tiles)
            if "f" in ph:
                self.ffn_phase(l, tiles)
            return
        jts = list(enumerate(tiles))
        pj = [(j, t) for (j, t) in jts if t < self.npt]
        sj = [(j, t) for (j, t) in jts if t >= self.npt]
        for mi, name in enumerate(("hg", "gla", "rw", "ret")):
            if name not in self.mixers:
                for j in range(len(tiles)):
                    k.TS(self.oT_t[:, 2 * mi:2 * mi + 2, j * 128:(j + 1) * 128], self.hT_t[:, 0:2, j * 128:(j + 1) * 128],
                         0.0, ALU.mult, r=[self.hT_b[j]], w=[self.oT_b[j * 4 + mi]])
        two_stream = all(n in self.mixers for n in ("hg", "gla", "rw", "ret")) and len(pj) > 0 \
            and os.environ.get("NO_TWO_STREAM") is None
        if two_stream:
            m0 = self.sc.mark()
            P = self.k.P
            P.cur = "A"
            for _ in self.mixer_rw(l, pj, 2, wslot=1):
                pass
            P.cur = "B"
            self.sc.top = self.rw_top
            for mname, mi_ in (("hg", 0), ("gla", 1), ("ret", 3)):
                for _ in getattr(self, "mixer_" + mname)(l, pj, mi_, nsets=1, wslot=0):
                    pass
            P.cur = None
            P.merge_streams()
            self.sc.reset(m0)
            rest = sj
        else:
            rest = jts
        if rest:
            for mi, name in enumerate(("hg", "gla", "rw", "ret")):
                if name in self.mixers:
                    for _ in getattr(self, "mixer_" + name)(l, rest, mi, nsets=(1 if two_stream else 2)):
                        pass
        self.merge_phase(l, tiles)

        def after():
            last = (l == self.depth - 1)
            wt = self.norm_prep(self.W["final_norm_w"] if last else self.W["attn_norm_w"][l + 1])
            sets = self.norm_alloc()
            if last:
                self.final_done = True
                return lambda j: self.final_tile(j, tiles[j], wt, sets[j % 2])
            self.attn_norm_done = True
            step, flush = self.lagged_norm(wt, sets)
            self.after_flush = flush
            return step
        self.ffn_phase(l, tiles, after=after)


    def load_layer_params(self, l):
        k = self.k
        W = self.W
        pb = {}
        pb["mu"] = self.load_pb(PB_MU, W["rw_mu"][l])
        k.TS(pb["mu"].ap, pb["mu"].ap, -1.0, ALU.mult, 1.0, ALU.add, r=[pb["mu"]], w=[pb["mu"]])
        pb["hgn"] = self.load_pb(PB_HGN, W["hg_norm_w"][l])
        pb["gba"] = self.load_pb(PB_GBA, W["gla_ba"][l])
        pb["gln"] = self.load_pb(PB_GLN, W["gla_norm_w"][l])
        for nm, off in (("rw_w0", PB_W0), ("rw_a0", PB_A0), ("rw_kk", PB_KK), ("rw_ka", PB_KA), ("rw_rk", PB_RK),
                        ("rw_ln_w", PB_LNW), ("rw_ln_b", PB_LNB)):
            pb[nm] = self.load_pb(off, W[nm][l])
        self.pb = pb
        lbc = T(self.lbc_t[:], self.lbc_b.all)
        if l == 0:
            k.MEMSET(self.lbc_t[:].rearrange("p a b -> p (a b)"), 0.5, w=[lbc])
        else:
            assert DEPTH == 2
            l0 = self.load_pb(PB_LB, W["hg_lb_logits"][0])
            l1 = self.load_pb(PB_LB + 256, W["hg_lb_logits"][1])
            k.TT(l0.ap, l1.ap, l0.ap, ALU.subtract, r=[l0, l1], w=[l0])
            k.ACT(l0.ap, l0.ap, AF.Tanh, r=[l0], w=[l0], scale=0.5)
            k.TS(self.lbc_t[:, 0, :], l0.ap, 0.25, ALU.mult, 0.75, ALU.add, r=[l0], w=[lbc])
            k.TS(self.lbc_t[:, 1, :], l0.ap, -0.25, ALU.mult, 0.25, ALU.add, r=[l0], w=[lbc])
        rwl = T(self.rwl_t[:], self.rwl_b.all)
        k.DMA(self.rwl_t[0:64, 0, :], W["rw_w2"][l], w=[rwl], iss="pool")
        k.DMA(self.rwl_t[64:128, 1, :], W["rw_a2"][l], w=[rwl], iss="pool")
        k.DMA(self.rwl_t[:, 2, :], W["rw_g2"][l], w=[rwl], iss="pool")
        k.TS(self.gwa_t[:], self.ident.ap[0:32, :], 0.0, ALU.mult, r=[self.ident], w=[self.gwa_b])
        k.DMA(self.gwa_t[0:16, :], W["gla_wa2"][l], w=[self.gwa_b], iss="pool")

    def load_mixer_w(self, l, mi, c0, ncols):
        wv = self.wa.view((mi % 2) * 16384, [8, 1024], BF16)
        src = self.W["w_in"][l][:, c0:c0 + ncols].rearrange("(kt p) n -> p kt n", p=128)
        self.k.DMA(wv.ap[:, :, 0:ncols], src, w=[wv], iss="pool")
        return wv

    def project(self, wv, j, ncols):
        k = self.k
        pp = self.ps_big()
        c = 0
        while c < ncols:
            n = min(512, ncols - c)
            for kt in range(8):
                k.MM(pp.ap[:, c:c + n], self.hT_t[:, kt, j * 128:(j + 1) * 128], wv.ap[:, kt, c:c + n],
                     start=(kt == 0), stop=(kt == 7), r=[self.hT_b[j], wv], w=[pp], inc=(kt == 7))
            c += n
        return pp

    def gla_sets(self, n):
        sets = []
        for _ in range(n):
            d = {}
            for nm in ("a0", "a1", "a2", "a3", "a4", "a5", "g", "Ep", "Em", "gsb", "tg"):
                d[nm] = self.sc.alloc([256], F32)
            d["dd"] = self.sc.alloc([2, 32], F32)
            d["KVd"] = self.sc.alloc([2, 2, 64], F32)
            d["Smid"] = self.sc.alloc([2, 64], F32)
            d["tmp"] = self.sc.alloc([2, 64], F32)
            d["st4"] = self.sc.alloc([8], F32)
            for nm in ("qt", "kt", "vbf", "ob"):
                d[nm] = self.sc.alloc([256], BF16)
            d["qkT"] = self.sc.alloc([4, 128], BF16)
            d["AT"] = self.sc.alloc([4, 128], BF16)
            sets.append(d)
        return sets

    def S_view(self, l, mi):
        return T(self.S_t[:, l * 4 + mi, :, :], self.S_b[l * 4 + mi])

    def gla_A(self, s, kind, q, kk, g, S, smidB, cidx, samp=None):
        k = self.k
        nind = 4 if kind == "p" else 32
        cst = [T(self.c128_t[:], self.c128_b.all)]
        bp = self.ps_small()
        k.MM(bp.ap[:, 0:256], self.cm(kind, "mcum"), g.ap, r=cst + [g], w=[bp], inc=False)
        for jj in range(2):
            k.MM(bp.ap[:, 256 + jj * nind:256 + (jj + 1) * nind], g.ap[:, jj * 128:(jj + 1) * 128],
                 self.cm(kind, "ind"), r=cst + [g], w=[bp], inc=(jj == 1))
        yield
        k.ACT(s["Ep"].ap, bp.ap[:, 0:256], AF.Exp, r=[bp], w=[s["Ep"]])
        k.ACT(s["Em"].ap, bp.ap[:, 0:256], AF.Exp, r=[bp], w=[s["Em"]], scale=-1.0)
        dd = s["dd"]
        k.ACT(dd.ap[:, :, 0:nind], bp.ap[:, 256:256 + 2 * nind].rearrange("p (a b) -> p a b", a=2), AF.Exp,
              r=[bp], w=[dd])
        k.TT(s["qt"].ap, q.ap, s["Ep"].ap, ALU.mult, r=[q, s["Ep"]], w=[s["qt"]])
        k.TT(s["kt"].ap, kk.ap, s["Em"].ap, ALU.mult, r=[kk, s["Em"]], w=[s["kt"]])
        yield
        tb = self.ps_tb()
        for i, src in enumerate((s["qt"], s["qt"], s["kt"], s["kt"])):
            k.TR(tb.ap[:, i * 128:(i + 1) * 128], src.ap[:, (i % 2) * 128:(i % 2 + 1) * 128], self.ident.ap,
                 r=[src, self.ident], w=[tb], inc=(i == 3))
        qkT = s["qkT"]
        k.CP(qkT.ap, tb.ap[:, 0:512].rearrange("p (a b) -> p a b", a=4), r=[tb], w=[qkT], eng="act")
        yield
        scps = [self.ps_small(), self.ps_small()]
        for hl in range(2):
            for jj in range(2):
                k.MM(scps[hl].ap[:, jj * 128:(jj + 1) * 128], qkT.ap[64 * hl:64 * hl + 64, 2 + jj, :],
                     qkT.ap[64 * hl:64 * hl + 64, jj, :], r=[qkT], w=[scps[hl]], inc=(jj == 1))
        mask = self.cm(kind, "ti")
        AT4 = s["AT"].ap.rearrange("p (j h) t -> p j h t", j=2)
        for hl in range(2):
            k.TT(AT4[:, :, hl, :], scps[hl].ap[:, 0:256].rearrange("p (a b) -> p a b", a=2),
                 mask.unsqueeze(1).to_broadcast([128, 2, 128]), ALU.mult, r=[scps[hl]] + cst, w=[s["AT"]])
        kt, vbf = s["kt"], s["vbf"]
        yield
        if kind == "p":
            KVd = s["KVd"]
            kvps = [self.ps_small(), self.ps_small()]
            for c in range(2):
                for jj in range(2):
                    k.MM(kvps[c].ap[:, jj * 128:(jj + 1) * 128],
                         kt.ap[64 * c:64 * c + 64, jj * 128:(jj + 1) * 128],
                         vbf.ap[64 * c:64 * c + 64, jj * 128:(jj + 1) * 128], r=[kt, vbf], w=[kvps[c]],
                         inc=(jj == 1))
            for c in range(2):
                kv2 = kvps[c].ap[:, 0:256].rearrange("p (a b) -> p a b", a=2)
                k.CP(KVd.ap[0:64, c], kv2[0:64, :, 0:64], r=[kvps[c]], w=[KVd])
                k.CP(KVd.ap[64:128, c], kv2[64:128, :, 64:128], r=[kvps[c]], w=[KVd], eng="act")
            yield
            for c in range(2):
                d1 = dd.ap[:, :, 2 * c:2 * c + 1].to_broadcast([128, 2, 64])
                d2 = dd.ap[:, :, 2 * c + 1:2 * c + 2].to_broadcast([128, 2, 64])
                k.TT(s["Smid"].ap, S.ap, d1, ALU.mult, r=[S, dd], w=[s["Smid"]])
                k.CP(smidB.ap[:, cidx + c], s["Smid"].ap, r=[s["Smid"]], w=[smidB], eng="act")
                k.TT(s["tmp"].ap, s["Smid"].ap, KVd.ap[:, c], ALU.add, r=[s["Smid"], KVd], w=[s["tmp"]])
                k.TT(S.ap, s["tmp"].ap, d2, ALU.mult, r=[s["tmp"], dd], w=[S])
        else:
            S0, KVd, SB = samp["S0"], samp["KVd"], samp["SmidB"]
            ddv = dd.ap.rearrange("p j (q t) -> p q j t", t=2)
            ktm = samp["ktm"]
            for hf in range(2):
                q0 = 8 * hf
                d1 = ddv[:, q0:q0 + 8, :, 0:1].to_broadcast([128, 8, 2, 64])
                d2 = ddv[:, q0:q0 + 8, :, 1:2].to_broadcast([128, 8, 2, 64])
                samp["load"](hf)
                k.TT(S0.ap, S0.ap, d1, ALU.mult, r=[S0, dd], w=[S0])
                k.CP(SB.ap[:, q0:q0 + 8], S0.ap, r=[S0], w=[SB], eng="act")
                for q2 in range(4):
                    kvp = self.ps_small()
                    for u in range(2):
                        sq = q0 + 2 * q2 + u
                        km = ktm[sq % len(ktm)]
                        k.TS(km.ap, kt.ap, self.c128_t[:, C_ROWS + sq:C_ROWS + sq + 1], ALU.mult, r=[kt] + cst,
                             w=[km])
                        for jj in range(2):
                            k.MM(kvp.ap[:, (u * 2 + jj) * 128:(u * 2 + jj + 1) * 128],
                                 km.ap[:, jj * 128:(jj + 1) * 128], vbf.ap[:, jj * 128:(jj + 1) * 128],
                                 r=[km, vbf], w=[kvp], inc=(u == 1 and jj == 1))
                    kv4 = kvp.ap.rearrange("p (a b) -> p a b", a=4)
                    kd4 = KVd.ap[:, 2 * q2:2 * q2 + 2].rearrange("p a b c -> p (a b) c")
                    k.CP(kd4[0:64], kv4[0:64, :, 0:64], r=[kvp], w=[KVd])
                    k.CP(kd4[64:128], kv4[64:128, :, 64:128], r=[kvp], w=[KVd], eng="act")
                k.TT(KVd.ap, KVd.ap, S0.ap, ALU.add, r=[KVd, S0], w=[KVd])
                k.TT(KVd.ap, KVd.ap, d2, ALU.mult, r=[KVd, dd], w=[KVd])
                samp["store"](hf)

    def gla_B(self, s, kind, smidB, cidx, samp=None):
        k = self.k
        op_ = self.ps_small()
        AT, vbf, qkT = s["AT"], s["vbf"], s["qkT"]
        if kind == "p":
            for c in range(2):
                for h in range(4):
                    jj, hl = h // 2, h % 2
                    out = op_.ap[64 * c:64 * c + 64, h * 64:(h + 1) * 64]
                    k.MM(out, AT.ap[:, h, 64 * c:64 * c + 64], vbf.ap[:, h * 64:(h + 1) * 64], start=True, stop=False,
                         r=[AT, vbf], w=[op_], inc=False)
                    k.MM(out, qkT.ap[64 * hl:64 * hl + 64, jj, 64 * c:64 * c + 64],
                         smidB.ap[64 * hl:64 * hl + 64, cidx + c, jj, :], start=False, stop=True,
                         r=[qkT, smidB], w=[op_], inc=(c == 1 and h == 3))
        else:
            SB, qtm = samp["SmidB"], samp["qtm"]
            cmd = [T(self.cm_t[:], self.cm_b.all)]
            n = 0
            for h in range(4):
                jj, hl = h // 2, h % 2
                out = op_.ap[:, h * 64:(h + 1) * 64]
                k.MM(out, AT.ap[:, h, :], vbf.ap[:, h * 64:(h + 1) * 64], start=True, stop=False,
                     r=[AT, vbf], w=[op_], inc=False)
                for sq in range(16):
                    qm = qtm[n % len(qtm)]
                    n += 1
                    k.TT(qm.ap[64 * hl:64 * hl + 64, :], qkT.ap[64 * hl:64 * hl + 64, jj, :],
                         self.cm_t[64 * hl:64 * hl + 64, sq, :], ALU.mult, r=[qkT] + cmd, w=[qm])
                    k.MM(out, qm.ap[64 * hl:64 * hl + 64, :], SB.ap[64 * hl:64 * hl + 64, sq, jj, :], start=False,
                         stop=(sq == 15), r=[qm, SB], w=[op_], inc=True)
        return op_

    def samp_alloc(self, name, l, dk):
        k = self.k
        d = dict(S0=self.sc.alloc([8, 2, 64], F32), KVd=self.sc.alloc([8, 2, 64], F32),
                 SmidB=self.sc.alloc([16, 2, 64], BF16), ktm=[self.sc.alloc([256], BF16) for _ in range(3)],
                 qtm=[self.sc.alloc([128], BF16) for _ in range(4)])
        S0, Sn = d["S0"], d["KVd"]
        src = self.st_in[name][l]
        dst = self.s_out[name][l]

        def load(hf):
            if dk < 64:
                k.MEMSET(S0.ap.rearrange("p a b c -> p (a b c)"), 0.0, w=[S0])
            for hl in range(2):
                for jj in range(2):
                    k.DMA(S0.ap[64 * hl:64 * hl + dk, :, jj, :],
                          src[8 * hf:8 * hf + 8, 2 * jj + hl].rearrange("s k v -> k s v"), w=[S0])

        def store(hf):
            for hl in range(2):
                for jj in range(2):
                    k.DMA(dst[8 * hf:8 * hf + 8, 2 * jj + hl].rearrange("s k v -> k s v"),
                          Sn.ap[64 * hl:64 * hl + dk, :, jj, :], r=[Sn], is_output=True)
        d["load"], d["store"] = load, store
        return d

    def store_state_p(self, name, l, S, dk):
        k = self.k
        dst = self.p_out[name][l]
        for hl in range(2):
            for jj in range(2):
                k.DMA(dst[2 * jj + hl], S.ap[64 * hl:64 * hl + dk, jj, :], r=[S], is_output=True)

    def post_rms(self, s, o_ps, normw, j, mi):
        k = self.k
        osb, sq, on, tg = s["a0"], s["a1"], s["a2"], s["tg"]
        st = s["st4"]
        gate_ap, gate_dep = s["gsb"].ap, s["gsb"]
        k.CP(osb.ap, o_ps.ap[:, 0:256], r=[o_ps], w=[osb], eng="act")
        k.TT(sq.ap, osb.ap, osb.ap, ALU.mult, r=[osb], w=[sq])
        k.RSUM(st.ap[:, 0:4], sq.ap.rearrange("p (a b) -> p a b", a=4), r=[sq], w=[st])
        k.ACT(st.ap[:, 4:8], st.ap[:, 0:4], AF.Ln, r=[st, self.cst_b], w=[st], bias=self.cst(self.CST["EPS64"]),
              scale=1.0 / 64)
        k.ACT(st.ap[:, 0:4], st.ap[:, 4:8], AF.Exp, r=[st, self.cst_b], w=[st], bias=self.cst(self.CST["LNH"]),
              scale=-0.5)
        yield
        k.TT(on.ap.rearrange("p (a b) -> p a b", a=4), osb.ap.rearrange("p (a b) -> p a b", a=4),
             st.ap[:, 0:4].unsqueeze(2).to_broadcast([128, 4, 64]), ALU.mult, r=[osb, st], w=[on])
        if normw is not None:
            k.TT(on.ap.rearrange("p (a b) -> p a b", a=4), on.ap.rearrange("p (a b) -> p a b", a=4),
                 normw.ap.unsqueeze(1).to_broadcast([128, 4, 64]), ALU.mult, r=[on, normw], w=[on])
        k.STT(tg.ap, tg.ap, 1.0, gate_ap, ALU.add, ALU.mult, r=[tg, gate_dep], w=[tg])
        k.TT(s["ob"].ap, on.ap, tg.ap, ALU.mult, r=[on, tg], w=[s["ob"]])
        yield
        self.to_oT(s["ob"], j, mi)

    def to_oT(self, ob, j, mi):
        k = self.k
        tb = self.ps_tb()
        for i in range(2):
            k.TR(tb.ap[:, i * 128:(i + 1) * 128], ob.ap[:, i * 128:(i + 1) * 128], self.ident.ap,
                 r=[ob, self.ident], w=[tb], inc=(i == 1))
        k.CP(self.oT_t[:, 2 * mi:2 * mi + 2, j * 128:(j + 1) * 128], tb.ap[:, 0:256].rearrange("p (a b) -> p a b", a=2),
             r=[tb], w=[self.oT_b[j * 4 + mi]])

    def gla_stream(self, name, l, jts, mi, c0, ncols, dk, pre, post, init=None, nsets=2, wslot=None):
        k = self.k
        m = self.sc.mark()
        wv = self.load_mixer_w(l, mi if wslot is None else wslot, c0, ncols)
        sets = self.gla_sets(nsets)
        if init is not None:
            init(sets)
        tiles = [t for (_, t) in jts]
        smidB = self.sc.alloc([2 * nsets, 2, 64], BF16)
        has_s = any(t >= self.npt for t in tiles)
        samp = None
        if has_s:
            samp = self.samp_alloc(name, l, dk)
        S = self.S_view(l, mi)

        def tile_gen(j, t, s, si):
            kind = self.tile_kind(t)
            pp = self.project(wv, j, ncols)
            yield
            q, kk, g = pre(s, pp, j, t, l, wv)
            yield
            yield from self.gla_A(s, kind, q, kk, g, S, smidB, 2 * si, samp if kind == "s" else None)
            if kind == "p" and t == self.npt - 1:
                self.store_state_p(name, l, S, dk)
            yield
            o_ps = self.gla_B(s, kind, smidB, 2 * si, samp if kind == "s" else None)
            yield
            yield from post(s, o_ps, pp, j, mi, l)

        active = []
        nxt = 0
        free = list(range(nsets))
        while nxt < len(jts) or active:
            if nxt < len(jts) and free:
                si = free.pop(0)
                active.append((tile_gen(jts[nxt][0], jts[nxt][1], sets[si], si), si))
                nxt += 1
            still = []
            for gen, si in active:
                try:
                    next(gen)
                    still.append((gen, si))
                except StopIteration:
                    free.append(si)
            active = still
            yield
        self.sc.reset(m)

    def mixer_hg(self, l, jts, mi, nsets=2, wslot=None):
        k = self.k
        lbc = T(self.lbc_t[:], self.lbc_b.all)

        def pre(s, pp, j, t, l, wv):
            th, fg, kf, qf, tq = s["a0"], s["a1"], s["a2"], s["a3"], s["a4"]
            P_ = pp.ap
            k.ACT(th.ap, P_[:, 256:512], AF.Tanh, r=[pp], w=[th], scale=0.5)
            k.ACT(tq.ap, P_[:, 0:256], AF.Tanh, r=[pp], w=[tq], scale=0.5)
            k.ACT(s["tg"].ap, P_[:, 768:1024], AF.Tanh, r=[pp], w=[s["tg"]], scale=0.5)
            k.TT(fg.ap, th.ap, self.lbc_t[:, 1, :], ALU.mult, r=[th, lbc], w=[fg])
            k.TT(fg.ap, fg.ap, self.lbc_t[:, 0, :], ALU.add, r=[fg, lbc], w=[fg])
            k.TS(kf.ap, fg.ap, -1.0, ALU.mult, 1.0, ALU.add, r=[fg], w=[kf])
            k.TS(fg.ap, fg.ap, 1e-30, ALU.max, r=[fg], w=[fg])
            k.ACT(s["g"].ap, fg.ap, AF.Ln, r=[fg], w=[s["g"]])
            k.STT(qf.ap, tq.ap, 1.0, P_[:, 0:256], ALU.add, ALU.mult, r=[tq, pp], w=[qf])
            k.TS(qf.ap, qf.ap, 0.5, ALU.mult, r=[qf], w=[qf])
            k.CP(s["vbf"].ap, P_[:, 512:768], r=[pp], w=[s["vbf"]], eng="act")
            k.CP(s["gsb"].ap, P_[:, 768:1024], r=[pp], w=[s["gsb"]], eng="act")
            return qf, kf, s["g"]

        def post(s, o_ps, pp, j, mi, l):
            yield from self.post_rms(s, o_ps, self.pb["hgn"], j, mi)

        return self.gla_stream("hg", l, jts, mi, C_HG, 1024, 64, pre, post, nsets=nsets, wslot=wslot)


    def mixer_ret(self, l, jts, mi, nsets=2, wslot=None):
        k = self.k
        gc = T(self.gconst_t[:], self.gconst_b.all)

        def pre(s, pp, j, t, l, wv):
            par = j % 2
            rt = T(self.rot_t[:, par, :], self.rot_b[par])
            k.DMA(rt.ap, self.rot_d[t], w=[rt])
            cosb = self.rot_t[:, par, 0:32].unsqueeze(1).to_broadcast([128, 8, 32])
            sinb = self.rot_t[:, par, 32:64].unsqueeze(1).to_broadcast([128, 8, 32])
            v8 = lambda ap: ap.rearrange("p (h i) -> p h i", h=8)
            v4 = lambda ap: ap.rearrange("p (h a i) -> p h a i", h=4, a=2)
            outs = []
            for (c0, t1, t2, o, scale) in ((0, s["a0"], s["a1"], s["a2"], 1.0), (256, s["a3"], s["a4"], s["a5"], 0.125)):
                src = v8(pp.ap[:, c0:c0 + 256])
                k.STT(v8(t1.ap), src, scale, cosb, ALU.mult, ALU.mult, r=[pp, rt], w=[t1])
                k.STT(v8(t2.ap), src, scale, sinb, ALU.mult, ALU.mult, r=[pp, rt], w=[t2])
                k.TT(v4(o.ap)[:, :, 0, :], v4(t1.ap)[:, :, 0, :], v4(t2.ap)[:, :, 1, :], ALU.subtract,
                     r=[t1, t2], w=[o])
                k.TT(v4(o.ap)[:, :, 1, :], v4(t2.ap)[:, :, 0, :], v4(t1.ap)[:, :, 1, :], ALU.add,
                     r=[t1, t2], w=[o])
                outs.append(o)
            k.CP(s["vbf"].ap, pp.ap[:, 512:768], r=[pp], w=[s["vbf"]], eng="act")
            k.CP(s["gsb"].ap, pp.ap[:, 768:1024], r=[pp], w=[s["gsb"]], eng="act")
            k.ACT(s["tg"].ap, pp.ap[:, 768:1024], AF.Tanh, r=[pp], w=[s["tg"]], scale=0.5)
            return outs[0], outs[1], gc

        def post(s, o_ps, pp, j, mi, l):
            yield from self.post_rms(s, o_ps, None, j, mi)

        return self.gla_stream("ret", l, jts, mi, C_RET, 1024, 64, pre, post, nsets=nsets, wslot=wslot)

    def mixer_gla(self, l, jts, mi, nsets=2, wslot=None):
        k = self.k

        def init(sets):
            for s in sets:
                for nm in ("a4", "a5", "g"):
                    k.MEMSET(s[nm].ap, 0.0, w=[s[nm]])

        def pre(s, pp, j, t, l, wv):
            v3 = lambda ap: ap.rearrange("p (h i) -> p h i", h=4)
            k.ACT(s["tg"].ap, pp.ap[:, 528:784], AF.Tanh, r=[pp], w=[s["tg"]], scale=0.5)
            ap_ = self.ps_small()
            for kt in range(8):
                k.MM(ap_.ap[0:32, 0:128], wv.ap[:, kt, 512:544], self.hT_t[:, kt, j * 128:(j + 1) * 128],
                     start=(kt == 0), stop=(kt == 7), r=[wv, self.hT_b[j]], w=[ap_], inc=(kt == 7))
            adT = s["ob"]
            k.CP(adT.ap[0:32, 0:128], ap_.ap[0:32, 0:128], r=[ap_], w=[adT])
            zp = self.ps_small()
            k.MM(zp.ap[:, 0:128], adT.ap[0:32, 0:128], self.gwa_t[:], r=[adT, self.gwa_b], w=[zp])
            z = s["a0"]
            k.TT(z.ap[:, 0:128], zp.ap[:, 0:128], self.pb["gba"].ap, ALU.add, r=[zp, self.pb["gba"]], w=[z])
            k.ACT(z.ap[:, 0:128], z.ap[:, 0:128], AF.Exp, r=[z], w=[z], scale=-1.0)
            k.ACT(z.ap[:, 0:128], z.ap[:, 0:128], AF.Ln, r=[z, self.cst_b], w=[z], bias=self.cst(self.CST["ONE"]))
            k.TS(v3(s["g"].ap)[:, :, 0:32], v3(z.ap[:, 0:128]), -1.0 / 16.0, ALU.mult, r=[z], w=[s["g"]])
            k.TS(v3(s["a4"].ap)[:, :, 0:32], v3(pp.ap[:, 0:128]), 32.0 ** -0.5, ALU.mult, r=[pp], w=[s["a4"]])
            k.CP(v3(s["a5"].ap)[:, :, 0:32], v3(pp.ap[:, 128:256]), r=[pp], w=[s["a5"]])
            k.CP(s["vbf"].ap, pp.ap[:, 256:512], r=[pp], w=[s["vbf"]], eng="act")
            k.CP(s["gsb"].ap, pp.ap[:, 528:784], r=[pp], w=[s["gsb"]], eng="act")
            return s["a4"], s["a5"], s["g"]

        def post(s, o_ps, pp, j, mi, l):
            yield from self.post_rms(s, o_ps, self.pb["gln"], j, mi)

        return self.gla_stream("gla", l, jts, mi, C_GLA, 784, 32, pre, post, init=init, nsets=nsets, wslot=wslot)


    def mixer_rw(self, l, jts, mi, nsets=1, wslot=None):
        k = self.k
        m = self.sc.mark()
        wv = self.load_mixer_w(l, mi if wslot is None else wslot, C_RW, 1024)
        tiles = [t for (_, t) in jts]
        A = self.sc.alloc
        c128 = self.c128_t
        cst = [T(self.c128_t[:], self.c128_b.all)]
        pb = self.pb
        rw_sb = A([1024], F32)
        f = {nm: A([256], F32) for nm in ("a", "lw", "gsb", "kk", "kp", "bv", "bon", "t0", "t1", "NTAVf")}
        f["t2"] = f["NTAVf"]
        st = A([16], F32)
        X = A([256], BF16)
        XT = A([2, 128], BF16)
        tb16 = {nm: A([256], BF16) for nm in ("rt", "kkt", "vbf", "AV", "NTAVb", "NU", "ob")}
        KB3 = A([3, 256], BF16)
        T8 = A([8, 128], BF16)
        RT2 = A([2, 2, 128], BF16)
        SM = A([4, 4, 128], BF16)
        Pa, Pb, Qa, Qb, Xc = [A([4, 128], BF16) for _ in range(5)]
        mask4 = A([4, 128], BF16)
        NG = A([2, 2, 128], BF16)
        Hd = A([2, 2, 64], F32)
        Smid = A([2, 64], F32)
        dd_buf = A([2, 32], F32)
        tmp = A([2, 64], F32)
        SmidB = A([2, 2, 64], BF16)
        SmidX = A([2, 2, 128], BF16)
        has_s = any(t >= self.npt for t in tiles)
        if has_s:
            sp = dict(S0=A([8, 2, 64], F32), Sn=A([8, 2, 64], F32), KBm=[A([3, 256], BF16) for _ in range(2)],
                      NGs=[A([2, 128], BF16) for _ in range(2)], SmB=[A([2, 64], BF16) for _ in range(2)],
                      SmX=[A([2, 128], BF16) for _ in range(2)], RTm=[A([2, 2, 128], BF16) for _ in range(2)])
        zsrc = self.cm_t[:, 0:4, :].rearrange("p a b -> p (a b)")
        zr = [T(self.cm_t[:], self.cm_b.all)]
        k.TS(SmidX.ap.rearrange("p a b c -> p (a b c)"), zsrc, 0.0, ALU.mult, r=zr, w=[SmidX])
        if has_s:
            for b_ in sp["SmX"]:
                k.TS(b_.ap.rearrange("p a b -> p (a b)"), zsrc[:, 0:256], 0.0, ALU.mult, r=zr, w=[b_])
        mu = pb["mu"]
        S = self.S_view(l, mi)
        rwp = T(self.rwp_t[:, l, :], self.rwp_b[l])
        c1 = -0.5 * math.exp(-0.5)
        v3 = lambda ap: ap.rearrange("p (a b) -> p a b", a=4)
        cur_kind = [None]

        def set_masks(kind):
            if cur_kind[0] == kind:
                return
            cur_kind[0] = kind
            ts_, ti_ = self.cm(kind, "ts"), self.cm(kind, "ti")
            k.CP(mask4.ap[:, 0, :], ts_, r=cst, w=[mask4])
            k.CP(mask4.ap[:, 1, :], ti_, r=cst, w=[mask4])
            k.TS(mask4.ap[:, 2, :], ts_, -1.0, ALU.mult, r=cst, w=[mask4])
            k.CP(mask4.ap[:, 3, :], ti_, r=cst, w=[mask4])

        for j, t in jts:
            kind = self.tile_kind(t)
            nind = 4 if kind == "p" else 32
            set_masks(kind)
            pp = self.project(wv, j, 1024)
            k.CP(rw_sb.ap, pp.ap, r=[pp], w=[rw_sb], eng="act")
            if kind == "p" and t == self.npt - 1:
                k.DMA(self.p_shift[l:l + 1, :], rw_sb.ap[127:128, :], r=[rw_sb], is_output=True)
            if kind == "s":
                for q in range(16):
                    k.DMA(self.s_shift[l, q:q + 1, :], rw_sb.ap[8 * q + 7:8 * q + 8, :], r=[rw_sb], is_output=True)
                k.DMA(self.rwp_t[0:16, l, :], self.st_shift[l], w=[rwp])
            yield
            pv = self.ps_big()
            shm = C_SHP if kind == "p" else C_SHS
            carry = (kind == "s") or (t > 0)
            for half in range(2):
                hs = slice(half * 512, (half + 1) * 512)
                k.MM(pv.ap[:, hs], c128[:, shm:shm + 128], rw_sb.ap[:, hs], start=True, stop=not carry,
                     r=cst + [rw_sb], w=[pv], inc=(not carry and half == 1))
                if carry and kind == "p":
                    k.MM(pv.ap[:, hs], c128[:, C_CAR:C_CAR + 128], self.rwp_t[:, l, hs], start=False, stop=True,
                         r=cst + [rwp], w=[pv], inc=(half == 1))
                elif carry:
                    k.MM(pv.ap[:, hs], c128[0:32, C_SEL:C_SEL + 128], self.rwp_t[0:32, l, hs], start=False,
                         stop=True, r=cst + [rwp], w=[pv], inc=(half == 1))
            if kind == "p":
                k.CP(self.rwp_t[64:128, l, :], rw_sb.ap[64:128, :], r=[rw_sb], w=[rwp], eng="act")
            k.TT(rw_sb.ap, rw_sb.ap, pv.ap, ALU.subtract, r=[rw_sb, pv], w=[rw_sb])
            k.TT(rw_sb.ap, rw_sb.ap, mu.ap, ALU.mult, r=[rw_sb, mu], w=[rw_sb])
            k.TT(rw_sb.ap, rw_sb.ap, pv.ap, ALU.add, r=[rw_sb, pv], w=[rw_sb])
            mx = rw_sb.ap
            r_, wd_, kx_, v_, ad_, gd_ = mx[:, 0:256], mx[:, 256:320], mx[:, 320:576], mx[:, 576:832], mx[:, 832:896], mx[:, 896:1024]
            yield
            k.ACT(X.ap[:, 0:64], wd_, AF.Tanh, r=[rw_sb], w=[X])
            k.CP(X.ap[:, 64:128], ad_, r=[rw_sb], w=[X])
            k.ACT(f["t0"].ap[:, 0:128], gd_, AF.Tanh, r=[rw_sb], w=[f["t0"]], scale=0.5)
            k.TS(X.ap[:, 128:256], f["t0"].ap[:, 0:128], 0.5, ALU.mult, 0.5, ALU.add, r=[f["t0"]], w=[X])
            tb = self.ps_tb()
            for i in range(2):
                k.TR(tb.ap[:, i * 128:(i + 1) * 128], X.ap[:, i * 128:(i + 1) * 128], self.ident.ap,
                     r=[X, self.ident], w=[tb], inc=(i == 1))
            k.CP(XT.ap, tb.ap[:, 0:256].rearrange("p (a b) -> p a b", a=2), r=[tb], w=[XT], eng="act")
            rwl = T(self.rwl_t[:], self.rwl_b.all)
            lwb = self.ps_small()
            lab = self.ps_small()
            k.MM(lwb.ap[:, 0:256], XT.ap[0:64, 0, :], self.rwl_t[0:64, 0, :], r=[XT, rwl], w=[lwb], inc=False)
            k.MM(lab.ap[:, 0:256], XT.ap[64:128, 0, :], self.rwl_t[64:128, 1, :], r=[XT, rwl], w=[lab])
            k.MM(lwb.ap[:, 256:512], XT.ap[:, 1, :], self.rwl_t[:, 2, :], r=[XT, rwl], w=[lwb])
            k.TT(f["t0"].ap, lwb.ap[:, 0:256], pb["rw_w0"].ap, ALU.add, r=[lwb, pb["rw_w0"]], w=[f["t0"]])
            k.ACT(f["t0"].ap, f["t0"].ap, AF.Tanh, r=[f["t0"]], w=[f["t0"]], scale=0.5)
            k.TS(f["lw"].ap, f["t0"].ap, c1, ALU.mult, c1, ALU.add, r=[f["t0"]], w=[f["lw"]])
            k.TT(f["t1"].ap, lab.ap[:, 0:256], pb["rw_a0"].ap, ALU.add, r=[lab, pb["rw_a0"]], w=[f["t1"]])
            k.ACT(f["t1"].ap, f["t1"].ap, AF.Tanh, r=[f["t1"]], w=[f["t1"]], scale=0.5)
            k.TS(f["a"].ap, f["t1"].ap, 0.5, ALU.mult, 0.5, ALU.add, r=[f["t1"]], w=[f["a"]])
            k.CP(f["gsb"].ap, lwb.ap[:, 256:512], r=[lwb], w=[f["gsb"]], eng="act")
            yield
            k.TT(f["kk"].ap, kx_, pb["rw_kk"].ap, ALU.mult, r=[rw_sb, pb["rw_kk"]], w=[f["kk"]])
            k.TT(f["t0"].ap, f["kk"].ap, f["kk"].ap, ALU.mult, r=[f["kk"]], w=[f["t0"]])
            k.RSUM(st.ap[:, 0:4], v3(f["t0"].ap), r=[f["t0"]], w=[st])
            k.ACT(st.ap[:, 4:8], st.ap[:, 0:4], AF.Ln, r=[st, self.cst_b], w=[st], bias=self.cst(self.CST["TINY"]))
            k.ACT(st.ap[:, 0:4], st.ap[:, 4:8], AF.Exp, r=[st], w=[st], scale=-0.5)
            k.TT(v3(f["kk"].ap), v3(f["kk"].ap), st.ap[:, 0:4].unsqueeze(2).to_broadcast([128, 4, 64]), ALU.mult,
                 r=[f["kk"], st], w=[f["kk"]])
            k.STT(f["t0"].ap, f["a"].ap, -1.0, pb["rw_ka"].ap, ALU.add, ALU.mult, r=[f["a"], pb["rw_ka"]], w=[f["t0"]])
            k.STT(f["kp"].ap, f["t0"].ap, 1.0, kx_, ALU.add, ALU.mult, r=[f["t0"], rw_sb], w=[f["kp"]])
            k.TT(f["bv"].ap, f["a"].ap, f["kk"].ap, ALU.mult, r=[f["a"], f["kk"]], w=[f["bv"]])
            k.TT(f["t0"].ap, r_, f["kp"].ap, ALU.mult, r=[rw_sb, f["kp"]], w=[f["t0"]])
            k.TT(f["t0"].ap, f["t0"].ap, pb["rw_rk"].ap, ALU.mult, r=[f["t0"], pb["rw_rk"]], w=[f["t0"]])
            k.RSUM(st.ap[:, 8:12], v3(f["t0"].ap), r=[f["t0"]], w=[st])
            k.TT(v3(f["bon"].ap), v3(v_), st.ap[:, 8:12].unsqueeze(2).to_broadcast([128, 4, 64]), ALU.mult,
                 r=[rw_sb, st], w=[f["bon"]])
            vbf = tb16["vbf"]
            k.CP(vbf.ap, v_, r=[rw_sb], w=[vbf], eng="act")
            yield
            bp = self.ps_small()
            k.MM(bp.ap[:, 0:256], self.cm(kind, "mcum"), f["lw"].ap, r=cst + [f["lw"]], w=[bp], inc=False)
            for jj in range(2):
                k.MM(bp.ap[:, 256 + jj * nind:256 + (jj + 1) * nind], f["lw"].ap[:, jj * 128:(jj + 1) * 128],
                     self.cm(kind, "ind"), r=cst + [f["lw"]], w=[bp], inc=(jj == 1))
            dd = dd_buf
            k.ACT(f["t0"].ap, bp.ap[:, 0:256], AF.Exp, r=[bp], w=[f["t0"]])
            k.ACT(f["t1"].ap, bp.ap[:, 0:256], AF.Exp, r=[bp], w=[f["t1"]], scale=-1.0)
            k.ACT(dd.ap[:, :, 0:nind], bp.ap[:, 256:256 + 2 * nind].rearrange("p (a b) -> p a b", a=2), AF.Exp,
                  r=[bp], w=[dd])
            k.TT(f["t2"].ap, bp.ap[:, 0:256], f["lw"].ap, ALU.subtract, r=[bp, f["lw"]], w=[f["t2"]])
            k.ACT(f["t2"].ap, f["t2"].ap, AF.Exp, r=[f["t2"]], w=[f["t2"]])
            rt, kkt = tb16["rt"], tb16["kkt"]
            k.TT(rt.ap, r_, f["t0"].ap, ALU.mult, r=[rw_sb, f["t0"]], w=[rt])
            k.TT(kkt.ap, f["kk"].ap, f["t2"].ap, ALU.mult, r=[f["kk"], f["t2"]], w=[kkt])
            k.TT(KB3.ap[:, 0, :], f["kp"].ap, f["t1"].ap, ALU.mult, r=[f["kp"], f["t1"]], w=[KB3])
            k.TT(KB3.ap[:, 1, :], f["bv"].ap, f["t1"].ap, ALU.mult, r=[f["bv"], f["t1"]], w=[KB3])
            yield
            srcs = [(kkt.ap, kkt, 0), (rt.ap, rt, 0), (kkt.ap, kkt, 1), (rt.ap, rt, 1),
                    (KB3.ap[:, 0, :], KB3, 0), (KB3.ap[:, 0, :], KB3, 1), (KB3.ap[:, 1, :], KB3, 0),
                    (KB3.ap[:, 1, :], KB3, 1)]
            tb = self.ps_tb()
            for i, (ap_, dep_, jj) in enumerate(srcs):
                k.TR(tb.ap[:, i * 128:(i + 1) * 128], ap_[:, jj * 128:(jj + 1) * 128], self.ident.ap,
                     r=[dep_, self.ident], w=[tb], inc=(i == 7))
            k.CP(T8.ap, tb.ap.rearrange("p (a b) -> p a b", a=8), r=[tb], w=[T8], eng="act")
            k.CP(RT2.ap[:, :, 1, :], tb.ap[:, 0:512].rearrange("p (a b c) -> p a b c", a=2, b=2)[:, :, 1, :],
                 r=[tb], w=[RT2])
            yield
            for h in range(4):
                jj, hl = h // 2, h % 2
                R = slice(64 * hl, 64 * hl + 64)
                sb_ = self.ps_small()
                rhs = T8.ap[R, 2 * jj:2 * jj + 2, :]
                k.MM(sb_.ap[:, 0:256].rearrange("p (a b) -> p a b", a=2), T8.ap[R, 4 + jj, :], rhs, r=[T8],
                     w=[sb_], inc=False)
                k.MM(sb_.ap[:, 256:512].rearrange("p (a b) -> p a b", a=2), T8.ap[R, 6 + jj, :], rhs, r=[T8],
                     w=[sb_])
                k.TT(SM.ap[:, h], sb_.ap.rearrange("p (a b) -> p a b", a=4), mask4.ap, ALU.mult,
                     r=[sb_, mask4], w=[SM])
            Lb = [self.ps_small(), self.ps_small()]
            for hl in range(2):
                R = slice(64 * hl, 64 * hl + 64)
                for jj in range(2):
                    k.MM(Lb[hl].ap[:, jj * 128:(jj + 1) * 128], T8.ap[R, 2 * jj, :], T8.ap[R, 6 + jj, :], r=[T8],
                         w=[Lb[hl]], inc=(jj == 1))
            low = self.cm(kind, "low")
            Pa4 = Pa.ap.rearrange("p (j h) t -> p j h t", j=2)
            for hl in range(2):
                k.STT(Pa4[:, :, hl, :], Lb[hl].ap[:, 0:256].rearrange("p (a b) -> p a b", a=2), -1.0,
                      low.unsqueeze(1).to_broadcast([128, 2, 128]), ALU.mult, ALU.mult, r=[Lb[hl]] + cst, w=[Pa])
            Q0 = T(SM.ap[:, :, 2, :], SM.d)
            k.TT(Xc.ap, Q0.ap, self.ident.ap.unsqueeze(1).to_broadcast([128, 4, 128]), ALU.add,
                 r=[SM, self.ident], w=[Xc])
            yield
            nsteps = 5 if kind == "p" else 2
            Pc, Qc = Pa, Q0
            for step in range(1, nsteps + 1):
                last = step == nsteps
                Pn = Pb if Pc is Pa else Pa
                Qn = Qb if (Qc is Qa or Qc is Q0) else Qa
                pb_ = self.ps_small()
                for h in range(4):
                    k.MM(pb_.ap[:, h * 128:(h + 1) * 128], Qc.ap[:, h], Pc.ap[:, h], r=[Qc, Pc], w=[pb_],
                         inc=(h == 3))
                if not last:
                    qb_ = self.ps_small()
                    for h in range(4):
                        k.MM(qb_.ap[:, h * 128:(h + 1) * 128], Pc.ap[:, h], Qc.ap[:, h], r=[Qc, Pc], w=[qb_],
                             inc=(h == 3))
                k.CP(Pn.ap, pb_.ap.rearrange("p (a b) -> p a b", a=4), r=[pb_], w=[Pn], eng="act")
                if not last:
                    k.CP(Qn.ap, qb_.ap.rearrange("p (a b) -> p a b", a=4), r=[qb_], w=[Qn], eng="act")
                xb_ = self.ps_small()
                for h in range(4):
                    k.MM(xb_.ap[:, h * 128:(h + 1) * 128], Pn.ap[:, h], Xc.ap[:, h], r=[Pn, Xc], w=[xb_],
                         inc=(h == 3))
                k.TT(Xc.ap, Xc.ap, xb_.ap.rearrange("p (a b) -> p a b", a=4), ALU.add, r=[Xc, xb_], w=[Xc])
                Pc, Qc = Pn, Qn
                yield
            yield
            tk = self.ps_small()
            for h in range(4):
                jj, hl = h // 2, h % 2
                k.MM(tk.ap[64 * hl:64 * hl + 64, jj * 128:(jj + 1) * 128], kkt.ap[:, h * 64:(h + 1) * 64], Xc.ap[:, h],
                     r=[kkt, Xc], w=[tk], inc=False)
            for h in range(4):
                k.MM(tk.ap[:, 256 + h * 64:256 + (h + 1) * 64], Xc.ap[:, h], kkt.ap[:, h * 64:(h + 1) * 64],
                     r=[kkt, Xc], w=[tk], inc=(h == 3))
            k.CP(RT2.ap[:, :, 0, :], tk.ap[:, 0:256].rearrange("p (a b) -> p a b", a=2), r=[tk], w=[RT2], eng="act")
            k.CP(KB3.ap[:, 2, :], tk.ap[:, 256:512], r=[tk], w=[KB3])
            av = self.ps_small()
            AV, NTAVb, NU = tb16["AV"], tb16["NTAVb"], tb16["NU"]
            for h in range(4):
                k.MM(av.ap[:, h * 64:(h + 1) * 64], SM.ap[:, h, 0, :], vbf.ap[:, h * 64:(h + 1) * 64], r=[SM, vbf],
                     w=[av], inc=(h == 3))
            k.CP(AV.ap, av.ap[:, 0:256], r=[av], w=[AV], eng="act")
            for h in range(4):
                k.MM(av.ap[:, 256 + h * 64:256 + (h + 1) * 64], Xc.ap[:, h], AV.ap[:, h * 64:(h + 1) * 64],
                     r=[Xc, AV], w=[av], inc=(h == 3))
            k.TS(f["NTAVf"].ap, av.ap[:, 256:512], -1.0, ALU.mult, r=[av], w=[f["NTAVf"]])
            k.CP(NTAVb.ap, f["NTAVf"].ap, r=[f["NTAVf"]], w=[NTAVb], eng="act")
            bd = c128[:, C_BD:C_BD + 128].unsqueeze(1).to_broadcast([128, 2, 128])
            yield
            if kind == "p":
                Ys = [self.ps_small(), self.ps_small()]
                for c in range(2):
                    Rc = slice(64 * c, 64 * c + 64)
                    for jj in range(2):
                        cs = slice(jj * 128, (jj + 1) * 128)
                        k.MM(Ys[c].ap[:, cs], KB3.ap[Rc, 2, cs], KB3.ap[Rc, 1, cs], r=[KB3], w=[Ys[c]], inc=False)
                    for jj in range(2):
                        cs = slice(jj * 128, (jj + 1) * 128)
                        co = slice(256 + jj * 128, 256 + (jj + 1) * 128)
                        k.MM(Ys[c].ap[:, co], KB3.ap[Rc, 0, cs], vbf.ap[Rc, cs], start=True, stop=False,
                             r=[KB3, vbf], w=[Ys[c]], inc=False)
                        k.MM(Ys[c].ap[:, co], KB3.ap[Rc, 1, cs], NTAVb.ap[Rc, cs], start=False, stop=True,
                             r=[KB3, NTAVb], w=[Ys[c]], inc=(jj == 1))
                for c in range(2):
                    k.STT(NG.ap[:, c], Ys[c].ap[:, 0:256].rearrange("p (a b) -> p a b", a=2), -1.0, bd, ALU.mult,
                          ALU.mult, r=[Ys[c]] + cst, w=[NG])
                    y2 = Ys[c].ap[:, 256:512].rearrange("p (a b) -> p a b", a=2)
                    k.CP(Hd.ap[0:64, c], y2[0:64, :, 0:64], r=[Ys[c]], w=[Hd])
                    k.CP(Hd.ap[64:128, c], y2[64:128, :, 64:128], r=[Ys[c]], w=[Hd], eng="act")
                for c in range(2):
                    d1 = dd.ap[:, :, 2 * c:2 * c + 1].to_broadcast([128, 2, 64])
                    d2 = dd.ap[:, :, 2 * c + 1:2 * c + 2].to_broadcast([128, 2, 64])
                    k.TT(Smid.ap, S.ap, d1, ALU.mult, r=[S, dd], w=[Smid])
                    k.CP(SmidB.ap[:, c], Smid.ap, r=[Smid], w=[SmidB], eng="act")
                    k.CP(SmidX.ap[0:64, c, :, 0:64], Smid.ap[0:64], r=[Smid], w=[SmidX])
                    k.CP(SmidX.ap[64:128, c, :, 64:128], Smid.ap[64:128], r=[Smid], w=[SmidX], eng="act")
                    Z = self.ps_small()
                    for jj in range(2):
                        k.MM(Z.ap[:, jj * 64:(jj + 1) * 64], NG.ap[:, c, jj, :], SmidB.ap[:, c, jj, :],
                             r=[NG, SmidB], w=[Z], inc=(jj == 1))
                    k.TT(tmp.ap, Smid.ap, Z.ap[:, 0:128].rearrange("p (a b) -> p a b", a=2), ALU.add,
                         r=[Smid, Z], w=[tmp])
                    k.TT(tmp.ap, tmp.ap, Hd.ap[:, c], ALU.add, r=[tmp, Hd], w=[tmp])
                    k.TT(S.ap, tmp.ap, d2, ALU.mult, r=[tmp, dd], w=[S])
                if t == self.npt - 1:
                    self.store_state_p("rw", l, S, 64)
                yield
                ub = self.ps_small()
                for c in range(2):
                    Rc = slice(64 * c, 64 * c + 64)
                    for jj in range(2):
                        k.MM(ub.ap[Rc, jj * 128:(jj + 1) * 128], RT2.ap[:, jj, 0, Rc], SmidX.ap[:, c, jj, :],
                             r=[RT2, SmidX], w=[ub], inc=(c == 1 and jj == 1))
                k.STT(NU.ap, ub.ap[:, 0:256], -1.0, f["NTAVf"].ap, ALU.mult, ALU.add, r=[ub, f["NTAVf"]], w=[NU])
                yield
                ob_ = self.ps_small()
                for c in range(2):
                    Rc = slice(64 * c, 64 * c + 64)
                    for h in range(4):
                        jj, hl = h // 2, h % 2
                        Hc = slice(h * 64, (h + 1) * 64)
                        out = ob_.ap[Rc, Hc]
                        k.MM(out, SM.ap[:, h, 1, Rc], vbf.ap[:, Hc], start=True, stop=False, r=[SM, vbf], w=[ob_],
                             inc=False)
                        k.MM(out, SM.ap[:, h, 3, Rc], NU.ap[:, Hc], start=False, stop=False, r=[SM, NU], w=[ob_],
                             inc=False)
                        k.MM(out, RT2.ap[:, jj, 1, Rc], SmidX.ap[:, c, jj, hl * 64:(hl + 1) * 64], start=False,
                             stop=True, r=[RT2, SmidX], w=[ob_], inc=(c == 1 and h == 3))
            else:
                S0, Sn = sp["S0"], sp["Sn"]
                src = self.st_in["rw"][l]
                dst = self.s_out["rw"][l]
                big = self.ps_big()
                ub = T(big.ap[:, 0:512], (big.d[0], (big.d[1][0],)))
                ob_ = T(big.ap[:, 512:1024], (big.d[0], (big.d[1][1],)))
                ddv = dd.ap.rearrange("p j (q t) -> p q j t", t=2)
                cmd = [T(self.cm_t[:], self.cm_b.all)]
                zrhs = self.cm_t[:, 0:2, :].rearrange("p a b -> p (a b)")
                for bk in (ub, ob_):
                    k.MM(bk.ap[:, 0:256], self.zl.ap, zrhs, start=True, stop=False, r=[self.zl] + cmd, w=[bk])
                for hf in range(2):
                    q0 = 8 * hf
                    d1 = ddv[:, q0:q0 + 8, :, 0:1].to_broadcast([128, 8, 2, 64])
                    d2 = ddv[:, q0:q0 + 8, :, 1:2].to_broadcast([128, 8, 2, 64])
                    for hl in range(2):
                        for jj in range(2):
                            k.DMA(S0.ap[64 * hl:64 * hl + 64, :, jj, :],
                                  src[q0:q0 + 8, 2 * jj + hl].rearrange("s k v -> k s v"), w=[S0])
                    k.TT(S0.ap, S0.ap, d1, ALU.mult, r=[S0, dd], w=[S0])
                    for q in range(8):
                        sq = q0 + q
                        KBm, NGs, SmB, SmX, RTm = [sp[n_][sq % 2] for n_ in ("KBm", "NGs", "SmB", "SmX", "RTm")]
                        k.TS(KBm.ap.rearrange("p a b -> p (a b)"), KB3.ap.rearrange("p a b -> p (a b)"),
                             c128[:, C_ROWS + sq:C_ROWS + sq + 1], ALU.mult, r=[KB3] + cst, w=[KBm])
                        Y = self.ps_small()
                        for jj in range(2):
                            cs = slice(jj * 128, (jj + 1) * 128)
                            k.MM(Y.ap[:, cs], KBm.ap[:, 2, cs], KB3.ap[:, 1, cs], r=[KBm, KB3], w=[Y], inc=False)
                        for jj in range(2):
                            cs = slice(jj * 128, (jj + 1) * 128)
                            co = slice(256 + jj * 128, 256 + (jj + 1) * 128)
                            k.MM(Y.ap[:, co], KBm.ap[:, 0, cs], vbf.ap[:, cs], start=True, stop=False,
                                 r=[KBm, vbf], w=[Y], inc=False)
                            k.MM(Y.ap[:, co], KBm.ap[:, 1, cs], NTAVb.ap[:, cs], start=False, stop=True,
                                 r=[KBm, NTAVb], w=[Y], inc=(jj == 1))
                        k.STT(NGs.ap, Y.ap[:, 0:256].rearrange("p (a b) -> p a b", a=2), -1.0, bd, ALU.mult,
                              ALU.mult, r=[Y] + cst, w=[NGs])
                        y2 = Y.ap[:, 256:512].rearrange("p (a b) -> p a b", a=2)
                        k.CP(Sn.ap[0:64, q], y2[0:64, :, 0:64], r=[Y], w=[Sn])
                        k.CP(Sn.ap[64:128, q], y2[64:128, :, 64:128], r=[Y], w=[Sn], eng="act")
                        k.CP(SmB.ap, S0.ap[:, q], r=[S0], w=[SmB], eng="act")
                        k.CP(SmX.ap[0:64, :, 0:64], S0.ap[0:64, q], r=[S0], w=[SmX])
                        k.CP(SmX.ap[64:128, :, 64:128], S0.ap[64:128, q], r=[S0], w=[SmX], eng="act")
                        Z = self.ps_small()
                        for jj in range(2):
                            k.MM(Z.ap[:, jj * 64:(jj + 1) * 64], NGs.ap[:, jj, :], SmB.ap[:, jj, :], r=[NGs, SmB],
                                 w=[Z], inc=(jj == 1))
                        k.TT(Sn.ap[:, q], Sn.ap[:, q], Z.ap[:, 0:128].rearrange("p (a b) -> p a b", a=2), ALU.add,
                             r=[Sn, Z], w=[Sn])
                        k.TT(RTm.ap.rearrange("p a b c -> p (a b) c"), RT2.ap.rearrange("p a b c -> p (a b) c"),
                             self.cm_t[:, sq:sq + 1, :].to_broadcast([128, 4, 128]), ALU.mult, r=[RT2] + cmd,
                             w=[RTm])
                        for jj in range(2):
                            k.MM(ub.ap[:, jj * 128:(jj + 1) * 128], RTm.ap[:, jj, 0, :], SmX.ap[:, jj, :],
                                 start=False, stop=False, r=[RTm, SmX], w=[ub], inc=False)
                        for h in range(4):
                            jj, hl = h // 2, h % 2
                            k.MM(ob_.ap[:, h * 64:(h + 1) * 64], RTm.ap[:, jj, 1, :],
                                 SmX.ap[:, jj, hl * 64:(hl + 1) * 64], start=False, stop=False,
                                 r=[RTm, SmX], w=[ob_], inc=(h == 3))
                    k.TT(Sn.ap, Sn.ap, S0.ap, ALU.add, r=[Sn, S0], w=[Sn])
                    k.TT(Sn.ap, Sn.ap, d2, ALU.mult, r=[Sn, dd], w=[Sn])
                    for hl in range(2):
                        for jj in range(2):
                            k.DMA(dst[q0:q0 + 8, 2 * jj + hl].rearrange("s k v -> k s v"),
                                  Sn.ap[64 * hl:64 * hl + 64, :, jj, :], r=[Sn], is_output=True)
                k.MM(ub.ap[:, 0:256], self.zl.ap, zrhs, start=False, stop=True, r=[self.zl] + cmd, w=[ub])
                k.STT(NU.ap, ub.ap[:, 0:256], -1.0, f["NTAVf"].ap, ALU.mult, ALU.add, r=[ub, f["NTAVf"]], w=[NU])
                for h in range(4):
                    Hc = slice(h * 64, (h + 1) * 64)
                    k.MM(ob_.ap[:, Hc], SM.ap[:, h, 1, :], vbf.ap[:, Hc], start=False, stop=False, r=[SM, vbf],
                         w=[ob_], inc=False)
                    k.MM(ob_.ap[:, Hc], SM.ap[:, h, 3, :], NU.ap[:, Hc], start=False, stop=False, r=[SM, NU],
                         w=[ob_], inc=False)
                k.MM(ob_.ap[:, 0:256], self.zl.ap, zrhs, start=False, stop=True, r=[self.zl] + cmd, w=[ob_])
            yield
            osb, t0 = f["t1"], f["t0"]
            k.CP(osb.ap, ob_.ap[:, 0:256], r=[ob_], w=[osb], eng="act")
            k.RSUM(st.ap[:, 0:4], v3(osb.ap), r=[osb], w=[st])
            k.TT(t0.ap, osb.ap, osb.ap, ALU.mult, r=[osb], w=[t0])
            k.RSUM(st.ap[:, 4:8], v3(t0.ap), r=[t0], w=[st])
            k.TS(st.ap[:, 8:12], st.ap[:, 0:4], 1.0 / 64, ALU.mult, r=[st], w=[st])
            k.TT(st.ap[:, 12:16], st.ap[:, 8:12], st.ap[:, 8:12], ALU.mult, r=[st], w=[st])
            k.STT(st.ap[:, 4:8], st.ap[:, 4:8], 1.0 / 64, st.ap[:, 12:16], ALU.mult, ALU.subtract, r=[st], w=[st])
            k.ACT(st.ap[:, 0:4], st.ap[:, 4:8], AF.Ln, r=[st, self.cst_b], w=[st], bias=self.cst(self.CST["EPSLN"]))
            k.ACT(st.ap[:, 0:4], st.ap[:, 0:4], AF.Exp, r=[st], w=[st], scale=-0.5)
            k.TT(v3(t0.ap), v3(osb.ap), st.ap[:, 8:12].unsqueeze(2).to_broadcast([128, 4, 64]), ALU.subtract,
                 r=[osb, st], w=[t0])
            k.TT(v3(t0.ap), v3(t0.ap), st.ap[:, 0:4].unsqueeze(2).to_broadcast([128, 4, 64]), ALU.mult,
                 r=[t0, st], w=[t0])
            k.TT(t0.ap, t0.ap, pb["rw_ln_w"].ap, ALU.mult, r=[t0, pb["rw_ln_w"]], w=[t0])
            k.TT(t0.ap, t0.ap, pb["rw_ln_b"].ap, ALU.add, r=[t0, pb["rw_ln_b"]], w=[t0])
            k.TT(t0.ap, t0.ap, f["bon"].ap, ALU.add, r=[t0, f["bon"]], w=[t0])
            k.TT(tb16["ob"].ap, t0.ap, f["gsb"].ap, ALU.mult, r=[t0, f["gsb"]], w=[tb16["ob"]])
            self.to_oT(tb16["ob"], j, mi)
            yield
        self.rw_top = self.sc.top
        self.sc.reset(m)


_WNAMES = ("attn_norm_w", "w_in", "hg_lb_logits", "hg_norm_w", "gla_wa2", "gla_ba", "gla_norm_w", "rw_mu", "rw_w0",
           "rw_w2", "rw_a0", "rw_a2", "rw_g2", "rw_kk", "rw_ka", "rw_rk", "rw_ln_w", "rw_ln_b", "w_branch", "w_out",
           "ffn_norm_w", "w_ffn_in", "w_ffn_out", "final_norm_w")


def make_in_map(inputs, core, npt, consts):
    f = lambda a: np.ascontiguousarray(np.asarray(a, dtype=np.float32))
    xp = f(inputs["x_prompt"])[core, :npt * 128]
    s0, s1 = core * DEC_PER_CORE, (core + 1) * DEC_PER_CORE
    xs = f(inputs["x_sample"])[s0:s1].reshape(DEC_PER_CORE * DEC_SEQ, D)
    m = {"xin": np.ascontiguousarray(np.concatenate([xp, xs], 0)),
         "st_hg": f(inputs["state_hgrn"])[:, s0:s1], "st_gla": f(inputs["state_gla"])[:, s0:s1],
         "st_rw": f(inputs["state_rwkv"])[:, s0:s1], "st_ret": f(inputs["state_ret"])[:, s0:s1],
         "st_shift": f(inputs["state_rwkv_shift"])[:, s0:s1]}
    for n in _WNAMES:
        m[n] = f(inputs[n])
    m["c128"], m["colmask"], m["rot"] = consts
    return {k_: np.ascontiguousarray(v) for k_, v in m.items()}


_NC_CACHE = {}


def kernel(**inputs):
    npt = SEQ // 128
    passes = [list(range(0, 8)), list(range(8, 17))]
    key = (npt, str(passes))
    if key not in _NC_CACHE:
        _NC_CACHE[key] = build(npt, passes)
    nc = _NC_CACHE[key]
    consts = make_consts(npt)
    in_maps = [make_in_map(inputs, c, npt, consts) for c in range(N_CORES)]
    res = run_bass_kernel_spmd(nc, in_maps, core_ids=list(range(N_CORES))).results
    y_prompt = np.stack([r["yout"][:npt * 128] for r in res], 0)
    y_sample = np.concatenate([r["yout"][npt * 128:].reshape(DEC_PER_CORE, DEC_SEQ, D) for r in res], 0)
    outs = [y_prompt.astype(np.float32), y_sample.astype(np.float32)]
    for nm in ("p_hg", "p_gla", "p_rw", "p_shift", "p_ret"):
        outs.append(np.stack([r[nm] for r in res], 1).astype(np.float32))
    for nm in ("s_hg", "s_gla", "s_rw", "s_shift", "s_ret"):
        outs.append(np.concatenate([r[nm] for r in res], 1).astype(np.float32))
    return tuple(outs)
```
